# Optimizing a Trainium2 kernel written in Bass

```python
import math
import jax
import jax.numpy as jnp
from jax import lax
import numpy as np

D_MODEL = 2048
BATCH = 4
SEQ = 8192
DEPTH = 2

CTX_LEN = 256
GRID_W = 64
N_BRANCH = 3
BRANCH_WIDTH = 1024
N_HEADS = 8
N_KV_HEADS = 2
HEAD_DIM = 128
GROUP = N_HEADS // N_KV_HEADS
WINDOW = 128
Q_BLOCK = 128
ROPE_BASE = 10000.0
RNN_WIDTH = BRANCH_WIDTH
RNN_BLOCKS = 8
RNN_BLOCK = RNN_WIDTH // RNN_BLOCKS
RNN_CONV = 4
RNN_CONV_LEFT = 2
LRU_C = 8.0
SC_WIDTH = BRANCH_WIDTH
SC_CONV = 3
SC_CONV_LEFT = 1
D_FF = 5504
N_MOD = 9
EPS = 1e-6
NEG_INF = -1e30
IN_SIZES = (RNN_WIDTH, RNN_WIDTH, SC_WIDTH, SC_WIDTH, SC_WIDTH,
            N_HEADS * HEAD_DIM, N_KV_HEADS * HEAD_DIM, N_KV_HEADS * HEAD_DIM,
            N_BRANCH * D_MODEL)
IN_COLS = sum(IN_SIZES)

kernel_name = "hybrid_rglru_shortconv_swa_diffusion_block"


def rmsnorm(x, g):
    xf = x.astype(jnp.float32)
    y = xf * lax.rsqrt(jnp.mean(xf * xf, axis=-1, keepdims=True) + EPS)
    return (y * g.astype(jnp.float32)).astype(x.dtype)


def modulate(x, shift, scale):
    return x * (1 + scale) + shift


def swiglu(u, w13, w2):
    gu = u @ w13
    gate, up = gu[..., :D_FF], gu[..., D_FF:]
    return (jax.nn.silu(gate) * up) @ w2


def split_in(z):
    out, start = [], 0
    for n in IN_SIZES:
        out.append(z[..., start:start + n])
        start += n
    return out


def dwconv(x, w, left):
    k_w, ch = w.shape
    return lax.conv_general_dilated(
        x, w[:, None, :], window_strides=(1,), padding=[(left, k_w - 1 - left)],
        dimension_numbers=("NWC", "WIO", "NWC"), feature_group_count=ch)


def axial_rope(n_tok):
    rows = n_tok // GRID_W
    row = jnp.repeat(jnp.arange(rows), GRID_W).astype(jnp.float32)
    col = jnp.tile(jnp.arange(GRID_W), rows).astype(jnp.float32)
    half = HEAD_DIM // 2
    inv = ROPE_BASE ** (-jnp.arange(0, half, 2, dtype=jnp.float32) / half)
    ang = jnp.concatenate([row[:, None] * inv, col[:, None] * inv], axis=-1)
    ang = ang.reshape(n_tok, 2, half // 2)
    return jnp.cos(ang), jnp.sin(ang)


def apply_rope(x, cos, sin):
    b, l, h, d = x.shape
    xr = x.astype(jnp.float32).reshape(b, l, h, 2, 2, d // 4)
    x1, x2 = xr[..., 0, :], xr[..., 1, :]
    cs, sn = cos[None, :, None], sin[None, :, None]
    out = jnp.stack([x1 * cs - x2 * sn, x2 * cs + x1 * sn], axis=-2)
    return out.reshape(b, l, h, d).astype(x.dtype)


def linear_scan(a, b, h0):
    def combine(e1, e2):
        a1, b1 = e1
        a2, b2 = e2
        return a1 * a2, a2 * b1 + b2
    a_cum, b_cum = lax.associative_scan(combine, (a, b), axis=1)
    return b_cum + a_cum * h0[:, None, :]


def rglru(x, w_a, b_a, w_x, b_x, lam, h0, reverse):
    b, l, r = x.shape
    xb = x.reshape(b, l, RNN_BLOCKS, RNN_BLOCK)
    rg = jax.nn.sigmoid((jnp.einsum("blnd,nde->blne", xb, w_a).reshape(b, l, r) + b_a).astype(jnp.float32))
    ig = jax.nn.sigmoid((jnp.einsum("blnd,nde->blne", xb, w_x).reshape(b, l, r) + b_x).astype(jnp.float32))
    log_a = -LRU_C * rg * jax.nn.softplus(-lam.astype(jnp.float32))
    a = jnp.exp(log_a)
    u = jnp.sqrt(-jnp.expm1(2.0 * log_a)) * (ig * x.astype(jnp.float32))
    if reverse:
        a, u = jnp.flip(a, axis=1), jnp.flip(u, axis=1)
    h = linear_scan(a, u, h0)
    h_last = h[:, -1]
    if reverse:
        h = jnp.flip(h, axis=1)
    return h.astype(x.dtype), h_last


def sink_softmax(logits, sink):
    s = jnp.broadcast_to(sink.astype(jnp.float32)[None, :, :, None, None], logits.shape[:-1] + (1,))
    p = jax.nn.softmax(jnp.concatenate([s, logits], axis=-1), axis=-1)
    return p[..., 1:]


def banded_attention(q, k, v, kc, vc, sink):
    b, l = q.shape[0], q.shape[1]
    nblk = l // Q_BLOCK
    scale = HEAD_DIM ** -0.5
    qb = q.reshape(b, nblk, Q_BLOCK, N_KV_HEADS, GROUP, HEAD_DIM)
    pad = ((0, 0), (Q_BLOCK, Q_BLOCK), (0, 0), (0, 0))
    kp, vp = jnp.pad(k, pad), jnp.pad(v, pad)
    span = 3 * Q_BLOCK
    offs_q = jnp.arange(Q_BLOCK)
    offs_k = jnp.arange(span) - Q_BLOCK

    def block(n):
        qn = lax.dynamic_index_in_dim(qb, n, axis=1, keepdims=False)
        kn = lax.dynamic_slice_in_dim(kp, n * Q_BLOCK, span, axis=1)
        vn = lax.dynamic_slice_in_dim(vp, n * Q_BLOCK, span, axis=1)
        qpos = n * Q_BLOCK + offs_q
        kpos = n * Q_BLOCK + offs_k
        valid = (jnp.abs(qpos[:, None] - kpos[None, :]) <= WINDOW) & (kpos >= 0)[None, :] & (kpos < l)[None, :]
        s_loc = jnp.einsum("bqkgd,bskd->bkgqs", qn, kn).astype(jnp.float32) * scale
        s_loc = jnp.where(valid, s_loc, NEG_INF)
        s_ctx = jnp.einsum("bqkgd,bckd->bkgqc", qn, kc).astype(jnp.float32) * scale
        p = sink_softmax(jnp.concatenate([s_loc, s_ctx], axis=-1), sink).astype(v.dtype)
        return (jnp.einsum("bkgqs,bskd->bqkgd", p[..., :span], vn)
                + jnp.einsum("bkgqc,bckd->bqkgd", p[..., span:], vc))

    out = lax.map(block, jnp.arange(nblk))
    return jnp.moveaxis(out, 0, 1).reshape(b, l, N_HEADS * HEAD_DIM)


def context_attention(qc, kc, vc, sink):
    b, n = qc.shape[0], qc.shape[1]
    s = jnp.einsum("bqkgd,bckd->bkgqc", qc, kc).astype(jnp.float32) * (HEAD_DIM ** -0.5)
    p = sink_softmax(s, sink).astype(vc.dtype)
    return jnp.einsum("bkgqc,bckd->bqkgd", p, vc).reshape(b, n, N_HEADS * HEAD_DIM)


def merge_branches(ys, g, b_merge, w_branch, w_out):
    gates = jax.nn.sigmoid(g.reshape(g.shape[:-1] + (N_BRANCH, D_MODEL)) + b_merge)
    merged = gates[..., 0, :] * (ys[0] @ w_branch[0])
    for i in range(1, N_BRANCH):
        merged = merged + gates[..., i, :] * (ys[i] @ w_branch[i])
    return merged @ w_out


def token_mixer(u, uc, cos, sin, w_in, b_merge, rnn_conv_w, rnn_conv_b, lru_w_a, lru_b_a,
                lru_w_x, lru_b_x, lru_lambda, sc_conv_w, attn_sink, w_branch, w_out, with_ctx_out):
    b, l, _ = u.shape
    n_ctx = uc.shape[1]
    rx, rg, sb, scg, sx, q, k, v, g = split_in(u @ w_in)
    rxc, rgc, sbc, scgc, sxc, qc, kc, vc, gc = split_in(uc @ w_in)

    xa = dwconv(rx, rnn_conv_w, RNN_CONV_LEFT) + rnn_conv_b
    xac = dwconv(rxc, rnn_conv_w, RNN_CONV_LEFT) + rnn_conv_b
    h0 = jnp.zeros((b, RNN_WIDTH), jnp.float32)
    hc_f, last_f = rglru(xac, lru_w_a[0], lru_b_a[0], lru_w_x[0], lru_b_x[0], lru_lambda[0], h0, False)
    hc_b, last_b = rglru(xac, lru_w_a[1], lru_b_a[1], lru_w_x[1], lru_b_x[1], lru_lambda[1], h0, True)
    hl_f, _ = rglru(xa, lru_w_a[0], lru_b_a[0], lru_w_x[0], lru_b_x[0], lru_lambda[0], last_f, False)
    hl_b, _ = rglru(xa, lru_w_a[1], lru_b_a[1], lru_w_x[1], lru_b_x[1], lru_lambda[1], last_b, True)
    ya = (hl_f + hl_b) * jax.nn.gelu(rg)

    yb = sb * dwconv(scg * sx, sc_conv_w, SC_CONV_LEFT)

    sink = attn_sink.reshape(N_KV_HEADS, GROUP)
    q = apply_rope(q.reshape(b, l, N_HEADS, HEAD_DIM), cos, sin).reshape(b, l, N_KV_HEADS, GROUP, HEAD_DIM)
    k = apply_rope(k.reshape(b, l, N_KV_HEADS, HEAD_DIM), cos, sin)
    v = v.reshape(b, l, N_KV_HEADS, HEAD_DIM)
    kc = kc.reshape(b, n_ctx, N_KV_HEADS, HEAD_DIM)
    vc = vc.reshape(b, n_ctx, N_KV_HEADS, HEAD_DIM)
    yatt = banded_attention(q, k, v, kc, vc, sink)

    y = merge_branches((ya, yb, yatt), g, b_merge, w_branch, w_out)
    if not with_ctx_out:
        return y, None

    yac = (hc_f + hc_b) * jax.nn.gelu(rgc)
    ybc = sbc * dwconv(scgc * sxc, sc_conv_w, SC_CONV_LEFT)
    yattc = context_attention(qc.reshape(b, n_ctx, N_KV_HEADS, GROUP, HEAD_DIM), kc, vc, sink)
    yc = merge_branches((yac, ybc, yattc), gc, b_merge, w_branch, w_out)
    return y, yc


def setup_inputs(seed: int = 0) -> dict:
    key = jax.random.key(seed)
    ks = jax.random.split(key, 32)
    f32 = jnp.float32

    def nrm(k, shape, scale):
        return jax.random.normal(k, shape, f32) * scale

    a_c = jax.random.uniform(ks[15], (DEPTH, 2, RNN_WIDTH), f32, minval=0.9, maxval=0.999)
    s_l = a_c ** (1.0 / LRU_C)
    lam = jnp.log(s_l) - jnp.log1p(-s_l)
    return {
        "x": nrm(ks[0], (BATCH, SEQ, D_MODEL), 1.0),
        "c": nrm(ks[1], (BATCH, D_MODEL), 1.0),
        "ctx": nrm(ks[2], (BATCH, CTX_LEN, D_MODEL), 1.0),
        "c_ctx": nrm(ks[3], (D_MODEL,), 1.0),
        "ada_w": nrm(ks[4], (DEPTH, D_MODEL, N_MOD * D_MODEL), 0.5 * D_MODEL ** -0.5),
        "ada_b": nrm(ks[5], (DEPTH, N_MOD * D_MODEL), 0.02),
        "norm_g": 1.0 + nrm(ks[6], (DEPTH, 3, D_MODEL), 0.02),
        "ffn1_w13": nrm(ks[7], (DEPTH, D_MODEL, 2 * D_FF), D_MODEL ** -0.5),
        "ffn1_w2": nrm(ks[8], (DEPTH, D_FF, D_MODEL), D_FF ** -0.5),
        "w_in": nrm(ks[9], (DEPTH, D_MODEL, IN_COLS), D_MODEL ** -0.5),
        "b_merge": nrm(ks[10], (DEPTH, N_BRANCH, D_MODEL), 0.02),
        "rnn_conv_w": nrm(ks[11], (DEPTH, RNN_CONV, RNN_WIDTH), RNN_CONV ** -0.5),
        "rnn_conv_b": nrm(ks[12], (DEPTH, RNN_WIDTH), 0.02),
        "lru_w_a": nrm(ks[13], (DEPTH, 2, RNN_BLOCKS, RNN_BLOCK, RNN_BLOCK), RNN_BLOCK ** -0.5),
        "lru_b_a": nrm(ks[14], (DEPTH, 2, RNN_WIDTH), 0.02),
        "lru_w_x": nrm(ks[16], (DEPTH, 2, RNN_BLOCKS, RNN_BLOCK, RNN_BLOCK), RNN_BLOCK ** -0.5),
        "lru_b_x": nrm(ks[17], (DEPTH, 2, RNN_WIDTH), 0.02),
        "lru_lambda": lam,
        "sc_conv_w": nrm(ks[18], (DEPTH, SC_CONV, SC_WIDTH), SC_CONV ** -0.5),
        "attn_sink": nrm(ks[19], (DEPTH, N_HEADS), 0.5),
        "w_branch": nrm(ks[20], (DEPTH, N_BRANCH, BRANCH_WIDTH, D_MODEL), BRANCH_WIDTH ** -0.5),
        "w_out": nrm(ks[21], (DEPTH, D_MODEL, D_MODEL), D_MODEL ** -0.5),
        "ffn2_w13": nrm(ks[22], (DEPTH, D_MODEL, 2 * D_FF), D_MODEL ** -0.5),
        "ffn2_w2": nrm(ks[23], (DEPTH, D_FF, D_MODEL), D_FF ** -0.5),
        "final_norm_g": 1.0 + nrm(ks[24], (D_MODEL,), 0.02),
    }


def reference(x, c, ctx, c_ctx, ada_w, ada_b, norm_g, ffn1_w13, ffn1_w2, w_in, b_merge,
              rnn_conv_w, rnn_conv_b, lru_w_a, lru_b_a, lru_w_x, lru_b_x, lru_lambda,
              sc_conv_w, attn_sink, w_branch, w_out, ffn2_w13, ffn2_w2, final_norm_g):
    b, l, _ = x.shape
    cos, sin = axial_rope(l)
    silu_c = jax.nn.silu(c)
    silu_cc = jax.nn.silu(c_ctx)
    h, hc = x, ctx
    for layer in range(DEPTH):
        last = layer == DEPTH - 1
        mod = (silu_c @ ada_w[layer] + ada_b[layer]).reshape(b, N_MOD, 1, D_MODEL)
        modc = (silu_cc @ ada_w[layer] + ada_b[layer]).reshape(N_MOD, D_MODEL)

        u = modulate(rmsnorm(h, norm_g[layer, 0]), mod[:, 0], mod[:, 1])
        h = h + 0.5 * mod[:, 2] * swiglu(u, ffn1_w13[layer], ffn1_w2[layer])
        uc = modulate(rmsnorm(hc, norm_g[layer, 0]), modc[0], modc[1])
        hc = hc + 0.5 * modc[2] * swiglu(uc, ffn1_w13[layer], ffn1_w2[layer])

        u = modulate(rmsnorm(h, norm_g[layer, 1]), mod[:, 3], mod[:, 4])
        uc = modulate(rmsnorm(hc, norm_g[layer, 1]), modc[3], modc[4])
        y, yc = token_mixer(u, uc, cos, sin, w_in[layer], b_merge[layer], rnn_conv_w[layer],
                            rnn_conv_b[layer], lru_w_a[layer], lru_b_a[layer], lru_w_x[layer],
                            lru_b_x[layer], lru_lambda[layer], sc_conv_w[layer], attn_sink[layer],
                            w_branch[layer], w_out[layer], not last)
        h = h + mod[:, 5] * y

        u = modulate(rmsnorm(h, norm_g[layer, 2]), mod[:, 6], mod[:, 7])
        h = h + 0.5 * mod[:, 8] * swiglu(u, ffn2_w13[layer], ffn2_w2[layer])
        if not last:
            hc = hc + modc[5] * yc
            uc = modulate(rmsnorm(hc, norm_g[layer, 2]), modc[6], modc[7])
            hc = hc + 0.5 * modc[8] * swiglu(uc, ffn2_w13[layer], ffn2_w2[layer])
    return rmsnorm(h, final_norm_g)
```

```python
import numpy as np
import ml_dtypes
from contextlib import ExitStack
import concourse.bass as bass
import concourse.mybir as mybir
from concourse.bass_utils import run_bass_kernel_spmd

F32 = mybir.dt.float32
BF16 = mybir.dt.bfloat16
AF = mybir.ActivationFunctionType
ALU = mybir.AluOpType

D = 2048
DC = 16
DFF = 5504
FC = 43
NCTX = 256
NMOD = 9
INCOLS = 12800
NLAYER = 2
EPS = 1e-6
TT = 512
SB = 1024
ATT_SCALE = 128 ** -0.5
CBFN = 1280


class Buf:
    __slots__ = ("name", "t", "last_w", "readers", "sem")

    def __init__(self, name, t=None):
        self.name = name
        self.t = t
        self.last_w = None
        self.readers = []
        self.sem = None

    def __getitem__(self, idx):
        return self.t[idx]


class DmaSem:
    def __init__(self, sem):
        self.sem = sem
        self.count = 0
        self.last_op = None


class Op:
    __slots__ = ("eng", "fn", "deps", "needed", "cnt", "dsem", "dcnt")

    def __init__(self, eng, fn, deps):
        self.eng = eng
        self.fn = fn
        self.deps = deps
        self.needed = False
        self.cnt = 0
        self.dsem = None
        self.dcnt = 0


class Prog:
    ENGS = ("pe", "act", "dve", "pool", "sp")

    def __init__(self, nc, es):
        self.nc = nc
        self.es = es
        self.ops = {e: [] for e in self.ENGS}
        self.last = {e: None for e in self.ENGS}
        self.fence = {e: [] for e in self.ENGS}
        self.dsems = []
        self.banks = []
        self.bank_i = 0
        self.esem = {e: es.enter_context(nc.semaphore("s_" + e)) for e in ("pe", "act", "dve", "pool")}
        self.root = es
        self.pool_free = {}
        self.phase_bufs = None
        self.nsem = 0

    def sbuf(self, name, shape, dt, dma=False):
        self.nsb = getattr(self, "nsb", 0) + 1
        t = self.es.enter_context(self.nc.sbuf_tensor("sb%d_%s" % (self.nsb, name), shape, dt))
        b = Buf(name, t)
        if dma:
            self.add_dsem(b)
        return b

    def add_dsem(self, b):
        b.sem = {}
        if self.phase_bufs is not None:
            self.phase_bufs.append(b)

    def get_dsem(self, b, q):
        if q not in b.sem:
            free = self.pool_free.setdefault(q, [])
            if free:
                ds = free.pop()
            else:
                ds = DmaSem(self.root.enter_context(self.nc.semaphore("dsem%d" % self.nsem)))
                self.nsem += 1
                self.dsems.append(ds)
            b.sem[q] = ds
        return b.sem[q]

    def phase(self):
        prog = self

        class _Ph:
            def __enter__(self_):
                self_.st = ExitStack()
                self_.prev = prog.es
                self_.prev_bufs = prog.phase_bufs
                prog.es = self_.st
                prog.phase_bufs = []
                return self_

            def __exit__(self_, *a):
                prog.barrier()
                for b in prog.phase_bufs:
                    for q_, ds_ in b.sem.items():
                        prog.pool_free.setdefault(q_, []).append(ds_)
                prog.phase_bufs = self_.prev_bufs
                prog.es = self_.prev
                self_.st.close()
                return False
        return _Ph()

    def make_banks(self):
        for i in range(8):
            t = self.es.enter_context(self.nc.psum_tensor("bank%d" % i, [128, 512], F32))
            self.banks.append(Buf("bank%d" % i, t))

    def bank(self):
        b = self.banks[self.bank_i % 8]
        self.bank_i += 1
        return b

    def _deps(self, eng, reads, writes):
        deps = []
        for r in reads:
            if r.last_w is not None:
                deps.append(r.last_w)
        for w in writes:
            if w.last_w is not None:
                deps.append(w.last_w)
            deps.extend(w.readers)
        deps.extend(self.fence[eng])
        self.fence[eng] = []
        out = []
        seen = set()
        for d in deps:
            if id(d) in seen:
                continue
            seen.add(id(d))
            if d.eng == "pe" and eng == "pe" and d.dsem is None:
                continue
            out.append(d)
        return out

    def op(self, eng, fn, reads=(), writes=()):
        o = Op(eng, fn, self._deps(eng, reads, writes))
        for d in o.deps:
            d.needed = True
        self.ops[eng].append(o)
        self.last[eng] = o
        for r in reads:
            self._add_reader(r, o)
        for w in writes:
            w.last_w = o
            w.readers = []
        return o

    @staticmethod
    def _add_reader(buf, o):
        key = o.eng if o.dsem is None else id(o.dsem)
        rl = buf.readers
        for i, x in enumerate(rl):
            kx = x.eng if x.dsem is None else id(x.dsem)
            if kx == key:
                rl[i] = o
                return
        rl.append(o)

    def dma(self, q, out, in_, sbuf_side, reads=(), writes=()):
        ds = self.get_dsem(sbuf_side, q)
        o = Op(q, None, self._deps(q, reads, writes))
        if ds.last_op is not None and all(ds.last_op is not d for d in o.deps):
            o.deps.append(ds.last_op)
        for d in o.deps:
            d.needed = True
        ds.count += 1
        o.dsem = ds
        o.dcnt = ds.count
        ds.last_op = o
        o.fn = lambda e, out=out, in_=in_: e.dma_start(out=out, in_=in_)
        self.ops[q].append(o)
        self.last[q] = o
        for r in reads:
            self._add_reader(r, o)
        for w in writes:
            w.last_w = o
            w.readers = []
        return o

    def barrier(self):
        lasts = [self.last[e] for e in self.ENGS if self.last[e] is not None]
        for ds in self.dsems:
            if ds.last_op is not None:
                lasts.append(ds.last_op)
        for e in self.ENGS:
            self.fence[e] = list(lasts)

    def mm(self, out, lhsT, rhs, start, stop, reads, writes):
        return self.op("pe", lambda e: e.matmul(out, lhsT, rhs, start=start, stop=stop), reads, writes)

    def act(self, out, in_, func, reads, writes, bias=None, scale=None, eng="act"):
        kw = {}
        if bias is not None:
            kw["bias"] = bias
        if scale is not None:
            kw["scale"] = scale
        return self.op(eng, lambda e: e.activation(out=out, in_=in_, func=func, **kw), reads, writes)

    def tt(self, eng, out, in0, in1, op, reads, writes):
        return self.op(eng, lambda e: e.tensor_tensor(out=out, in0=in0, in1=in1, op=op), reads, writes)

    def ts(self, eng, out, in0, s1, s2, op0, op1, reads, writes):
        if s2 is None:
            return self.op(eng, lambda e: e.tensor_single_scalar(out=out, in_=in0, scalar=s1, op=op0), reads, writes)
        return self.op(eng, lambda e: e.tensor_scalar(out=out, in0=in0, scalar1=s1, scalar2=s2, op0=op0, op1=op1),
                       reads, writes)

    def stt(self, eng, out, in0, scalar, in1, op0, op1, reads, writes):
        return self.op(eng, lambda e: e.scalar_tensor_tensor(out=out, in0=in0, scalar=scalar, in1=in1, op0=op0, op1=op1),
                       reads, writes)

    def copy(self, eng, out, in_, reads, writes):
        if eng == "act":
            return self.op(eng, lambda e: e.activation(out=out, in_=in_, func=AF.Identity), reads, writes)
        return self.op(eng, lambda e: e.tensor_copy(out=out, in_=in_), reads, writes)

    def memset(self, eng, ap, val, writes):
        return self.op(eng, lambda e: e.memset(ap, val), (), writes)

    def emit(self, final_ops):
        nc = self.nc
        for e in ("pe", "act", "dve", "pool"):
            c = 0
            for o in self.ops[e]:
                if o.dsem is None and o.needed:
                    c += 1
                    o.cnt = c
                elif o.dsem is None:
                    o.cnt = -1
        for o in final_ops:
            assert o.dsem is not None
        handles = {"pe": None, "act": None, "dve": None, "pool": None, "sp": None}
        prog = self

        def run(eng_name, eng):
            waited = {}
            for o in prog.ops[eng_name]:
                for d in o.deps:
                    if d.dsem is not None:
                        key = id(d.dsem)
                        val = 16 * d.dcnt
                        sem = d.dsem.sem
                    else:
                        key = d.eng
                        val = d.cnt
                        sem = prog.esem[d.eng]
                        assert val > 0
                    if waited.get(key, 0) < val:
                        eng.wait_ge(sem, val)
                        waited[key] = val
                ins = o.fn(eng)
                if o.dsem is not None:
                    ins.then_inc(o.dsem.sem, 16)
                elif o.needed:
                    ins.then_inc(prog.esem[eng_name], 1)
            if eng_name == "pool":
                for o in final_ops:
                    eng.wait_ge(o.dsem.sem, 16 * o.dcnt)

        with nc.Block() as block:
            @block.tensor
            def _(e):
                run("pe", e)

            @block.scalar
            def _(e):
                run("act", e)

            @block.vector
            def _(e):
                run("dve", e)

            @block.gpsimd
            def _(e):
                run("pool", e)

            @block.sync
            def _(e):
                run("sp", e)


class VecLayout:
    def __init__(self):
        self.off = {}
        self.n = 0

    def add(self, name, ncols):
        self.off[name] = (self.n, ncols)
        self.n += ncols

    def sl(self, name, i=0, n=None):
        o, c = self.off[name]
        if n is None:
            n = c - i
        return slice(o + i, o + i + n)


def make_layout():
    L = VecLayout()
    L.add("cvec", 32)
    for l in range(NLAYER):
        L.add("ada_b%d" % l, 144)
        L.add("norm_g%d" % l, 48)
        L.add("b_merge%d" % l, 48)
        L.add("rconv_w%d" % l, 32)
        L.add("rconv_b%d" % l, 8)
        L.add("lru_b_a%d" % l, 16)
        L.add("lru_b_x%d" % l, 16)
        L.add("lam%d" % l, 16)
        L.add("sconv_w%d" % l, 24)
        L.add("sink%d" % l, 8)
    L.add("fnorm_g", 16)
    return L


def fm(v):
    v = np.asarray(v, np.float32)
    return np.ascontiguousarray(v.reshape(-1, 128).T)


def build(T):
    TTOT = T + NCTX
    nc = bass.Bass("TRN2", target_bir_lowering=False)
    L = make_layout()

    def din(name, shape, dt=F32):
        return nc.dram_tensor(name, list(shape), dt, kind="ExternalInput").ap()

    def dscr(name, shape, dt):
        return nc.dram_tensor(name, list(shape), dt, kind="Internal").ap()

    xT = din("xT", [D, TTOT])
    vecs_d = din("vecs", [128, L.n])
    ada_w = din("ada_w", [NLAYER, D, NMOD * D])
    w13_d = [din("ffn1_w13", [NLAYER, D, 2 * DFF]), din("ffn2_w13", [NLAYER, D, 2 * DFF])]
    w2_d = [din("ffn1_w2", [NLAYER, DFF, D]), din("ffn2_w2", [NLAYER, DFF, D])]
    w_in_d = din("w_in", [NLAYER, D, INCOLS])
    wbr_d = din("w_branch", [NLAYER, 3, 1024, D])
    wout_d = din("w_out", [NLAYER, D, D])
    lruw_d = din("lru_w", [NLAYER, 128, 32, 128])
    rope_d = din("rope", [2, 128, T])
    cbf_d = din("cbf", [128, CBFN], BF16)
    outT = nc.dram_tensor("outT", [D, T], F32, kind="ExternalOutput").ap()

    hbuf = dscr("hbuf", [D, TTOT], F32)
    zr = dscr("zr", [5 * 1024, TTOT], F32)
    zq = dscr("zq", [10 * 128, TTOT], BF16)
    zv = dscr("zv", [TTOT, 256], BF16)
    zg = dscr("zg", [3 * D, TTOT], BF16)
    hfb = dscr("hfb", [1024, TTOT], F32)
    ybuf = dscr("ybuf", [3 * 1024, TTOT], BF16)
    pw13 = [[dscr("pw13_%d_%d" % (f, l), [FC, 128, DC, 256], BF16) for l in range(NLAYER)] for f in range(2)]
    pw2 = [[dscr("pw2_%d_%d" % (f, l), [DC, 128, FC, 128], BF16) for l in range(NLAYER)] for f in range(2)]
    pwin = [dscr("pwin_%d" % l, [50, 128, DC, 256], BF16) for l in range(NLAYER)]
    pwbr = [[dscr("pwbr_%d_%d" % (l, i), [8, 128, 8, 256], BF16) for i in range(3)] for l in range(NLAYER)]
    pwout = [dscr("pwout_%d" % l, [8, 128, DC, 256], BF16) for l in range(NLAYER)]

    xTv = xT.rearrange("(c p) t -> p c t", p=128)
    hbv = hbuf.rearrange("(c p) t -> p c t", p=128)
    zrv = zr.rearrange("(c p) t -> p c t", p=128)
    zqv = zq.rearrange("(c p) t -> p c t", p=128)
    zgv = zg.rearrange("(c p) t -> p c t", p=128)
    zg4 = zg.rearrange("(i c p) t -> p i c t", i=3, p=128)
    ybv = ybuf.rearrange("(c p) t -> p c t", p=128)
    outv = outT.rearrange("(c p) t -> p c t", p=128)

    with ExitStack() as es:
        P = Prog(nc, es)
        P.make_banks()
        d_w = Buf("d_weights")
        d_in = Buf("d_inputs")
        d_h = {}
        d_z = Buf("d_z")
        d_y = Buf("d_y")
        d_hf = Buf("d_hf")
        d_out = Buf("d_out")

        def dh(key):
            if key not in d_h:
                d_h[key] = Buf("d_h%s" % (key,))
            return d_h[key]

        vecs = P.sbuf("vecs", [128, L.n], F32, dma=True)
        DVN = NLAYER * 2 * (144 + 9 * 16) + NLAYER * 32 + 64
        dv = P.sbuf("dv", [128, DVN], F32)
        cbf = P.sbuf("cbf", [128, CBFN], BF16, dma=True)
        ones_f = P.sbuf("ones_f", [128, 128], F32)
        ones_b = P.sbuf("ones_b", [128, 128], BF16)
        perm_ap = cbf[:, 0:128]
        ident_ap = cbf[:, 128:256]
        maskp_ap = cbf[:, 256:768]
        maskn_ap = cbf[:, 768:1280]

        dvo = {}
        dvn = [0]

        def dvadd(name, n):
            dvo[name] = (dvn[0], n)
            dvn[0] += n

        for l in range(NLAYER):
            for s in range(2):
                dvadd("mod%d%d" % (l, s), 144)
                for nm in ("A1", "B1", "G1", "A2", "B2", "G2", "A3", "B3", "G3"):
                    dvadd("%s%d%d" % (nm, l, s), 16)
            dvadd("lc%d" % l, 16)
            dvadd("lc2%d" % l, 16)
        dvadd("sc", 32)
        dvadd("tmp", 32)
        assert dvn[0] <= DVN

        def DV(name, i=0, n=None):
            o, c = dvo[name]
            if n is None:
                n = c - i
            return dv[:, o + i:o + i + n]

        def VC(name, i=0, n=None):
            return vecs[:, L.sl(name, i, n)]

        P.dma("sp", vecs[:, :], vecs_d, vecs, reads=[d_in], writes=[vecs])
        P.dma("sp", cbf[:, :], cbf_d, cbf, reads=[d_in], writes=[cbf])
        P.memset("dve", ones_f[:, :], 1.0, [ones_f])
        P.memset("dve", ones_b[:, :], 1.0, [ones_b])

        import os
        kstop0 = int(os.environ.get("KSTOP", "99"))
        with P.phase():
            apan = [P.sbuf("apan%d" % i, [128, DC, 512], F32, dma=True) for i in range(2)]
            P.act(DV("sc"), VC("cvec"), AF.Silu, [vecs], [dv])
            sc3 = DV("sc").rearrange("p (k s) -> p k s", s=2)
            pi = 0
            for l in range(NLAYER):
                bk = P.bank()
                bk3 = bk[:, 0:288].rearrange("p (j s) -> p j s", s=2)
                for pq in range(36):
                    pan = apan[pi % 2]
                    pi += 1
                    src = ada_w[l].rearrange("(k p) n -> p k n", p=128)[:, :, pq * 512:(pq + 1) * 512]
                    P.dma("sp", pan[:, :, :], src, pan, reads=[d_in], writes=[pan])
                    for jj in range(4):
                        j = pq * 4 + jj
                        for kc in range(DC):
                            P.mm(bk3[:, j, :], pan[:, kc, jj * 128:(jj + 1) * 128], sc3[:, kc, :],
                                 kc == 0, kc == DC - 1, [pan, dv], [bk])
                for s in range(2):
                    P.tt("dve", DV("mod%d%d" % (l, s)), bk3[:, :, s], VC("ada_b%d" % l), ALU.add, [bk, vecs], [dv])
                    md = lambda m, l=l, s=s: DV("mod%d%d" % (l, s), m * 16, 16)
                    ng = lambda i, l=l: VC("norm_g%d" % l, i * 16, 16)
                    for (nm, gi, sh, scl, gt, gmul) in (("1", 0, 0, 1, 2, 0.5), ("2", 1, 3, 4, 5, 1.0), ("3", 2, 6, 7, 8, 0.5)):
                        P.stt("dve", DV("A%s%d%d" % (nm, l, s)), md(scl), 1.0, ng(gi), ALU.add, ALU.mult, [dv, vecs], [dv])
                        P.copy("dve", DV("B%s%d%d" % (nm, l, s)), md(sh), [dv], [dv])
                        P.ts("dve", DV("G%s%d%d" % (nm, l, s)), md(gt), gmul, None, ALU.mult, None, [dv], [dv])
                P.act(DV("tmp", 0, 16), VC("lam%d" % l), AF.Exp, [vecs], [dv], scale=-1.0)
                P.act(DV("tmp", 16, 16), DV("tmp", 0, 16), AF.Ln, [dv], [dv], bias=1.0)
                P.ts("dve", DV("lc%d" % l), DV("tmp", 16, 16), -8.0, None, ALU.mult, None, [dv], [dv])
                P.ts("dve", DV("lc2%d" % l), DV("tmp", 16, 16), -16.0, None, ALU.mult, None, [dv], [dv])

        with P.phase():
            STG = 11008
            stg = [P.sbuf("stg%d" % i, [128, STG], F32, dma=True) for i in range(2)]
            pbf = [P.sbuf("pbf%d" % i, [128, STG], BF16, dma=True) for i in range(2)]
            cnt = [0]
            ceng = ["act", "dve", "pool"]
            cei = [0]

            def cast(out, in_, sbuf_in, sbuf_out):
                e = ceng[cei[0] % 3]
                cei[0] += 1
                P.copy(e, out, in_, [sbuf_in], [sbuf_out])

            def prep_generic(W, kcs, npan, dest, G):
                Wv = W.rearrange("(k p) n -> p k n", p=128)
                for q0 in range(0, npan, G):
                    g = min(G, npan - q0)
                    s_ = stg[cnt[0] % 2]
                    b_ = pbf[cnt[0] % 2]
                    cnt[0] += 1
                    sv = s_[:, 0:kcs * g * 256].rearrange("p (k n) -> p k n", k=kcs)
                    P.dma("sp", sv, Wv[:, :, q0 * 256:(q0 + g) * 256], s_, reads=[d_in], writes=[s_])
                    bv = b_[:, 0:g * kcs * 256].rearrange("p (q k n) -> p q k n", q=g, k=kcs)
                    for q in range(g):
                        cast(bv[:, q, :, :], sv[:, :, q * 256:(q + 1) * 256], s_, b_)
                    P.dma("pool", dest[q0:q0 + g].rearrange("q p k n -> p q (k n)"),
                          b_[:, 0:g * kcs * 256].rearrange("p (q m) -> p q m", q=g), b_, reads=[b_], writes=[d_w])

            def prep_w13(W, dest):
                Wv = W.rearrange("(k p) n -> p k n", p=128)
                G = 2
                for j0 in range(0, FC, G):
                    g = min(G, FC - j0)
                    s_ = stg[cnt[0] % 2]
                    b_ = pbf[cnt[0] % 2]
                    cnt[0] += 1
                    sv = s_[:, 0:DC * 2 * g * 128].rearrange("p (k s n) -> p k s n", k=DC, s=2)
                    P.dma("sp", sv[:, :, 0, :], Wv[:, :, j0 * 128:(j0 + g) * 128], s_, reads=[d_in], writes=[s_])
                    P.dma("sp", sv[:, :, 1, :], Wv[:, :, DFF + j0 * 128:DFF + (j0 + g) * 128], s_, reads=[d_in], writes=[s_])
                    bv = b_[:, 0:g * DC * 256].rearrange("p (q k s n) -> p q k s n", q=g, k=DC, s=2)
                    for q in range(g):
                        for s in range(2):
                            cast(bv[:, q, :, s, :], sv[:, :, s, q * 128:(q + 1) * 128], s_, b_)
                    P.dma("pool", dest[j0:j0 + g].rearrange("q p k n -> p q (k n)"),
                          b_[:, 0:g * DC * 256].rearrange("p (q m) -> p q m", q=g), b_, reads=[b_], writes=[d_w])

            def prep_w2(W, dest):
                Wv = W.rearrange("(k p) n -> p k n", p=128)
                for c0 in range(0, DC, 2):
                    s_ = stg[cnt[0] % 2]
                    b_ = pbf[cnt[0] % 2]
                    cnt[0] += 1
                    sv = s_[:, 0:FC * 256].rearrange("p (k n) -> p k n", k=FC)
                    P.dma("sp", sv, Wv[:, :, c0 * 128:(c0 + 2) * 128], s_, reads=[d_in], writes=[s_])
                    bv = b_[:, 0:2 * FC * 128].rearrange("p (q k n) -> p q k n", q=2, k=FC)
                    for q in range(2):
                        cast(bv[:, q, :, :], sv[:, :, q * 128:(q + 1) * 128], s_, b_)
                    P.dma("pool", dest[c0:c0 + 2].rearrange("q p k n -> p q (k n)"),
                          b_[:, 0:2 * FC * 128].rearrange("p (q m) -> p q m", q=2), b_, reads=[b_], writes=[d_w])

            for l in range(NLAYER if kstop0 >= 1 else 0):
                for f in range(2):
                    prep_w13(w13_d[f][l], pw13[f][l])
                    prep_w2(w2_d[f][l], pw2[f][l])
                prep_generic(w_in_d[l], DC, 50, pwin[l], 2)
                for i in range(3):
                    prep_generic(wbr_d[l, i], 8, 8, pwbr[l][i], 4)
                prep_generic(wout_d[l], DC, 8, pwout[l], 2)

        def tiles(with_ctx=True):
            ts_ = [(t0, TT, 0) for t0 in range(0, T, TT)]
            if with_ctx:
                ts_.append((T, NCTX, 1))
            return ts_

        class TileCtx:
            pass

        def alloc_tile_bufs():
            C = TileCtx()
            C.htile = P.sbuf("htile", [128, DC, TT], F32, dma=True)
            C.ubuf = P.sbuf("ubuf", [128, DC, TT], BF16)
            C.big = P.sbuf("big", [128, FC * TT], BF16, dma=True)
            C.ring8 = [P.sbuf("ring8_%d" % i, [128, DC * 256], BF16, dma=True) for i in range(3)]
            C.ring11 = [P.sbuf("ring11_%d" % i, [128, FC * 128], BF16, dma=True) for i in range(2)]
            C.r8i = 0
            C.r11i = 0
            C.tmpf = [P.sbuf("tmpf%d" % i, [128, TT], F32) for i in range(4)]
            C.tfi = 0
            C.rstd = P.sbuf("rstd", [128, TT], F32)
            C.tmpb = [P.sbuf("tmpb%d" % i, [128, TT], BF16) for i in range(3)]
            C.tbi = 0
            return C

        def tmp(C):
            b = C.tmpf[C.tfi % 4]
            C.tfi += 1
            return b

        def next8(C):
            b = C.ring8[C.r8i % 3]
            C.r8i += 1
            return b

        def next11(C):
            b = C.ring11[C.r11i % 2]
            C.r11i += 1
            return b

        def rms_stats(C, n):
            htile = C.htile
            bk = P.bank()
            for c in range(DC):
                sq = C.tmpb[C.tbi % 3]
                C.tbi += 1
                P.act(sq[:, :n], htile[:, c, :n], AF.Square, [htile], [sq])
                P.mm(bk[:, :n], ones_b[:, :], sq[:, :n], c == 0, c == DC - 1, [ones_b, sq], [bk])
            P.act(C.rstd[:, :n], bk[:, :n], AF.Sqrt, [bk], [C.rstd], scale=1.0 / D, bias=EPS)
            P.op("dve", lambda e: e.reciprocal(out=C.rstd[:, :n], in_=C.rstd[:, :n]), [C.rstd], [C.rstd])

        def rmsnorm_mod(C, n, A, B):
            rms_stats(C, n)
            for c in range(DC):
                t_ = tmp(C)
                P.stt("dve", t_[:, :n], C.htile[:, c, :n], A[:, c:c + 1], C.rstd[:, :n], ALU.mult, ALU.mult,
                      [C.htile, C.rstd, dv, vecs], [t_])
                P.act(C.ubuf[:, c, :n], t_[:, :n], AF.Identity, [t_, dv], [C.ubuf], bias=B[:, c:c + 1])

        def ffn(C, f, l, n, G):
            htile, ubuf, big = C.htile, C.ubuf, C.big
            actv = big[:, 0:FC * TT].rearrange("p (j t) -> p j t", j=FC)
            for j in range(FC):
                slot = next8(C)
                sv = slot[:, :].rearrange("p (k n) -> p k n", k=DC)
                P.dma("sp", slot[:, :], pw13[f][l][j].rearrange("p k n -> p (k n)"), slot, reads=[d_w], writes=[slot])
                bg = P.bank()
                bu = P.bank()
                for kc in range(DC):
                    P.mm(bg[:, :n], sv[:, kc, 0:128], ubuf[:, kc, :n], kc == 0, kc == DC - 1, [slot, ubuf], [bg])
                for kc in range(DC):
                    P.mm(bu[:, :n], sv[:, kc, 128:256], ubuf[:, kc, :n], kc == 0, kc == DC - 1, [slot, ubuf], [bu])
                sg = tmp(C)
                P.act(sg[:, :n], bg[:, :n], AF.Silu, [bg], [sg])
                P.tt("dve", actv[:, j, :n], sg[:, :n], bu[:, :n], ALU.mult, [sg, bu], [big])
            for c in range(DC):
                slot = next11(C)
                sv = slot[:, :].rearrange("p (k n) -> p k n", k=FC)
                P.dma("sp", slot[:, :], pw2[f][l][c].rearrange("p k n -> p (k n)"), slot, reads=[d_w], writes=[slot])
                bk = P.bank()
                for j in range(FC):
                    P.mm(bk[:, :n], sv[:, j, :], actv[:, j, :n], j == 0, j == FC - 1, [slot, big], [bk])
                P.stt("dve", htile[:, c, :n], bk[:, :n], G[:, c:c + 1], htile[:, c, :n], ALU.mult, ALU.add,
                      [bk, htile, dv], [htile])

        def phaseA(l):
            with P.phase():
                C = alloc_tile_bufs()
                htile, ubuf = C.htile, C.ubuf
                ropets = [P.sbuf("ropet%d" % i, [128, 2, TT], F32, dma=True) for i in range(2)]
                qbf = [P.sbuf("qbf%d" % i, [128, TT], BF16) for i in range(2)]
                stf = [P.sbuf("stf%d" % i, [128, 2, TT], F32, dma=True) for i in range(3)]
                stb = [P.sbuf("stb%d" % i, [128, 2, TT], BF16, dma=True) for i in range(3)]
                stgv = P.sbuf("stgv", [128, 4, 256], BF16, dma=True)
                qbi = 0
                sfi = 0
                sbi = 0
                tl = tiles()
                src = xTv if l == 0 else hbv

                def load_h(ti):
                    (t0_, n_, s_) = tl[ti]
                    P.dma("sp", htile[:, :, :n_], src[:, :, t0_:t0_ + n_], htile,
                          reads=[d_in if l == 0 else dh((l, "c", t0_))], writes=[htile])
                load_h(0)
                for ti, (t0, n, s) in enumerate(tl):
                    ropet = ropets[ti % 2]
                    if not s:
                        P.dma("sp", ropet[:, :, :n], rope_d.rearrange("s p t -> p s t")[:, :, t0:t0 + n], ropet,
                              reads=[d_in], writes=[ropet])
                    rmsnorm_mod(C, n, DV("A1%d%d" % (l, s)), DV("B1%d%d" % (l, s)))
                    ksub = int(os.environ.get("KSUB", "99"))
                    if ksub >= 2:
                        ffn(C, 0, l, n, DV("G1%d%d" % (l, s)))
                    P.dma(os.environ.get("KSTQ", "pool"), hbv[:, :, t0:t0 + n], htile[:, :, :n], htile, reads=[htile], writes=[dh((l, "a", t0))])
                    if ksub < 3:
                        if ti + 1 < len(tl):
                            load_h(ti + 1)
                        continue
                    rmsnorm_mod(C, n, DV("A2%d%d" % (l, s)), DV("B2%d%d" % (l, s)))
                    if ti + 1 < len(tl):
                        load_h(ti + 1)
                    for q in range({3: 20, 4: 25, 5: 26}.get(ksub, 50)):
                        slot = next8(C)
                        sv = slot[:, :].rearrange("p (k n) -> p k n", k=DC)
                        P.dma("sp", slot[:, :], pwin[l][q].rearrange("p k n -> p (k n)"), slot, reads=[d_w], writes=[slot])
                        if q == 25:
                            for sb_ in range(n // 128):
                                bk = P.bank()
                                for kc in range(DC):
                                    P.mm(bk[:, 0:256], ubuf[:, kc, sb_ * 128:(sb_ + 1) * 128], sv[:, kc, :],
                                         kc == 0, kc == DC - 1, [slot, ubuf], [bk])
                                P.copy("act", stgv[:, sb_, :], bk[:, 0:256], [bk], [stgv])
                            P.dma("pool", zv[t0:t0 + n, :].rearrange("(b p) f -> p b f", p=128), stgv[:, 0:n // 128, :],
                                  stgv, reads=[stgv], writes=[d_z])
                            continue
                        if q < 20:
                            st_ = stf[sfi % 3]
                            sfi += 1
                        else:
                            st_ = stb[sbi % 3]
                            sbi += 1
                        for half in range(2):
                            ch = 2 * q + half
                            bk = P.bank()
                            for kc in range(DC):
                                P.mm(bk[:, :n], sv[:, kc, half * 128:(half + 1) * 128], ubuf[:, kc, :n],
                                     kc == 0, kc == DC - 1, [slot, ubuf], [bk])
                            if ch < 40:
                                P.copy("dve" if half == 0 else "act", st_[:, half, :n], bk[:, :n], [bk], [st_])
                            elif ch < 50:
                                if s:
                                    P.copy("act", st_[:, half, :n], bk[:, :n], [bk], [st_])
                                else:
                                    qb = qbf[qbi % 2]
                                    qbi += 1
                                    P.copy("act", qb[:, :n], bk[:, :n], [bk], [qb, bk])
                                    bk2 = P.bank()
                                    P.mm(bk2[:, :n], perm_ap, qb[:, :n], True, True, [cbf, qb], [bk2])
                                    t1 = tmp(C)
                                    t2 = tmp(C)
                                    P.tt("dve", t1[:, :n], bk[:, :n], ropet[:, 0, :n], ALU.mult, [bk, ropet], [t1])
                                    P.tt("dve", t2[:, :n], bk2[:, :n], ropet[:, 1, :n], ALU.mult, [bk2, ropet], [t2])
                                    P.tt("dve", st_[:, half, :n], t1[:, :n], t2[:, :n], ALU.add, [t1, t2], [st_])
                            else:
                                gi = ch - 52
                                P.act(st_[:, half, :n], bk[:, :n], AF.Sigmoid, [bk, vecs], [st_],
                                      bias=VC("b_merge%d" % l, gi, 1))
                        c0 = 2 * q
                        if c0 < 40:
                            P.dma("pool", zrv[:, c0:c0 + 2, t0:t0 + n], st_[:, :, :n], st_, reads=[st_], writes=[d_z])
                        elif c0 < 50:
                            P.dma("pool", zqv[:, c0 - 40:c0 - 38, t0:t0 + n], st_[:, :, :n], st_, reads=[st_], writes=[d_z])
                        else:
                            P.dma("pool", zgv[:, c0 - 52:c0 - 50, t0:t0 + n], st_[:, :, :n], st_, reads=[st_], writes=[d_z])

        def phaseB(l, want_ctx):
            with P.phase():
                lw = P.sbuf("lw", [128, 32, 128], BF16, dma=True)
                P.dma("pool", lw[:, :, :], lruw_d[l], lw, reads=[d_in], writes=[lw])
                st = P.sbuf("st", [128, 8, 2], F32)
                zero1 = P.sbuf("zero1", [128, 1], F32)
                P.memset("dve", zero1[:, :], 0.0, [zero1])
                carry = P.sbuf("carry", [128, 1], F32)
                xaf = P.sbuf("xaf", [128, T], F32)
                xabf = P.sbuf("xabf", [128, T], BF16)
                xac = P.sbuf("xac", [128, NCTX], F32)
                xacb = P.sbuf("xacb", [128, NCTX], BF16)
                rxh = [P.sbuf("rxh%d" % i, [128, SB + 3], F32, dma=True) for i in range(2)]
                ctmp = P.sbuf("ctmp", [128, SB], F32)
                rb = P.sbuf("rb", [128, SB], F32)
                igb = [P.sbuf("igb%d" % i, [128, SB], F32) for i in range(2)]
                ab = [P.sbuf("ab%d" % i, [128, SB], F32) for i in range(2)]
                a2b = [P.sbuf("a2b%d" % i, [128, SB], F32) for i in range(2)]
                ub = P.sbuf("ub", [128, SB], F32)
                hb = [P.sbuf("hb%d" % i, [128, SB], F32, dma=True) for i in range(2)]
                hfl = [P.sbuf("hfl%d" % i, [128, SB], F32, dma=True) for i in range(2)]
                rgl = [P.sbuf("rgl%d" % i, [128, SB], F32, dma=True) for i in range(2)]
                yab = [P.sbuf("yab%d" % i, [128, SB], BF16, dma=True) for i in range(2)]
                scgl = P.sbuf("scgl", [128, SB + 2], F32, dma=True)
                sxl = P.sbuf("sxl", [128, SB + 2], F32, dma=True)
                sbl = P.sbuf("sbl", [128, SB], F32, dma=True)
                pb_ = P.sbuf("pb", [128, SB + 2], F32)
                ycv = P.sbuf("ycv", [128, SB], F32)
                yct = P.sbuf("yct", [128, SB], F32)
                ybo = [P.sbuf("ybo%d" % i, [128, SB], BF16, dma=True) for i in range(2)]
                cnt = {"rx": 0, "g": 0, "h": 0, "c": 0, "y": 0}

                def halo_load(buf, grp, n, base, length, s0, Lb, left, right):
                    lo = max(0, s0 - left)
                    hi = min(length, s0 + Lb + right)
                    o0 = s0 - left
                    if lo > o0:
                        P.memset("pool", buf[:, 0:lo - o0], 0.0, [buf])
                    if hi < s0 + Lb + right:
                        P.memset("pool", buf[:, hi - o0:Lb + left + right], 0.0, [buf])
                    row = grp * 8 + n
                    P.dma("sp", buf[:, lo - o0:hi - o0], zrv[:, row, base + lo:base + hi], buf, reads=[d_z], writes=[buf])

                def conv_block(n, base, length, s0, Lb, xa_t, xab_t, c0):
                    rx_ = rxh[cnt["rx"] % 2]
                    cnt["rx"] += 1
                    halo_load(rx_, 0, n, base, length, s0, Lb, 2, 1)
                    wv = lambda k: VC("rconv_w%d" % l, k * 8 + n, 1)
                    P.ts("pool", xa_t[:, c0:c0 + Lb], rx_[:, 0:Lb], wv(0), VC("rconv_b%d" % l, n, 1), ALU.mult, ALU.add,
                         [rx_, vecs], [xa_t])
                    for k in range(1, 4):
                        P.ts("pool", ctmp[:, :Lb], rx_[:, k:k + Lb], wv(k), None, ALU.mult, None, [rx_, vecs], [ctmp])
                        P.tt("pool", xa_t[:, c0:c0 + Lb], xa_t[:, c0:c0 + Lb], ctmp[:, :Lb], ALU.add, [xa_t, ctmp], [xa_t])
                    P.copy("dve", xab_t[:, c0:c0 + Lb], xa_t[:, c0:c0 + Lb], [xa_t], [xab_t])

                def gate_block(n, d, base, s0, Lb, xa_t, xab_t, c0, want_y):
                    gi = cnt["g"] % 2
                    cnt["g"] += 1
                    ig_, a_, a2_ = igb[gi], ab[gi], a2b[gi]
                    for s_ in range(0, Lb, 512):
                        ns = min(512, Lb - s_)
                        ba = P.bank()
                        bx = P.bank()
                        P.mm(ba[:, :ns], lw[:, (0 * 2 + d) * 8 + n, :], xab_t[:, c0 + s_:c0 + s_ + ns], True, True, [lw, xab_t], [ba])
                        P.mm(bx[:, :ns], lw[:, (1 * 2 + d) * 8 + n, :], xab_t[:, c0 + s_:c0 + s_ + ns], True, True, [lw, xab_t], [bx])
                        P.act(rb[:, s_:s_ + ns], ba[:, :ns], AF.Sigmoid, [ba, vecs], [rb],
                              bias=VC("lru_b_a%d" % l, d * 8 + n, 1))
                        P.act(ig_[:, s_:s_ + ns], bx[:, :ns], AF.Sigmoid, [bx, vecs], [ig_],
                              bias=VC("lru_b_x%d" % l, d * 8 + n, 1))
                    P.act(a_[:, :Lb], rb[:, :Lb], AF.Exp, [rb, dv], [a_], scale=DV("lc%d" % l, d * 8 + n, 1))
                    P.act(a2_[:, :Lb], rb[:, :Lb], AF.Exp, [rb, dv], [a2_], scale=DV("lc2%d" % l, d * 8 + n, 1))
                    P.act(a2_[:, :Lb], a2_[:, :Lb], AF.Relu, [a2_], [a2_], scale=-1.0, bias=1.0)
                    P.act(a2_[:, :Lb], a2_[:, :Lb], AF.Sqrt, [a2_], [a2_])
                    P.tt("dve", ub[:, :Lb], a2_[:, :Lb], ig_[:, :Lb], ALU.mult, [a2_, ig_], [ub])
                    P.tt("dve", ub[:, :Lb], ub[:, :Lb], xa_t[:, c0:c0 + Lb], ALU.mult, [ub, xa_t], [ub])
                    h_ = hb[cnt["h"] % 2]
                    cnt["h"] += 1
                    if d == 0:
                        P.op("dve", lambda e, Lb=Lb, h_=h_, a_=a_: e.tensor_tensor_scan(
                            out=h_[:, 0:Lb], data0=a_[:, 0:Lb], data1=ub[:, 0:Lb], initial=carry[:, 0:1],
                            op0=ALU.mult, op1=ALU.add), [a_, ub, carry], [h_])
                        P.copy("dve", carry[:, :], h_[:, Lb - 1:Lb], [h_], [carry])
                    else:
                        rs = slice(Lb - 1, None, -1)
                        P.op("dve", lambda e, rs=rs, h_=h_, a_=a_: e.tensor_tensor_scan(
                            out=h_[:, rs], data0=a_[:, rs], data1=ub[:, rs], initial=carry[:, 0:1],
                            op0=ALU.mult, op1=ALU.add), [a_, ub, carry], [h_])
                        P.copy("dve", carry[:, :], h_[:, 0:1], [h_], [carry])
                    if want_y:
                        if d == 0:
                            P.dma("pool", hfb[n * 128:(n + 1) * 128, base + s0:base + s0 + Lb], h_[:, :Lb], h_,
                                  reads=[h_], writes=[d_hf])
                        else:
                            ci = cnt["c"] % 2
                            cnt["c"] += 1
                            hf_, rg_, ya_ = hfl[ci], rgl[ci], yab[ci]
                            P.dma("sp", hf_[:, :Lb], hfb[n * 128:(n + 1) * 128, base + s0:base + s0 + Lb], hf_,
                                  reads=[d_hf], writes=[hf_])
                            P.dma("sp", rg_[:, :Lb], zrv[:, 8 + n, base + s0:base + s0 + Lb], rg_,
                                  reads=[d_z], writes=[rg_])
                            P.act(rg_[:, :Lb], rg_[:, :Lb], AF.Gelu_apprx_tanh, [rg_], [rg_])
                            P.tt("dve", hf_[:, :Lb], hf_[:, :Lb], h_[:, :Lb], ALU.add, [hf_, h_], [hf_])
                            P.tt("dve", ya_[:, :Lb], hf_[:, :Lb], rg_[:, :Lb], ALU.mult, [hf_, rg_], [ya_])
                            P.dma("pool", ybv[:, n, base + s0:base + s0 + Lb], ya_[:, :Lb], ya_,
                                  reads=[ya_], writes=[d_y])

                def sconv_block(n, base, length, s0, Lb):
                    halo_load(scgl, 3, n, base, length, s0, Lb, 1, 1)
                    halo_load(sxl, 4, n, base, length, s0, Lb, 1, 1)
                    P.dma("sp", sbl[:, :Lb], zrv[:, 16 + n, base + s0:base + s0 + Lb], sbl, reads=[d_z], writes=[sbl])
                    P.tt("pool", pb_[:, :Lb + 2], scgl[:, :Lb + 2], sxl[:, :Lb + 2], ALU.mult, [scgl, sxl], [pb_])
                    wv = lambda k: VC("sconv_w%d" % l, k * 8 + n, 1)
                    P.ts("pool", ycv[:, :Lb], pb_[:, 0:Lb], wv(0), None, ALU.mult, None, [pb_, vecs], [ycv])
                    for k in (1, 2):
                        P.ts("pool", yct[:, :Lb], pb_[:, k:k + Lb], wv(k), None, ALU.mult, None, [pb_, vecs], [yct])
                        P.tt("pool", ycv[:, :Lb], ycv[:, :Lb], yct[:, :Lb], ALU.add, [ycv, yct], [ycv])
                    yo = ybo[cnt["y"] % 2]
                    cnt["y"] += 1
                    P.tt("pool", yo[:, :Lb], ycv[:, :Lb], sbl[:, :Lb], ALU.mult, [ycv, sbl], [yo])
                    P.dma("pool", ybv[:, 8 + n, base + s0:base + s0 + Lb], yo[:, :Lb], yo, reads=[yo], writes=[d_y])

                def rglru_seq(n, base, length, h0, want_y, save_final, xa_t, xab_t, do_sconv):
                    nb = (length + SB - 1) // SB
                    blocks = [(bi * SB, min(SB, length - bi * SB)) for bi in range(nb)]
                    for d in range(2):
                        P.copy("dve", carry[:, :], h0[d], [st, zero1], [carry])
                        order = blocks if d == 0 else list(reversed(blocks))
                        for (s0, Lb) in order:
                            if d == 0:
                                conv_block(n, base, length, s0, Lb, xa_t, xab_t, s0)
                            gate_block(n, d, base, s0, Lb, xa_t, xab_t, s0, want_y)
                            if d == 1 and do_sconv:
                                sconv_block(n, base, length, s0, Lb)
                        if save_final:
                            P.copy("dve", st[:, n, d:d + 1], carry[:, :], [carry], [st])

                for n in range(8):
                    rglru_seq(n, T, NCTX, [zero1[:, 0:1], zero1[:, 0:1]], want_ctx, True, xac, xacb, want_ctx)
                    rglru_seq(n, 0, T, [st[:, n, 0:1], st[:, n, 1:2]], True, False, xaf, xabf, True)

            with P.phase():
                esk = P.sbuf("esink", [128, 8, 128], F32)
                P.act(DV("tmp", 0, 8), VC("sink%d" % l), AF.Exp, [vecs], [dv])
                for h in range(8):
                    P.act(esk[:, h, :], ones_f[:, :], AF.Identity, [dv, ones_f], [esk], scale=DV("tmp", h, 1))
                kctx = P.sbuf("kctx", [128, 2, NCTX], BF16, dma=True)
                vctx = P.sbuf("vctx", [128, 2, 256], BF16, dma=True)
                P.dma("sp", kctx[:, :, :], zqv[:, 8:10, T:T + NCTX], kctx, reads=[d_z], writes=[kctx])
                P.dma("sp", vctx[:, :, :], zv[T:T + NCTX, :].rearrange("(b p) f -> p b f", p=128), vctx,
                      reads=[d_z], writes=[vctx])
                qg = [P.sbuf("qg%d" % i, [128, 8, 512], BF16, dma=True) for i in range(2)]
                kg = [P.sbuf("kg%d" % i, [128, 2, 768], BF16, dma=True) for i in range(2)]
                vg = [P.sbuf("vg%d" % i, [128, 6, 256], BF16, dma=True) for i in range(2)]
                og = [P.sbuf("og%d" % i, [128, 8, 512], BF16, dma=True) for i in range(2)]
                pts = [P.sbuf("pt%d" % i, [128, 512], BF16) for i in range(3)]
                dsm = [P.sbuf("dsm%d" % i, [128, 512], F32) for i in range(2)]
                pti = [0]
                dsi = [0]

                def attend(qt, qcol, chunks, kv, ogt, ocol):
                    bo = P.bank()
                    bd = P.bank()
                    rhs_q = qt[:, kv * 4:(kv + 1) * 4, qcol:qcol + 128]
                    sbanks = []

                    def score(ci):
                        kT, v_, m_, bufs = chunks[ci]
                        b_ = P.bank()
                        P.mm(b_[:, :], kT, rhs_q, True, m_ is None, [qt] + bufs, [b_])
                        if m_ is not None:
                            P.mm(b_[:, :], ident_ap, m_, False, True, [cbf], [b_])
                        sbanks.append(b_)
                    nchunk = len(chunks)
                    score(0)
                    if nchunk > 1:
                        score(1)
                    for ci in range(nchunk):
                        kT, v_, m_, bufs = chunks[ci]
                        pt = pts[pti[0] % 3]
                        pti[0] += 1
                        P.act(pt[:, :], sbanks[ci][:, :], AF.Exp, [sbanks[ci]], [pt], scale=ATT_SCALE)
                        P.mm(bo[:, :], v_, pt[:, :], ci == 0, ci == nchunk - 1, [pt] + bufs, [bo])
                        P.mm(bd[:, :], ones_b[:, :], pt[:, :], ci == 0, ci == nchunk - 1, [pt, ones_b], [bd])
                        if ci + 2 < nchunk:
                            score(ci + 2)
                    ds_ = dsm[dsi[0] % 2]
                    dsi[0] += 1
                    P.tt("dve", ds_[:, :], bd[:, :], esk[:, kv * 4:(kv + 1) * 4, :], ALU.add, [bd, esk], [ds_])
                    P.op("dve", lambda e: e.reciprocal(out=ds_[:, :], in_=ds_[:, :]), [ds_], [ds_])
                    P.tt("dve", ogt[:, kv * 4:(kv + 1) * 4, ocol:ocol + 128],
                         bo[:, :].rearrange("p (h q) -> p h q", h=4),
                         ds_[:, :].rearrange("p (h q) -> p h q", h=4), ALU.mult, [bo, ds_], [ogt])

                NBLK = T // 128
                for gq in range(T // 512):
                    g0 = gq * 512
                    qt, kt, vt, ot = qg[gq % 2], kg[gq % 2], vg[gq % 2], og[gq % 2]
                    P.dma("sp", qt[:, :, :], zqv[:, 0:8, g0:g0 + 512], qt, reads=[d_z], writes=[qt])
                    lo = max(0, g0 - 128)
                    hi = min(T, g0 + 640)
                    P.dma("sp", kt[:, :, lo - (g0 - 128):hi - (g0 - 128)], zqv[:, 8:10, lo:hi], kt, reads=[d_z], writes=[kt])
                    P.dma("sp", vt[:, (lo - (g0 - 128)) // 128:(hi - (g0 - 128)) // 128, :],
                          zv[lo:hi, :].rearrange("(b p) f -> p b f", p=128), vt, reads=[d_z], writes=[vt])
                    for i in range(4):
                        nblk = gq * 4 + i
                        for kv in range(2):
                            chunks = []
                            for rel in (-1, 0, 1):
                                if 0 <= nblk + rel < NBLK:
                                    kb = i + 1 + rel
                                    m_ = None if rel == 0 else (maskp_ap if rel < 0 else maskn_ap)
                                    chunks.append((kt[:, kv, kb * 128:(kb + 1) * 128], vt[:, kb, kv * 128:(kv + 1) * 128],
                                                   m_, [kt, vt]))
                            for cc in range(2):
                                chunks.append((kctx[:, kv, cc * 128:(cc + 1) * 128], vctx[:, cc, kv * 128:(kv + 1) * 128],
                                               None, [kctx, vctx]))
                            attend(qt, i * 128, chunks, kv, ot, i * 128)
                    P.dma("pool", ybv[:, 16:24, g0:g0 + 512], ot[:, :, :], ot, reads=[ot], writes=[d_y])
                if want_ctx:
                    qt, ot = qg[0], og[0]
                    P.dma("sp", qt[:, :, 0:NCTX], zqv[:, 0:8, T:T + NCTX], qt, reads=[d_z], writes=[qt])
                    for i in range(2):
                        for kv in range(2):
                            chunks = []
                            for cc in range(2):
                                chunks.append((kctx[:, kv, cc * 128:(cc + 1) * 128], vctx[:, cc, kv * 128:(kv + 1) * 128],
                                               None, [kctx, vctx]))
                            attend(qt, i * 128, chunks, kv, ot, i * 128)
                    P.dma("pool", ybv[:, 16:24, T:T + NCTX], ot[:, :, 0:NCTX], ot, reads=[ot], writes=[d_y])

        final_ops = []

        def phaseC(l, last):
            with P.phase():
                C = alloc_tile_bufs()
                htile, ubuf, big = C.htile, C.ubuf, C.big
                gts = [P.sbuf("gts%d" % i, [128, 3, TT], BF16, dma=True) for i in range(2)]
                gti = 0
                ytb = P.sbuf("ytile", [128, 24, TT], BF16, dma=True)
                ytile = ytb
                wbr = [P.sbuf("wbr%d" % i, [128, 8 * 256], BF16, dma=True) for i in range(4)]
                wbi = 0
                merged = big[:, 0:16 * TT].rearrange("p (c t) -> p c t", c=16)
                for (t0, n, s) in tiles(with_ctx=not last):
                    P.dma("sp", ytile[:, :, :n], ybv[:, :, t0:t0 + n], ytb, reads=[d_y], writes=[ytb])
                    for q in range(8):
                        bks = []
                        for i in range(3):
                            slot = wbr[wbi % 4]
                            wbi += 1
                            sv = slot[:, 0:8 * 256].rearrange("p (k n) -> p k n", k=8)
                            P.dma("sp", slot[:, 0:8 * 256], pwbr[l][i][q].rearrange("p k n -> p (k n)"), slot,
                                  reads=[d_w], writes=[slot])
                            pair = []
                            for half in range(2):
                                bk = P.bank()
                                for kc in range(8):
                                    P.mm(bk[:, :n], sv[:, kc, half * 128:(half + 1) * 128], ytile[:, i * 8 + kc, :n],
                                         kc == 0, kc == 7, [slot, ytb], [bk])
                                pair.append(bk)
                            bks.append(pair)
                        for half in range(2):
                            c = 2 * q + half
                            gt = gts[gti % 2]
                            gti += 1
                            P.dma("sp", gt[:, :, :n], zg4[:, :, c, t0:t0 + n], gt, reads=[d_z], writes=[gt])
                            ts3 = [tmp(C) for _ in range(3)]
                            for i in range(3):
                                P.tt("dve", ts3[i][:, :n], bks[i][half][:, :n], gt[:, i, :n], ALU.mult,
                                     [bks[i][half], gt], [ts3[i]])
                            P.tt("dve", ts3[0][:, :n], ts3[0][:, :n], ts3[1][:, :n], ALU.add, [ts3[0], ts3[1]], [ts3[0]])
                            P.tt("dve", merged[:, c, :n], ts3[0][:, :n], ts3[2][:, :n], ALU.add, [ts3[0], ts3[2]], [big])
                    G2 = DV("G2%d%d" % (l, s))
                    P.dma("sp", htile[:, :, :n], hbv[:, :, t0:t0 + n], htile, reads=[dh((l, "a", t0))], writes=[htile])
                    for q in range(8):
                        slot = next8(C)
                        sv = slot[:, :].rearrange("p (k n) -> p k n", k=DC)
                        P.dma("sp", slot[:, :], pwout[l][q].rearrange("p k n -> p (k n)"), slot, reads=[d_w], writes=[slot])
                        for half in range(2):
                            c = 2 * q + half
                            bk = P.bank()
                            for kc in range(DC):
                                P.mm(bk[:, :n], sv[:, kc, half * 128:(half + 1) * 128], merged[:, kc, :n],
                                     kc == 0, kc == DC - 1, [slot, big], [bk])
                            P.stt("dve", htile[:, c, :n], bk[:, :n], G2[:, c:c + 1], htile[:, c, :n], ALU.mult, ALU.add,
                                  [bk, htile, dv], [htile])
                    rmsnorm_mod(C, n, DV("A3%d%d" % (l, s)), DV("B3%d%d" % (l, s)))
                    ffn(C, 1, l, n, DV("G3%d%d" % (l, s)))
                    if last:
                        rms_stats(C, n)
                        for c in range(DC):
                            P.stt("dve", htile[:, c, :n], htile[:, c, :n], VC("fnorm_g", c, 1), C.rstd[:, :n],
                                  ALU.mult, ALU.mult, [htile, C.rstd, vecs], [htile])
                        o = P.dma("pool", outv[:, :, t0:t0 + n], htile[:, :, :n], htile, reads=[htile], writes=[d_out])
                        final_ops.append(o)
                    else:
                        P.dma("pool", hbv[:, :, t0:t0 + n], htile[:, :, :n], htile, reads=[htile],
                              writes=[dh((l + 1, "c", t0))])

        import os
        kstop = int(os.environ.get("KSTOP", "99"))
        step = 2
        for l in range(NLAYER):
            last = (l == NLAYER - 1)
            for ph in (phaseA, phaseB, phaseC):
                if step <= kstop:
                    if ph is phaseA:
                        ph(l)
                    elif ph is phaseB:
                        ph(l, not last)
                    else:
                        ph(l, last)
                step += 1
        P.emit(final_ops)
    return nc


def host_consts(T):
    t = np.arange(T)
    row = (t // 64).astype(np.float32)
    col = (t % 64).astype(np.float32)
    inv = (np.float32(10000.0) ** (-np.arange(0, 64, 2, dtype=np.float32) / np.float32(64))).astype(np.float32)
    ang_r = (row[None, :] * inv[:, None]).astype(np.float32)
    ang_c = (col[None, :] * inv[:, None]).astype(np.float32)
    cos_full = np.concatenate([np.cos(ang_r), np.cos(ang_r), np.cos(ang_c), np.cos(ang_c)], 0).astype(np.float32)
    sin_sgn = np.concatenate([-np.sin(ang_r), np.sin(ang_r), -np.sin(ang_c), np.sin(ang_c)], 0).astype(np.float32)
    rope = np.stack([cos_full, sin_sgn], 0)
    cb = np.zeros((128, CBFN), np.float32)
    for m in range(128):
        partner = m + 32 if (m % 64) < 32 else m - 32
        cb[partner, m] = 1.0
    cb[:, 128:256] = np.eye(128, dtype=np.float32)
    j = np.arange(128)[:, None]
    i = np.arange(128)[None, :]
    mp = np.where(i <= j, 0.0, -30000.0).astype(np.float32)
    mn = np.where(j <= i, 0.0, -30000.0).astype(np.float32)
    cb[:, 256:768] = np.tile(mp, (1, 4))
    cb[:, 768:1280] = np.tile(mn, (1, 4))
    return rope, cb.astype(ml_dtypes.bfloat16)


_NC_CACHE = {}


def kernel(x, c, ctx, c_ctx, ada_w, ada_b, norm_g, ffn1_w13, ffn1_w2, w_in, b_merge,
           rnn_conv_w, rnn_conv_b, lru_w_a, lru_b_a, lru_w_x, lru_b_x, lru_lambda,
           sc_conv_w, attn_sink, w_branch, w_out, ffn2_w13, ffn2_w2, final_norm_g):
    f32 = lambda a: np.ascontiguousarray(np.asarray(a), dtype=np.float32)
    x = f32(x)
    B, T, _ = x.shape
    ctx = f32(ctx)
    L = make_layout()
    rope, cb = host_consts(T)
    lru_w_a = f32(lru_w_a)
    lru_w_x = f32(lru_w_x)
    lw = np.stack([lru_w_a, lru_w_x], 1)
    lw = np.ascontiguousarray(lw.transpose(0, 4, 1, 2, 3, 5).reshape(NLAYER, 128, 32, 128))
    shared = {
        "ada_w": f32(ada_w), "ffn1_w13": f32(ffn1_w13), "ffn2_w13": f32(ffn2_w13), "ffn1_w2": f32(ffn1_w2),
        "ffn2_w2": f32(ffn2_w2), "w_in": f32(w_in), "w_branch": f32(w_branch), "w_out": f32(w_out),
        "lru_w": lw, "rope": rope, "cbf": cb,
    }
    base = np.zeros((128, L.n), np.float32)

    def put(name, arr):
        sl = L.sl(name)
        base[:, sl] = arr

    for l in range(NLAYER):
        put("ada_b%d" % l, fm(f32(ada_b)[l]))
        put("norm_g%d" % l, fm(f32(norm_g)[l].reshape(-1)))
        put("b_merge%d" % l, fm(f32(b_merge)[l].reshape(-1)))
        put("rconv_w%d" % l, fm(f32(rnn_conv_w)[l].reshape(-1)))
        put("rconv_b%d" % l, fm(f32(rnn_conv_b)[l]))
        put("lru_b_a%d" % l, fm(f32(lru_b_a)[l].reshape(-1)))
        put("lru_b_x%d" % l, fm(f32(lru_b_x)[l].reshape(-1)))
        put("lam%d" % l, fm(f32(lru_lambda)[l].reshape(-1)))
        put("sconv_w%d" % l, fm(f32(sc_conv_w)[l].reshape(-1)))
        put("sink%d" % l, np.tile(f32(attn_sink)[l][None, :], (128, 1)))
    put("fnorm_g", fm(f32(final_norm_g)))
    in_maps = []
    for b in range(B):
        v = base.copy()
        cv = np.stack([fm(f32(c)[b]), fm(f32(c_ctx))], -1).reshape(128, 32)
        v[:, L.sl("cvec")] = cv
        xt = np.ascontiguousarray(np.concatenate([x[b].T, ctx[b].T], axis=1))
        m = dict(shared)
        m["xT"] = xt
        m["vecs"] = v
        in_maps.append(m)
    if T not in _NC_CACHE:
        _NC_CACHE[T] = build(T)
    nc = _NC_CACHE[T]
    res = run_bass_kernel_spmd(nc, in_maps, core_ids=list(range(B)))
    out = np.stack([np.ascontiguousarray(r["outT"].T) for r in res.results], 0)
    return out.astype(np.float32)
```

```python
import numpy as np
import ml_dtypes
from contextlib import ExitStack
import concourse.bass as bass
import concourse.mybir as mybir
from concourse.bass_utils import run_bass_kernel_spmd

F32 = mybir.dt.float32
BF16 = mybir.dt.bfloat16
AF = mybir.ActivationFunctionType
ALU = mybir.AluOpType

D = 2048
DC = 16
DFF = 5504
FC = 43
NCTX = 256
NMOD = 9
INCOLS = 12800
NLAYER = 2
EPS = 1e-6
TT = 512
SB = 1024
ATT_SCALE = 128 ** -0.5
CBFN = 1280


class Buf:
    __slots__ = ("name", "t", "last_w", "readers", "sem")

    def __init__(self, name, t=None):
        self.name = name
        self.t = t
        self.last_w = None
        self.readers = []
        self.sem = None

    def __getitem__(self, idx):
        return self.t[idx]


class DmaSem:
    def __init__(self, sem):
        self.sem = sem
        self.count = 0
        self.last_op = None


class Op:
    __slots__ = ("eng", "fn", "deps", "needed", "cnt", "dsem", "dcnt")

    def __init__(self, eng, fn, deps):
        self.eng = eng
        self.fn = fn
        self.deps = deps
        self.needed = False
        self.cnt = 0
        self.dsem = None
        self.dcnt = 0


class Prog:
    ENGS = ("pe", "act", "dve", "pool", "sp")

    def __init__(self, nc, es):
        self.nc = nc
        self.es = es
        self.ops = {e: [] for e in self.ENGS}
        self.last = {e: None for e in self.ENGS}
        self.fence = {e: [] for e in self.ENGS}
        self.dsems = []
        self.banks = []
        self.bank_i = 0
        self.esem = {e: es.enter_context(nc.semaphore("s_" + e)) for e in ("pe", "act", "dve", "pool")}
        self.root = es
        self.pool_free = {}
        self.phase_bufs = None
        self.nsem = 0

    def sbuf(self, name, shape, dt, dma=False):
        self.nsb = getattr(self, "nsb", 0) + 1
        t = self.es.enter_context(self.nc.sbuf_tensor("sb%d_%s" % (self.nsb, name), shape, dt))
        b = Buf(name, t)
        if dma:
            self.add_dsem(b)
        return b

    def add_dsem(self, b):
        b.sem = {}
        if self.phase_bufs is not None:
            self.phase_bufs.append(b)

    def get_dsem(self, b, q):
        if q not in b.sem:
            free = self.pool_free.setdefault(q, [])
            if free:
                ds = free.pop()
            else:
                ds = DmaSem(self.root.enter_context(self.nc.semaphore("dsem%d" % self.nsem)))
                self.nsem += 1
                self.dsems.append(ds)
            b.sem[q] = ds
        return b.sem[q]

    def phase(self):
        prog = self

        class _Ph:
            def __enter__(self_):
                self_.st = ExitStack()
                self_.prev = prog.es
                self_.prev_bufs = prog.phase_bufs
                prog.es = self_.st
                prog.phase_bufs = []
                return self_

            def __exit__(self_, *a):
                prog.barrier()
                for b in prog.phase_bufs:
                    for q_, ds_ in b.sem.items():
                        prog.pool_free.setdefault(q_, []).append(ds_)
                prog.phase_bufs = self_.prev_bufs
                prog.es = self_.prev
                self_.st.close()
                return False
        return _Ph()

    def make_banks(self):
        for i in range(8):
            t = self.es.enter_context(self.nc.psum_tensor("bank%d" % i, [128, 512], F32))
            self.banks.append(Buf("bank%d" % i, t))

    def bank(self):
        b = self.banks[self.bank_i % 8]
        self.bank_i += 1
        return b

    def _deps(self, eng, reads, writes):
        deps = []
        for r in reads:
            if r.last_w is not None:
                deps.append(r.last_w)
        for w in writes:
            if w.last_w is not None:
                deps.append(w.last_w)
            deps.extend(w.readers)
        deps.extend(self.fence[eng])
        self.fence[eng] = []
        out = []
        seen = set()
        for d in deps:
            if id(d) in seen:
                continue
            seen.add(id(d))
            if d.eng == "pe" and eng == "pe" and d.dsem is None:
                continue
            out.append(d)
        return out

    def op(self, eng, fn, reads=(), writes=()):
        o = Op(eng, fn, self._deps(eng, reads, writes))
        for d in o.deps:
            d.needed = True
        self.ops[eng].append(o)
        self.last[eng] = o
        for r in reads:
            self._add_reader(r, o)
        for w in writes:
            w.last_w = o
            w.readers = []
        return o

    @staticmethod
    def _add_reader(buf, o):
        key = o.eng if o.dsem is None else id(o.dsem)
        rl = buf.readers
        for i, x in enumerate(rl):
            kx = x.eng if x.dsem is None else id(x.dsem)
            if kx == key:
                rl[i] = o
                return
        rl.append(o)

    def dma(self, q, out, in_, sbuf_side, reads=(), writes=()):
        ds = self.get_dsem(sbuf_side, q)
        o = Op(q, None, self._deps(q, reads, writes))
        if ds.last_op is not None and all(ds.last_op is not d for d in o.deps):
            o.deps.append(ds.last_op)
        for d in o.deps:
            d.needed = True
        ds.count += 1
        o.dsem = ds
        o.dcnt = ds.count
        ds.last_op = o
        o.fn = lambda e, out=out, in_=in_: e.dma_start(out=out, in_=in_)
        self.ops[q].append(o)
        self.last[q] = o
        for r in reads:
            self._add_reader(r, o)
        for w in writes:
            w.last_w = o
            w.readers = []
        return o

    def barrier(self):
        lasts = [self.last[e] for e in self.ENGS if self.last[e] is not None]
        for ds in self.dsems:
            if ds.last_op is not None:
                lasts.append(ds.last_op)
        for e in self.ENGS:
            self.fence[e] = list(lasts)

    def mm(self, out, lhsT, rhs, start, stop, reads, writes):
        return self.op("pe", lambda e: e.matmul(out, lhsT, rhs, start=start, stop=stop), reads, writes)

    def act(self, out, in_, func, reads, writes, bias=None, scale=None, eng="act"):
        kw = {}
        if bias is not None:
            kw["bias"] = bias
        if scale is not None:
            kw["scale"] = scale
        return self.op(eng, lambda e: e.activation(out=out, in_=in_, func=func, **kw), reads, writes)

    def tt(self, eng, out, in0, in1, op, reads, writes):
        return self.op(eng, lambda e: e.tensor_tensor(out=out, in0=in0, in1=in1, op=op), reads, writes)

    def ts(self, eng, out, in0, s1, s2, op0, op1, reads, writes):
        if s2 is None:
            return self.op(eng, lambda e: e.tensor_single_scalar(out=out, in_=in0, scalar=s1, op=op0), reads, writes)
        return self.op(eng, lambda e: e.tensor_scalar(out=out, in0=in0, scalar1=s1, scalar2=s2, op0=op0, op1=op1),
                       reads, writes)

    def stt(self, eng, out, in0, scalar, in1, op0, op1, reads, writes):
        return self.op(eng, lambda e: e.scalar_tensor_tensor(out=out, in0=in0, scalar=scalar, in1=in1, op0=op0, op1=op1),
                       reads, writes)

    def copy(self, eng, out, in_, reads, writes):
        if eng == "act":
            return self.op(eng, lambda e: e.activation(out=out, in_=in_, func=AF.Identity), reads, writes)
        return self.op(eng, lambda e: e.tensor_copy(out=out, in_=in_), reads, writes)

    def memset(self, eng, ap, val, writes):
        return self.op(eng, lambda e: e.memset(ap, val), (), writes)

    def emit(self, final_ops):
        nc = self.nc
        for e in ("pe", "act", "dve", "pool"):
            c = 0
            for o in self.ops[e]:
                if o.dsem is None and o.needed:
                    c += 1
                    o.cnt = c
                elif o.dsem is None:
                    o.cnt = -1
        for o in final_ops:
            assert o.dsem is not None
        handles = {"pe": None, "act": None, "dve": None, "pool": None, "sp": None}
        prog = self

        def run(eng_name, eng):
            waited = {}
            for o in prog.ops[eng_name]:
                for d in o.deps:
                    if d.dsem is not None:
                        key = id(d.dsem)
                        val = 16 * d.dcnt
                        sem = d.dsem.sem
                    else:
                        key = d.eng
                        val = d.cnt
                        sem = prog.esem[d.eng]
                        assert val > 0
                    if waited.get(key, 0) < val:
                        eng.wait_ge(sem, val)
                        waited[key] = val
                ins = o.fn(eng)
                if o.dsem is not None:
                    ins.then_inc(o.dsem.sem, 16)
                elif o.needed:
                    ins.then_inc(prog.esem[eng_name], 1)
            if eng_name == "pool":
                for o in final_ops:
                    eng.wait_ge(o.dsem.sem, 16 * o.dcnt)

        with nc.Block() as block:
            @block.tensor
            def _(e):
                run("pe", e)

            @block.scalar
            def _(e):
                run("act", e)

            @block.vector
            def _(e):
                run("dve", e)

            @block.gpsimd
            def _(e):
                run("pool", e)

            @block.sync
            def _(e):
                run("sp", e)


class VecLayout:
    def __init__(self):
        self.off = {}
        self.n = 0

    def add(self, name, ncols):
        self.off[name] = (self.n, ncols)
        self.n += ncols

    def sl(self, name, i=0, n=None):
        o, c = self.off[name]
        if n is None:
            n = c - i
        return slice(o + i, o + i + n)


def make_layout():
    L = VecLayout()
    L.add("cvec", 32)
    for l in range(NLAYER):
        L.add("ada_b%d" % l, 144)
        L.add("norm_g%d" % l, 48)
        L.add("b_merge%d" % l, 48)
        L.add("rconv_w%d" % l, 32)
        L.add("rconv_b%d" % l, 8)
        L.add("lru_b_a%d" % l, 16)
        L.add("lru_b_x%d" % l, 16)
        L.add("lam%d" % l, 16)
        L.add("sconv_w%d" % l, 24)
        L.add("sink%d" % l, 8)
    L.add("fnorm_g", 16)
    return L


def fm(v):
    v = np.asarray(v, np.float32)
    return np.ascontiguousarray(v.reshape(-1, 128).T)


def build(T):
    TTOT = T + NCTX
    nc = bass.Bass("TRN2", target_bir_lowering=False)
    L = make_layout()

    def din(name, shape, dt=F32):
        return nc.dram_tensor(name, list(shape), dt, kind="ExternalInput").ap()

    def dscr(name, shape, dt):
        return nc.dram_tensor(name, list(shape), dt, kind="Internal").ap()

    xT = din("xT", [D, TTOT])
    vecs_d = din("vecs", [128, L.n])
    ada_w = din("ada_w", [NLAYER, D, NMOD * D])
    w13_d = [din("ffn1_w13", [NLAYER, D, 2 * DFF]), din("ffn2_w13", [NLAYER, D, 2 * DFF])]
    w2_d = [din("ffn1_w2", [NLAYER, DFF, D]), din("ffn2_w2", [NLAYER, DFF, D])]
    w_in_d = din("w_in", [NLAYER, D, INCOLS])
    wbr_d = din("w_branch", [NLAYER, 3, 1024, D])
    wout_d = din("w_out", [NLAYER, D, D])
    lruw_d = din("lru_w", [NLAYER, 128, 32, 128])
    rope_d = din("rope", [2, 128, T])
    cbf_d = din("cbf", [128, CBFN], BF16)
    outT = nc.dram_tensor("outT", [D, T], F32, kind="ExternalOutput").ap()

    hbuf = dscr("hbuf", [D, TTOT], F32)
    zr = dscr("zr", [5 * 1024, TTOT], F32)
    zq = dscr("zq", [10 * 128, TTOT], BF16)
    zv = dscr("zv", [TTOT, 256], BF16)
    zg = dscr("zg", [3 * D, TTOT], BF16)
    hfb = dscr("hfb", [1024, TTOT], F32)
    ybuf = dscr("ybuf", [3 * 1024, TTOT], BF16)
    pw13 = [[dscr("pw13_%d_%d" % (f, l), [FC, 128, DC, 256], BF16) for l in range(NLAYER)] for f in range(2)]
    pw2 = [[dscr("pw2_%d_%d" % (f, l), [DC, 128, FC, 128], BF16) for l in range(NLAYER)] for f in range(2)]
    pwin = [dscr("pwin_%d" % l, [50, 128, DC, 256], BF16) for l in range(NLAYER)]
    pwbr = [[dscr("pwbr_%d_%d" % (l, i), [8, 128, 8, 256], BF16) for i in range(3)] for l in range(NLAYER)]
    pwout = [dscr("pwout_%d" % l, [8, 128, DC, 256], BF16) for l in range(NLAYER)]

    xTv = xT.rearrange("(c p) t -> p c t", p=128)
    hbv = hbuf.rearrange("(c p) t -> p c t", p=128)
    zrv = zr.rearrange("(c p) t -> p c t", p=128)
    zqv = zq.rearrange("(c p) t -> p c t", p=128)
    zgv = zg.rearrange("(c p) t -> p c t", p=128)
    zg4 = zg.rearrange("(i c p) t -> p i c t", i=3, p=128)
    ybv = ybuf.rearrange("(c p) t -> p c t", p=128)
    outv = outT.rearrange("(c p) t -> p c t", p=128)

    with ExitStack() as es:
        P = Prog(nc, es)
        P.make_banks()
        d_w = Buf("d_weights")
        d_in = Buf("d_inputs")
        d_h = {}
        d_z = Buf("d_z")
        d_y = Buf("d_y")
        d_hf = Buf("d_hf")
        d_out = Buf("d_out")

        def dh(key):
            if key not in d_h:
                d_h[key] = Buf("d_h%s" % (key,))
            return d_h[key]

        vecs = P.sbuf("vecs", [128, L.n], F32, dma=True)
        DVN = NLAYER * 2 * (144 + 9 * 16) + NLAYER * 32 + 64
        dv = P.sbuf("dv", [128, DVN], F32)
        cbf = P.sbuf("cbf", [128, CBFN], BF16, dma=True)
        ones_f = P.sbuf("ones_f", [128, 128], F32)
        ones_b = P.sbuf("ones_b", [128, 128], BF16)
        perm_ap = cbf[:, 0:128]
        ident_ap = cbf[:, 128:256]
        maskp_ap = cbf[:, 256:768]
        maskn_ap = cbf[:, 768:1280]

        dvo = {}
        dvn = [0]

        def dvadd(name, n):
            dvo[name] = (dvn[0], n)
            dvn[0] += n

        for l in range(NLAYER):
            for s in range(2):
                dvadd("mod%d%d" % (l, s), 144)
                for nm in ("A1", "B1", "G1", "A2", "B2", "G2", "A3", "B3", "G3"):
                    dvadd("%s%d%d" % (nm, l, s), 16)
            dvadd("lc%d" % l, 16)
            dvadd("lc2%d" % l, 16)
        dvadd("sc", 32)
        dvadd("tmp", 32)
        assert dvn[0] <= DVN

        def DV(name, i=0, n=None):
            o, c = dvo[name]
            if n is None:
                n = c - i
            return dv[:, o + i:o + i + n]

        def VC(name, i=0, n=None):
            return vecs[:, L.sl(name, i, n)]

        P.dma("sp", vecs[:, :], vecs_d, vecs, reads=[d_in], writes=[vecs])
        P.dma("sp", cbf[:, :], cbf_d, cbf, reads=[d_in], writes=[cbf])
        P.memset("dve", ones_f[:, :], 1.0, [ones_f])
        P.memset("dve", ones_b[:, :], 1.0, [ones_b])

        import os
        kstop0 = int(os.environ.get("KSTOP", "99"))
        with P.phase():
            apan = [P.sbuf("apan%d" % i, [128, DC, 512], F32, dma=True) for i in range(2)]
            P.act(DV("sc"), VC("cvec"), AF.Silu, [vecs], [dv])
            sc3 = DV("sc").rearrange("p (k s) -> p k s", s=2)
            pi = 0
            for l in range(NLAYER):
                bk = P.bank()
                bk3 = bk[:, 0:288].rearrange("p (j s) -> p j s", s=2)
                for pq in range(36):
                    pan = apan[pi % 2]
                    pi += 1
                    src = ada_w[l].rearrange("(k p) n -> p k n", p=128)[:, :, pq * 512:(pq + 1) * 512]
                    P.dma("sp", pan[:, :, :], src, pan, reads=[d_in], writes=[pan])
                    for jj in range(4):
                        j = pq * 4 + jj
                        for kc in range(DC):
                            P.mm(bk3[:, j, :], pan[:, kc, jj * 128:(jj + 1) * 128], sc3[:, kc, :],
                                 kc == 0, kc == DC - 1, [pan, dv], [bk])
                for s in range(2):
                    P.tt("dve", DV("mod%d%d" % (l, s)), bk3[:, :, s], VC("ada_b%d" % l), ALU.add, [bk, vecs], [dv])
                    md = lambda m, l=l, s=s: DV("mod%d%d" % (l, s), m * 16, 16)
                    ng = lambda i, l=l: VC("norm_g%d" % l, i * 16, 16)
                    for (nm, gi, sh, scl, gt, gmul) in (("1", 0, 0, 1, 2, 0.5), ("2", 1, 3, 4, 5, 1.0), ("3", 2, 6, 7, 8, 0.5)):
                        P.stt("dve", DV("A%s%d%d" % (nm, l, s)), md(scl), 1.0, ng(gi), ALU.add, ALU.mult, [dv, vecs], [dv])
                        P.copy("dve", DV("B%s%d%d" % (nm, l, s)), md(sh), [dv], [dv])
                        P.ts("dve", DV("G%s%d%d" % (nm, l, s)), md(gt), gmul, None, ALU.mult, None, [dv], [dv])
                P.act(DV("tmp", 0, 16), VC("lam%d" % l), AF.Exp, [vecs], [dv], scale=-1.0)
                P.act(DV("tmp", 16, 16), DV("tmp", 0, 16), AF.Ln, [dv], [dv], bias=1.0)
                P.ts("dve", DV("lc%d" % l), DV("tmp", 16, 16), -8.0, None, ALU.mult, None, [dv], [dv])
                P.ts("dve", DV("lc2%d" % l), DV("tmp", 16, 16), -16.0, None, ALU.mult, None, [dv], [dv])

        with P.phase():
            STG = 11008
            stg = [P.sbuf("stg%d" % i, [128, STG], F32, dma=True) for i in range(2)]
            pbf = [P.sbuf("pbf%d" % i, [128, STG], BF16, dma=True) for i in range(2)]
            cnt = [0]
            ceng = ["act", "dve", "pool"]
            cei = [0]

            def cast(out, in_, sbuf_in, sbuf_out):
                e = ceng[cei[0] % 3]
                cei[0] += 1
                P.copy(e, out, in_, [sbuf_in], [sbuf_out])

            def prep_generic(W, kcs, npan, dest, G):
                Wv = W.rearrange("(k p) n -> p k n", p=128)
                for q0 in range(0, npan, G):
                    g = min(G, npan - q0)
                    s_ = stg[cnt[0] % 2]
                    b_ = pbf[cnt[0] % 2]
                    cnt[0] += 1
                    sv = s_[:, 0:kcs * g * 256].rearrange("p (k n) -> p k n", k=kcs)
                    P.dma("sp", sv, Wv[:, :, q0 * 256:(q0 + g) * 256], s_, reads=[d_in], writes=[s_])
                    bv = b_[:, 0:g * kcs * 256].rearrange("p (q k n) -> p q k n", q=g, k=kcs)
                    for q in range(g):
                        cast(bv[:, q, :, :], sv[:, :, q * 256:(q + 1) * 256], s_, b_)
                    P.dma("pool", dest[q0:q0 + g].rearrange("q p k n -> p q (k n)"),
                          b_[:, 0:g * kcs * 256].rearrange("p (q m) -> p q m", q=g), b_, reads=[b_], writes=[d_w])

            def prep_w13(W, dest):
                Wv = W.rearrange("(k p) n -> p k n", p=128)
                G = 2
                for j0 in range(0, FC, G):
                    g = min(G, FC - j0)
                    s_ = stg[cnt[0] % 2]
                    b_ = pbf[cnt[0] % 2]
                    cnt[0] += 1
                    sv = s_[:, 0:DC * 2 * g * 128].rearrange("p (k s n) -> p k s n", k=DC, s=2)
                    P.dma("sp", sv[:, :, 0, :], Wv[:, :, j0 * 128:(j0 + g) * 128], s_, reads=[d_in], writes=[s_])
                    P.dma("sp", sv[:, :, 1, :], Wv[:, :, DFF + j0 * 128:DFF + (j0 + g) * 128], s_, reads=[d_in], writes=[s_])
                    bv = b_[:, 0:g * DC * 256].rearrange("p (q k s n) -> p q k s n", q=g, k=DC, s=2)
                    for q in range(g):
                        for s in range(2):
                            cast(bv[:, q, :, s, :], sv[:, :, s, q * 128:(q + 1) * 128], s_, b_)
                    P.dma("pool", dest[j0:j0 + g].rearrange("q p k n -> p q (k n)"),
                          b_[:, 0:g * DC * 256].rearrange("p (q m) -> p q m", q=g), b_, reads=[b_], writes=[d_w])

            def prep_w2(W, dest):
                Wv = W.rearrange("(k p) n -> p k n", p=128)
                for c0 in range(0, DC, 2):
                    s_ = stg[cnt[0] % 2]
                    b_ = pbf[cnt[0] % 2]
                    cnt[0] += 1
                    sv = s_[:, 0:FC * 256].rearrange("p (k n) -> p k n", k=FC)
                    P.dma("sp", sv, Wv[:, :, c0 * 128:(c0 + 2) * 128], s_, reads=[d_in], writes=[s_])
                    bv = b_[:, 0:2 * FC * 128].rearrange("p (q k n) -> p q k n", q=2, k=FC)
                    for q in range(2):
                        cast(bv[:, q, :, :], sv[:, :, q * 128:(q + 1) * 128], s_, b_)
                    P.dma("pool", dest[c0:c0 + 2].rearrange("q p k n -> p q (k n)"),
                          b_[:, 0:2 * FC * 128].rearrange("p (q m) -> p q m", q=2), b_, reads=[b_], writes=[d_w])

            for l in range(NLAYER if kstop0 >= 1 else 0):
                for f in range(2):
                    prep_w13(w13_d[f][l], pw13[f][l])
                    prep_w2(w2_d[f][l], pw2[f][l])
                prep_generic(w_in_d[l], DC, 50, pwin[l], 2)
                for i in range(3):
                    prep_generic(wbr_d[l, i], 8, 8, pwbr[l][i], 4)
                prep_generic(wout_d[l], DC, 8, pwout[l], 2)

        def tiles(with_ctx=True):
            ts_ = [(t0, TT, 0) for t0 in range(0, T, TT)]
            if with_ctx:
                ts_.append((T, NCTX, 1))
            return ts_

        class TileCtx:
            pass

        def alloc_tile_bufs():
            C = TileCtx()
            C.htile = P.sbuf("htile", [128, DC, TT], F32, dma=True)
            C.ubuf = P.sbuf("ubuf", [128, DC, TT], BF16)
            C.big = P.sbuf("big", [128, FC * TT], BF16, dma=True)
            C.ring8 = [P.sbuf("ring8_%d" % i, [128, DC * 256], BF16, dma=True) for i in range(3)]
            C.ring11 = [P.sbuf("ring11_%d" % i, [128, FC * 128], BF16, dma=True) for i in range(2)]
            C.r8i = 0
            C.r11i = 0
            C.tmpf = [P.sbuf("tmpf%d" % i, [128, TT], F32) for i in range(4)]
            C.tfi = 0
            C.rstd = P.sbuf("rstd", [128, TT], F32)
            C.tmpb = [P.sbuf("tmpb%d" % i, [128, TT], BF16) for i in range(3)]
            C.tbi = 0
            return C

        def tmp(C):
            b = C.tmpf[C.tfi % 4]
            C.tfi += 1
            return b

        def next8(C):
            b = C.ring8[C.r8i % 3]
            C.r8i += 1
            return b

        def next11(C):
            b = C.ring11[C.r11i % 2]
            C.r11i += 1
            return b

        def rms_stats(C, n):
            htile = C.htile
            bk = P.bank()
            for c in range(DC):
                sq = C.tmpb[C.tbi % 3]
                C.tbi += 1
                P.act(sq[:, :n], htile[:, c, :n], AF.Square, [htile], [sq])
                P.mm(bk[:, :n], ones_b[:, :], sq[:, :n], c == 0, c == DC - 1, [ones_b, sq], [bk])
            P.act(C.rstd[:, :n], bk[:, :n], AF.Sqrt, [bk], [C.rstd], scale=1.0 / D, bias=EPS)
            P.op("dve", lambda e: e.reciprocal(out=C.rstd[:, :n], in_=C.rstd[:, :n]), [C.rstd], [C.rstd])

        def rmsnorm_mod(C, n, A, B):
            rms_stats(C, n)
            for c in range(DC):
                t_ = tmp(C)
                P.stt("dve", t_[:, :n], C.htile[:, c, :n], A[:, c:c + 1], C.rstd[:, :n], ALU.mult, ALU.mult,
                      [C.htile, C.rstd, dv, vecs], [t_])
                P.act(C.ubuf[:, c, :n], t_[:, :n], AF.Identity, [t_, dv], [C.ubuf], bias=B[:, c:c + 1])

        def ffn(C, f, l, n, G):
            htile, ubuf, big = C.htile, C.ubuf, C.big
            actv = big[:, 0:FC * TT].rearrange("p (j t) -> p j t", j=FC)
            for j in range(FC):
                slot = next8(C)
                sv = slot[:, :].rearrange("p (k n) -> p k n", k=DC)
                P.dma("sp", slot[:, :], pw13[f][l][j].rearrange("p k n -> p (k n)"), slot, reads=[d_w], writes=[slot])
                bg = P.bank()
                bu = P.bank()
                for kc in range(DC):
                    P.mm(bg[:, :n], sv[:, kc, 0:128], ubuf[:, kc, :n], kc == 0, kc == DC - 1, [slot, ubuf], [bg])
                for kc in range(DC):
                    P.mm(bu[:, :n], sv[:, kc, 128:256], ubuf[:, kc, :n], kc == 0, kc == DC - 1, [slot, ubuf], [bu])
                sg = tmp(C)
                P.act(sg[:, :n], bg[:, :n], AF.Silu, [bg], [sg])
                P.tt("dve", actv[:, j, :n], sg[:, :n], bu[:, :n], ALU.mult, [sg, bu], [big])
            for c in range(DC):
                slot = next11(C)
                sv = slot[:, :].rearrange("p (k n) -> p k n", k=FC)
                P.dma("sp", slot[:, :], pw2[f][l][c].rearrange("p k n -> p (k n)"), slot, reads=[d_w], writes=[slot])
                bk = P.bank()
                for j in range(FC):
                    P.mm(bk[:, :n], sv[:, j, :], actv[:, j, :n], j == 0, j == FC - 1, [slot, big], [bk])
                P.stt("dve", htile[:, c, :n], bk[:, :n], G[:, c:c + 1], htile[:, c, :n], ALU.mult, ALU.add,
                      [bk, htile, dv], [htile])

        def phaseA(l):
            with P.phase():
                C = alloc_tile_bufs()
                htile, ubuf = C.htile, C.ubuf
                ropets = [P.sbuf("ropet%d" % i, [128, 2, TT], F32, dma=True) for i in range(2)]
                qbf = [P.sbuf("qbf%d" % i, [128, TT], BF16) for i in range(2)]
                stf = [P.sbuf("stf%d" % i, [128, 2, TT], F32, dma=True) for i in range(3)]
                stb = [P.sbuf("stb%d" % i, [128, 2, TT], BF16, dma=True) for i in range(3)]
                stgv = P.sbuf("stgv", [128, 4, 256], BF16, dma=True)
                qbi = 0
                sfi = 0
                sbi = 0
                tl = tiles()
                src = xTv if l == 0 else hbv

                def load_h(ti):
                    (t0_, n_, s_) = tl[ti]
                    P.dma("sp", htile[:, :, :n_], src[:, :, t0_:t0_ + n_], htile,
                          reads=[d_in if l == 0 else dh((l, "c", t0_))], writes=[htile])
                load_h(0)
                for ti, (t0, n, s) in enumerate(tl):
                    ropet = ropets[ti % 2]
                    if not s:
                        P.dma("sp", ropet[:, :, :n], rope_d.rearrange("s p t -> p s t")[:, :, t0:t0 + n], ropet,
                              reads=[d_in], writes=[ropet])
                    rmsnorm_mod(C, n, DV("A1%d%d" % (l, s)), DV("B1%d%d" % (l, s)))
                    ksub = int(os.environ.get("KSUB", "99"))
                    if ksub >= 2:
                        ffn(C, 0, l, n, DV("G1%d%d" % (l, s)))
                    P.dma(os.environ.get("KSTQ", "pool"), hbv[:, :, t0:t0 + n], htile[:, :, :n], htile, reads=[htile], writes=[dh((l, "a", t0))])
                    if ksub < 3:
                        if ti + 1 < len(tl):
                            load_h(ti + 1)
                        continue
                    rmsnorm_mod(C, n, DV("A2%d%d" % (l, s)), DV("B2%d%d" % (l, s)))
                    if ti + 1 < len(tl):
                        load_h(ti + 1)
                    for q in range({3: 20, 4: 25, 5: 26}.get(ksub, 50)):
                        slot = next8(C)
                        sv = slot[:, :].rearrange("p (k n) -> p k n", k=DC)
                        P.dma("sp", slot[:, :], pwin[l][q].rearrange("p k n -> p (k n)"), slot, reads=[d_w], writes=[slot])
                        if q == 25:
                            for sb_ in range(n // 128):
                                bk = P.bank()
                                for kc in range(DC):
                                    P.mm(bk[:, 0:256], ubuf[:, kc, sb_ * 128:(sb_ + 1) * 128], sv[:, kc, :],
                                         kc == 0, kc == DC - 1, [slot, ubuf], [bk])
                                P.copy("act", stgv[:, sb_, :], bk[:, 0:256], [bk], [stgv])
                            P.dma("pool", zv[t0:t0 + n, :].rearrange("(b p) f -> p b f", p=128), stgv[:, 0:n // 128, :],
                                  stgv, reads=[stgv], writes=[d_z])
                            continue
                        if q < 20:
                            st_ = stf[sfi % 3]
                            sfi += 1
                        else:
                            st_ = stb[sbi % 3]
                            sbi += 1
                        for half in range(2):
                            ch = 2 * q + half
                            bk = P.bank()
                            for kc in range(DC):
                                P.mm(bk[:, :n], sv[:, kc, half * 128:(half + 1) * 128], ubuf[:, kc, :n],
                                     kc == 0, kc == DC - 1, [slot, ubuf], [bk])
                            if ch < 40:
                                P.copy("dve" if half == 0 else "act", st_[:, half, :n], bk[:, :n], [bk], [st_])
                            elif ch < 50:
                                if s:
                                    P.copy("act", st_[:, half, :n], bk[:, :n], [bk], [st_])
                                else:
                                    qb = qbf[qbi % 2]
                                    qbi += 1
                                    P.copy("act", qb[:, :n], bk[:, :n], [bk], [qb, bk])
                                    bk2 = P.bank()
                                    P.mm(bk2[:, :n], perm_ap, qb[:, :n], True, True, [cbf, qb], [bk2])
                                    t1 = tmp(C)
                                    t2 = tmp(C)
                                    P.tt("dve", t1[:, :n], bk[:, :n], ropet[:, 0, :n], ALU.mult, [bk, ropet], [t1])
                                    P.tt("dve", t2[:, :n], bk2[:, :n], ropet[:, 1, :n], ALU.mult, [bk2, ropet], [t2])
                                    P.tt("dve", st_[:, half, :n], t1[:, :n], t2[:, :n], ALU.add, [t1, t2], [st_])
                            else:
                                gi = ch - 52
                                P.act(st_[:, half, :n], bk[:, :n], AF.Sigmoid, [bk, vecs], [st_],
                                      bias=VC("b_merge%d" % l, gi, 1))
                        c0 = 2 * q
                        if c0 < 40:
                            P.dma("pool", zrv[:, c0:c0 + 2, t0:t0 + n], st_[:, :, :n], st_, reads=[st_], writes=[d_z])
                        elif c0 < 50:
                            P.dma("pool", zqv[:, c0 - 40:c0 - 38, t0:t0 + n], st_[:, :, :n], st_, reads=[st_], writes=[d_z])
                        else:
                            P.dma("pool", zgv[:, c0 - 52:c0 - 50, t0:t0 + n], st_[:, :, :n], st_, reads=[st_], writes=[d_z])

        def phaseB(l, want_ctx):
            with P.phase():
                lw = P.sbuf("lw", [128, 32, 128], BF16, dma=True)
                P.dma("pool", lw[:, :, :], lruw_d[l], lw, reads=[d_in], writes=[lw])
                st = P.sbuf("st", [128, 8, 2], F32)
                zero1 = P.sbuf("zero1", [128, 1], F32)
                P.memset("dve", zero1[:, :], 0.0, [zero1])
                carry = P.sbuf("carry", [128, 1], F32)
                xaf = P.sbuf("xaf", [128, T], F32)
                xabf = P.sbuf("xabf", [128, T], BF16)
                xac = P.sbuf("xac", [128, NCTX], F32)
                xacb = P.sbuf("xacb", [128, NCTX], BF16)
                rxh = [P.sbuf("rxh%d" % i, [128, SB + 3], F32, dma=True) for i in range(2)]
                rb = P.sbuf("rb", [128, SB], F32)
                igb = [P.sbuf("igb%d" % i, [128, SB], F32) for i in range(2)]
                ab = [P.sbuf("ab%d" % i, [128, SB], F32) for i in range(2)]
                a2b = [P.sbuf("a2b%d" % i, [128, SB], F32) for i in range(2)]
                ub = P.sbuf("ub", [128, SB], F32)
                hb = [P.sbuf("hb%d" % i, [128, SB], F32, dma=True) for i in range(2)]
                hfl = [P.sbuf("hfl%d" % i, [128, SB], F32, dma=True) for i in range(2)]
                rgl = [P.sbuf("rgl%d" % i, [128, SB], F32, dma=True) for i in range(2)]
                yab = [P.sbuf("yab%d" % i, [128, SB], BF16, dma=True) for i in range(2)]
                scgl = [P.sbuf("scgl%d" % i, [128, SB + 2], F32, dma=True) for i in range(2)]
                sxl = [P.sbuf("sxl%d" % i, [128, SB + 2], F32, dma=True) for i in range(2)]
                sbl = [P.sbuf("sbl%d" % i, [128, SB], F32, dma=True) for i in range(2)]
                pbs = [P.sbuf("pbs%d" % i, [128, SB + 2], F32) for i in range(2)]
                yts = [[P.sbuf("yts%d_%d" % (i, k), [128, SB], F32) for k in range(3)] for i in range(2)]
                ybo = [P.sbuf("ybo%d" % i, [128, SB], BF16, dma=True) for i in range(2)]
                cnt = {"rx": 0, "g": 0, "h": 0, "c": 0, "y": 0}

                def halo_load(buf, grp, n, base, length, s0, Lb, left, right):
                    lo = max(0, s0 - left)
                    hi = min(length, s0 + Lb + right)
                    o0 = s0 - left
                    if lo > o0:
                        P.memset("dve", buf[:, 0:lo - o0], 0.0, [buf])
                    if hi < s0 + Lb + right:
                        P.memset("dve", buf[:, hi - o0:Lb + left + right], 0.0, [buf])
                    row = grp * 8 + n
                    P.dma("sp", buf[:, lo - o0:hi - o0], zrv[:, row, base + lo:base + hi], buf, reads=[d_z], writes=[buf])

                def conv_block(n, base, length, s0, Lb, xa_t, xab_t, c0):
                    rx_ = rxh[cnt["rx"] % 2]
                    cnt["rx"] += 1
                    halo_load(rx_, 0, n, base, length, s0, Lb, 2, 1)
                    wv = lambda k: VC("rconv_w%d" % l, k * 8 + n, 1)
                    P.ts("dve", xa_t[:, c0:c0 + Lb], rx_[:, 0:Lb], wv(0), VC("rconv_b%d" % l, n, 1), ALU.mult, ALU.add,
                         [rx_, vecs], [xa_t])
                    for k in range(1, 4):
                        P.stt("dve", xa_t[:, c0:c0 + Lb], rx_[:, k:k + Lb], wv(k), xa_t[:, c0:c0 + Lb], ALU.mult, ALU.add,
                              [rx_, xa_t, vecs], [xa_t])
                    P.copy("dve", xab_t[:, c0:c0 + Lb], xa_t[:, c0:c0 + Lb], [xa_t], [xab_t])

                def gate_acts(n, d, Lb, xab_t, c0):
                    gi = cnt["g"] % 2
                    cnt["g"] += 1
                    ig_, a_, a2_ = igb[gi], ab[gi], a2b[gi]
                    for s_ in range(0, Lb, 512):
                        ns = min(512, Lb - s_)
                        ba = P.bank()
                        bx = P.bank()
                        P.mm(ba[:, :ns], lw[:, (0 * 2 + d) * 8 + n, :], xab_t[:, c0 + s_:c0 + s_ + ns], True, True, [lw, xab_t], [ba])
                        P.mm(bx[:, :ns], lw[:, (1 * 2 + d) * 8 + n, :], xab_t[:, c0 + s_:c0 + s_ + ns], True, True, [lw, xab_t], [bx])
                        P.act(rb[:, s_:s_ + ns], ba[:, :ns], AF.Sigmoid, [ba, vecs], [rb],
                              bias=VC("lru_b_a%d" % l, d * 8 + n, 1))
                        P.act(ig_[:, s_:s_ + ns], bx[:, :ns], AF.Sigmoid, [bx, vecs], [ig_],
                              bias=VC("lru_b_x%d" % l, d * 8 + n, 1))
                    P.act(a_[:, :Lb], rb[:, :Lb], AF.Exp, [rb, dv], [a_], scale=DV("lc%d" % l, d * 8 + n, 1))
                    P.act(a2_[:, :Lb], rb[:, :Lb], AF.Exp, [rb, dv], [a2_], scale=DV("lc2%d" % l, d * 8 + n, 1))
                    P.act(a2_[:, :Lb], a2_[:, :Lb], AF.Relu, [a2_], [a2_], scale=-1.0, bias=1.0)
                    P.act(a2_[:, :Lb], a2_[:, :Lb], AF.Sqrt, [a2_], [a2_])
                    return (ig_, a_, a2_)

                def gate_scan(n, d, base, s0, Lb, xa_t, c0, want_y, gbufs):
                    ig_, a_, a2_ = gbufs
                    P.tt("dve", ub[:, :Lb], a2_[:, :Lb], ig_[:, :Lb], ALU.mult, [a2_, ig_], [ub])
                    P.tt("dve", ub[:, :Lb], ub[:, :Lb], xa_t[:, c0:c0 + Lb], ALU.mult, [ub, xa_t], [ub])
                    h_ = hb[cnt["h"] % 2]
                    cnt["h"] += 1
                    if d == 0:
                        P.op("dve", lambda e, Lb=Lb, h_=h_, a_=a_: e.tensor_tensor_scan(
                            out=h_[:, 0:Lb], data0=a_[:, 0:Lb], data1=ub[:, 0:Lb], initial=carry[:, 0:1],
                            op0=ALU.mult, op1=ALU.add), [a_, ub, carry], [h_])
                        P.copy("dve", carry[:, :], h_[:, Lb - 1:Lb], [h_], [carry])
                    else:
                        rs = slice(Lb - 1, None, -1)
                        P.op("dve", lambda e, rs=rs, h_=h_, a_=a_: e.tensor_tensor_scan(
                            out=h_[:, rs], data0=a_[:, rs], data1=ub[:, rs], initial=carry[:, 0:1],
                            op0=ALU.mult, op1=ALU.add), [a_, ub, carry], [h_])
                        P.copy("dve", carry[:, :], h_[:, 0:1], [h_], [carry])
                    if want_y:
                        if d == 0:
                            P.dma("pool", hfb[n * 128:(n + 1) * 128, base + s0:base + s0 + Lb], h_[:, :Lb], h_,
                                  reads=[h_], writes=[d_hf])
                        else:
                            ci = cnt["c"] % 2
                            cnt["c"] += 1
                            hf_, rg_, ya_ = hfl[ci], rgl[ci], yab[ci]
                            P.dma("sp", hf_[:, :Lb], hfb[n * 128:(n + 1) * 128, base + s0:base + s0 + Lb], hf_,
                                  reads=[d_hf], writes=[hf_])
                            P.dma("sp", rg_[:, :Lb], zrv[:, 8 + n, base + s0:base + s0 + Lb], rg_,
                                  reads=[d_z], writes=[rg_])
                            P.act(rg_[:, :Lb], rg_[:, :Lb], AF.Gelu_apprx_tanh, [rg_], [rg_])
                            P.tt("dve", hf_[:, :Lb], hf_[:, :Lb], h_[:, :Lb], ALU.add, [hf_, h_], [hf_])
                            P.tt("dve", ya_[:, :Lb], hf_[:, :Lb], rg_[:, :Lb], ALU.mult, [hf_, rg_], [ya_])
                            P.dma("pool", ybv[:, n, base + s0:base + s0 + Lb], ya_[:, :Lb], ya_,
                                  reads=[ya_], writes=[d_y])

                def sconv_p(n, base, length, s0, Lb):
                    si = cnt["y"] % 2
                    cnt["y"] += 1
                    halo_load(scgl[si], 3, n, base, length, s0, Lb, 1, 1)
                    halo_load(sxl[si], 4, n, base, length, s0, Lb, 1, 1)
                    P.dma("sp", sbl[si][:, :Lb], zrv[:, 16 + n, base + s0:base + s0 + Lb], sbl[si], reads=[d_z], writes=[sbl[si]])
                    P.tt("dve", pbs[si][:, :Lb + 2], scgl[si][:, :Lb + 2], sxl[si][:, :Lb + 2], ALU.mult,
                         [scgl[si], sxl[si]], [pbs[si]])
                    return si

                def sconv_taps(n, Lb, si):
                    wv = lambda k: VC("sconv_w%d" % l, k * 8 + n, 1)
                    for k in range(3):
                        P.act(yts[si][k][:, :Lb], pbs[si][:, k:k + Lb], AF.Identity, [pbs[si], vecs], [yts[si][k]], scale=wv(k))

                def sconv_fin(n, base, s0, Lb, si):
                    y0, y1, y2 = yts[si]
                    P.tt("dve", y0[:, :Lb], y0[:, :Lb], y1[:, :Lb], ALU.add, [y0, y1], [y0])
                    P.tt("dve", y0[:, :Lb], y0[:, :Lb], y2[:, :Lb], ALU.add, [y0, y2], [y0])
                    yo = ybo[si]
                    P.tt("dve", yo[:, :Lb], y0[:, :Lb], sbl[si][:, :Lb], ALU.mult, [y0, sbl[si]], [yo])
                    P.dma("pool", ybv[:, 8 + n, base + s0:base + s0 + Lb], yo[:, :Lb], yo, reads=[yo], writes=[d_y])

                def rglru_seq(n, base, length, h0, want_y, save_final, xa_t, xab_t, do_sconv):
                    nb = (length + SB - 1) // SB
                    blocks = [(bi * SB, min(SB, length - bi * SB)) for bi in range(nb)]
                    for d in range(2):
                        P.copy("dve", carry[:, :], h0[d], [st, zero1], [carry])
                        order = blocks if d == 0 else list(reversed(blocks))
                        if d == 0:
                            conv_block(n, base, length, order[0][0], order[0][1], xa_t, xab_t, order[0][0])
                        for i, (s0, Lb) in enumerate(order):
                            si = None
                            if d == 1 and do_sconv:
                                si = sconv_p(n, base, length, s0, Lb)
                            gb = gate_acts(n, d, Lb, xab_t, s0)
                            if si is not None:
                                sconv_taps(n, Lb, si)
                            if d == 0 and i + 1 < len(order):
                                conv_block(n, base, length, order[i + 1][0], order[i + 1][1], xa_t, xab_t, order[i + 1][0])
                            gate_scan(n, d, base, s0, Lb, xa_t, s0, want_y, gb)
                            if si is not None:
                                sconv_fin(n, base, s0, Lb, si)
                        if save_final:
                            P.copy("dve", st[:, n, d:d + 1], carry[:, :], [carry], [st])

                for n in range(8):
                    rglru_seq(n, T, NCTX, [zero1[:, 0:1], zero1[:, 0:1]], want_ctx, True, xac, xacb, want_ctx)
                    rglru_seq(n, 0, T, [st[:, n, 0:1], st[:, n, 1:2]], True, False, xaf, xabf, True)

            with P.phase():
                esk = P.sbuf("esink", [128, 8, 128], F32)
                P.act(DV("tmp", 0, 8), VC("sink%d" % l), AF.Exp, [vecs], [dv])
                for h in range(8):
                    P.act(esk[:, h, :], ones_f[:, :], AF.Identity, [dv, ones_f], [esk], scale=DV("tmp", h, 1))
                kctx = P.sbuf("kctx", [128, 2, NCTX], BF16, dma=True)
                vctx = P.sbuf("vctx", [128, 2, 256], BF16, dma=True)
                P.dma("sp", kctx[:, :, :], zqv[:, 8:10, T:T + NCTX], kctx, reads=[d_z], writes=[kctx])
                P.dma("sp", vctx[:, :, :], zv[T:T + NCTX, :].rearrange("(b p) f -> p b f", p=128), vctx,
                      reads=[d_z], writes=[vctx])
                qg = [P.sbuf("qg%d" % i, [128, 8, 512], BF16, dma=True) for i in range(2)]
                kg = [P.sbuf("kg%d" % i, [128, 2, 768], BF16, dma=True) for i in range(2)]
                vg = [P.sbuf("vg%d" % i, [128, 6, 256], BF16, dma=True) for i in range(2)]
                og = [P.sbuf("og%d" % i, [128, 8, 512], BF16, dma=True) for i in range(2)]
                pts = [P.sbuf("pt%d" % i, [128, 512], BF16) for i in range(3)]
                dsm = [P.sbuf("dsm%d" % i, [128, 512], F32) for i in range(2)]
                pti = [0]
                dsi = [0]

                def attend(qt, qcol, chunks, kv, ogt, ocol):
                    bo = P.bank()
                    bd = P.bank()
                    rhs_q = qt[:, kv * 4:(kv + 1) * 4, qcol:qcol + 128]
                    sbanks = []

                    def score(ci):
                        kT, v_, m_, bufs = chunks[ci]
                        b_ = P.bank()
                        P.mm(b_[:, :], kT, rhs_q, True, m_ is None, [qt] + bufs, [b_])
                        if m_ is not None:
                            P.mm(b_[:, :], ident_ap, m_, False, True, [cbf], [b_])
                        sbanks.append(b_)
                    nchunk = len(chunks)
                    score(0)
                    if nchunk > 1:
                        score(1)
                    for ci in range(nchunk):
                        kT, v_, m_, bufs = chunks[ci]
                        pt = pts[pti[0] % 3]
                        pti[0] += 1
                        P.act(pt[:, :], sbanks[ci][:, :], AF.Exp, [sbanks[ci]], [pt], scale=ATT_SCALE)
                        P.mm(bo[:, :], v_, pt[:, :], ci == 0, ci == nchunk - 1, [pt] + bufs, [bo])
                        P.mm(bd[:, :], ones_b[:, :], pt[:, :], ci == 0, ci == nchunk - 1, [pt, ones_b], [bd])
                        if ci + 2 < nchunk:
                            score(ci + 2)
                    ds_ = dsm[dsi[0] % 2]
                    dsi[0] += 1
                    P.tt("dve", ds_[:, :], bd[:, :], esk[:, kv * 4:(kv + 1) * 4, :], ALU.add, [bd, esk], [ds_])
                    P.op("dve", lambda e: e.reciprocal(out=ds_[:, :], in_=ds_[:, :]), [ds_], [ds_])
                    P.tt("dve", ogt[:, kv * 4:(kv + 1) * 4, ocol:ocol + 128],
                         bo[:, :].rearrange("p (h q) -> p h q", h=4),
                         ds_[:, :].rearrange("p (h q) -> p h q", h=4), ALU.mult, [bo, ds_], [ogt])

                NBLK = T // 128
                for gq in range(T // 512):
                    g0 = gq * 512
                    qt, kt, vt, ot = qg[gq % 2], kg[gq % 2], vg[gq % 2], og[gq % 2]
                    P.dma("sp", qt[:, :, :], zqv[:, 0:8, g0:g0 + 512], qt, reads=[d_z], writes=[qt])
                    lo = max(0, g0 - 128)
                    hi = min(T, g0 + 640)
                    P.dma("sp", kt[:, :, lo - (g0 - 128):hi - (g0 - 128)], zqv[:, 8:10, lo:hi], kt, reads=[d_z], writes=[kt])
                    P.dma("sp", vt[:, (lo - (g0 - 128)) // 128:(hi - (g0 - 128)) // 128, :],
                          zv[lo:hi, :].rearrange("(b p) f -> p b f", p=128), vt, reads=[d_z], writes=[vt])
                    for i in range(4):
                        nblk = gq * 4 + i
                        for kv in range(2):
                            chunks = []
                            for rel in (-1, 0, 1):
                                if 0 <= nblk + rel < NBLK:
                                    kb = i + 1 + rel
                                    m_ = None if rel == 0 else (maskp_ap if rel < 0 else maskn_ap)
                                    chunks.append((kt[:, kv, kb * 128:(kb + 1) * 128], vt[:, kb, kv * 128:(kv + 1) * 128],
                                                   m_, [kt, vt]))
                            for cc in range(2):
                                chunks.append((kctx[:, kv, cc * 128:(cc + 1) * 128], vctx[:, cc, kv * 128:(kv + 1) * 128],
                                               None, [kctx, vctx]))
                            attend(qt, i * 128, chunks, kv, ot, i * 128)
                    P.dma("pool", ybv[:, 16:24, g0:g0 + 512], ot[:, :, :], ot, reads=[ot], writes=[d_y])
                if want_ctx:
                    qt, ot = qg[0], og[0]
                    P.dma("sp", qt[:, :, 0:NCTX], zqv[:, 0:8, T:T + NCTX], qt, reads=[d_z], writes=[qt])
                    for i in range(2):
                        for kv in range(2):
                            chunks = []
                            for cc in range(2):
                                chunks.append((kctx[:, kv, cc * 128:(cc + 1) * 128], vctx[:, cc, kv * 128:(kv + 1) * 128],
                                               None, [kctx, vctx]))
                            attend(qt, i * 128, chunks, kv, ot, i * 128)
                    P.dma("pool", ybv[:, 16:24, T:T + NCTX], ot[:, :, 0:NCTX], ot, reads=[ot], writes=[d_y])

        final_ops = []

        def phaseC(l, last):
            with P.phase():
                C = alloc_tile_bufs()
                htile, ubuf, big = C.htile, C.ubuf, C.big
                gts = [P.sbuf("gts%d" % i, [128, 3, TT], BF16, dma=True) for i in range(2)]
                gti = 0
                ytb = P.sbuf("ytile", [128, 24, TT], BF16, dma=True)
                ytile = ytb
                wbr = [P.sbuf("wbr%d" % i, [128, 8 * 256], BF16, dma=True) for i in range(4)]
                wbi = 0
                merged = big[:, 0:16 * TT].rearrange("p (c t) -> p c t", c=16)
                for (t0, n, s) in tiles(with_ctx=not last):
                    P.dma("sp", ytile[:, :, :n], ybv[:, :, t0:t0 + n], ytb, reads=[d_y], writes=[ytb])
                    for q in range(8):
                        bks = []
                        for i in range(3):
                            slot = wbr[wbi % 4]
                            wbi += 1
                            sv = slot[:, 0:8 * 256].rearrange("p (k n) -> p k n", k=8)
                            P.dma("sp", slot[:, 0:8 * 256], pwbr[l][i][q].rearrange("p k n -> p (k n)"), slot,
                                  reads=[d_w], writes=[slot])
                            pair = []
                            for half in range(2):
                                bk = P.bank()
                                for kc in range(8):
                                    P.mm(bk[:, :n], sv[:, kc, half * 128:(half + 1) * 128], ytile[:, i * 8 + kc, :n],
                                         kc == 0, kc == 7, [slot, ytb], [bk])
                                pair.append(bk)
                            bks.append(pair)
                        for half in range(2):
                            c = 2 * q + half
                            gt = gts[gti % 2]
                            gti += 1
                            P.dma("sp", gt[:, :, :n], zg4[:, :, c, t0:t0 + n], gt, reads=[d_z], writes=[gt])
                            ts3 = [tmp(C) for _ in range(3)]
                            for i in range(3):
                                P.tt("dve", ts3[i][:, :n], bks[i][half][:, :n], gt[:, i, :n], ALU.mult,
                                     [bks[i][half], gt], [ts3[i]])
                            P.tt("dve", ts3[0][:, :n], ts3[0][:, :n], ts3[1][:, :n], ALU.add, [ts3[0], ts3[1]], [ts3[0]])
                            P.tt("dve", merged[:, c, :n], ts3[0][:, :n], ts3[2][:, :n], ALU.add, [ts3[0], ts3[2]], [big])
                    G2 = DV("G2%d%d" % (l, s))
                    P.dma("sp", htile[:, :, :n], hbv[:, :, t0:t0 + n], htile, reads=[dh((l, "a", t0))], writes=[htile])
                    for q in range(8):
                        slot = next8(C)
                        sv = slot[:, :].rearrange("p (k n) -> p k n", k=DC)
                        P.dma("sp", slot[:, :], pwout[l][q].rearrange("p k n -> p (k n)"), slot, reads=[d_w], writes=[slot])
                        for half in range(2):
                            c = 2 * q + half
                            bk = P.bank()
                            for kc in range(DC):
                                P.mm(bk[:, :n], sv[:, kc, half * 128:(half + 1) * 128], merged[:, kc, :n],
                                     kc == 0, kc == DC - 1, [slot, big], [bk])
                            P.stt("dve", htile[:, c, :n], bk[:, :n], G2[:, c:c + 1], htile[:, c, :n], ALU.mult, ALU.add,
                                  [bk, htile, dv], [htile])
                    rmsnorm_mod(C, n, DV("A3%d%d" % (l, s)), DV("B3%d%d" % (l, s)))
                    ffn(C, 1, l, n, DV("G3%d%d" % (l, s)))
                    if last:
                        rms_stats(C, n)
                        for c in range(DC):
                            P.stt("dve", htile[:, c, :n], htile[:, c, :n], VC("fnorm_g", c, 1), C.rstd[:, :n],
                                  ALU.mult, ALU.mult, [htile, C.rstd, vecs], [htile])
                        o = P.dma("pool", outv[:, :, t0:t0 + n], htile[:, :, :n], htile, reads=[htile], writes=[d_out])
                        final_ops.append(o)
                    else:
                        P.dma("pool", hbv[:, :, t0:t0 + n], htile[:, :, :n], htile, reads=[htile],
                              writes=[dh((l + 1, "c", t0))])

        import os
        kstop = int(os.environ.get("KSTOP", "99"))
        step = 2
        for l in range(NLAYER):
            last = (l == NLAYER - 1)
            for ph in (phaseA, phaseB, phaseC):
                if step <= kstop:
                    if ph is phaseA:
                        ph(l)
                    elif ph is phaseB:
                        ph(l, not last)
                    else:
                        ph(l, last)
                step += 1
        P.emit(final_ops)
    return nc


def host_consts(T):
    t = np.arange(T)
    row = (t // 64).astype(np.float32)
    col = (t % 64).astype(np.float32)
    inv = (np.float32(10000.0) ** (-np.arange(0, 64, 2, dtype=np.float32) / np.float32(64))).astype(np.float32)
    ang_r = (row[None, :] * inv[:, None]).astype(np.float32)
    ang_c = (col[None, :] * inv[:, None]).astype(np.float32)
    cos_full = np.concatenate([np.cos(ang_r), np.cos(ang_r), np.cos(ang_c), np.cos(ang_c)], 0).astype(np.float32)
    sin_sgn = np.concatenate([-np.sin(ang_r), np.sin(ang_r), -np.sin(ang_c), np.sin(ang_c)], 0).astype(np.float32)
    rope = np.stack([cos_full, sin_sgn], 0)
    cb = np.zeros((128, CBFN), np.float32)
    for m in range(128):
        partner = m + 32 if (m % 64) < 32 else m - 32
        cb[partner, m] = 1.0
    cb[:, 128:256] = np.eye(128, dtype=np.float32)
    j = np.arange(128)[:, None]
    i = np.arange(128)[None, :]
    mp = np.where(i <= j, 0.0, -30000.0).astype(np.float32)
    mn = np.where(j <= i, 0.0, -30000.0).astype(np.float32)
    cb[:, 256:768] = np.tile(mp, (1, 4))
    cb[:, 768:1280] = np.tile(mn, (1, 4))
    return rope, cb.astype(ml_dtypes.bfloat16)


_NC_CACHE = {}


def kernel(x, c, ctx, c_ctx, ada_w, ada_b, norm_g, ffn1_w13, ffn1_w2, w_in, b_merge,
           rnn_conv_w, rnn_conv_b, lru_w_a, lru_b_a, lru_w_x, lru_b_x, lru_lambda,
           sc_conv_w, attn_sink, w_branch, w_out, ffn2_w13, ffn2_w2, final_norm_g):
    f32 = lambda a: np.ascontiguousarray(np.asarray(a), dtype=np.float32)
    x = f32(x)
    B, T, _ = x.shape
    ctx = f32(ctx)
    L = make_layout()
    rope, cb = host_consts(T)
    lru_w_a = f32(lru_w_a)
    lru_w_x = f32(lru_w_x)
    lw = np.stack([lru_w_a, lru_w_x], 1)
    lw = np.ascontiguousarray(lw.transpose(0, 4, 1, 2, 3, 5).reshape(NLAYER, 128, 32, 128))
    shared = {
        "ada_w": f32(ada_w), "ffn1_w13": f32(ffn1_w13), "ffn2_w13": f32(ffn2_w13), "ffn1_w2": f32(ffn1_w2),
        "ffn2_w2": f32(ffn2_w2), "w_in": f32(w_in), "w_branch": f32(w_branch), "w_out": f32(w_out),
        "lru_w": lw, "rope": rope, "cbf": cb,
    }
    base = np.zeros((128, L.n), np.float32)

    def put(name, arr):
        sl = L.sl(name)
        base[:, sl] = arr

    for l in range(NLAYER):
        put("ada_b%d" % l, fm(f32(ada_b)[l]))
        put("norm_g%d" % l, fm(f32(norm_g)[l].reshape(-1)))
        put("b_merge%d" % l, fm(f32(b_merge)[l].reshape(-1)))
        put("rconv_w%d" % l, fm(f32(rnn_conv_w)[l].reshape(-1)))
        put("rconv_b%d" % l, fm(f32(rnn_conv_b)[l]))
        put("lru_b_a%d" % l, fm(f32(lru_b_a)[l].reshape(-1)))
        put("lru_b_x%d" % l, fm(f32(lru_b_x)[l].reshape(-1)))
        put("lam%d" % l, fm(f32(lru_lambda)[l].reshape(-1)))
        put("sconv_w%d" % l, fm(f32(sc_conv_w)[l].reshape(-1)))
        put("sink%d" % l, np.tile(f32(attn_sink)[l][None, :], (128, 1)))
    put("fnorm_g", fm(f32(final_norm_g)))
    in_maps = []
    for b in range(B):
        v = base.copy()
        cv = np.stack([fm(f32(c)[b]), fm(f32(c_ctx))], -1).reshape(128, 32)
        v[:, L.sl("cvec")] = cv
        xt = np.ascontiguousarray(np.concatenate([x[b].T, ctx[b].T], axis=1))
        m = dict(shared)
        m["xT"] = xt
        m["vecs"] = v
        in_maps.append(m)
    if T not in _NC_CACHE:
        _NC_CACHE[T] = build(T)
    nc = _NC_CACHE[T]
    res = run_bass_kernel_spmd(nc, in_maps, core_ids=list(range(B)))
    out = np.stack([np.ascontiguousarray(r["outT"].T) for r in res.results], 0)
    return out.astype(np.float32)
```

```python
import numpy as np
import ml_dtypes
from contextlib import ExitStack
import concourse.bass as bass
import concourse.mybir as mybir
from concourse.bass_utils import run_bass_kernel_spmd

F32 = mybir.dt.float32
BF16 = mybir.dt.bfloat16
AF = mybir.ActivationFunctionType
ALU = mybir.AluOpType

D = 2048
DC = 16
DFF = 5504
FC = 43
NCTX = 256
NMOD = 9
INCOLS = 12800
NLAYER = 2
EPS = 1e-6
TT = 512
SB = 1024
ATT_SCALE = 128 ** -0.5
CBFN = 1280


class Buf:
    __slots__ = ("name", "t", "last_w", "readers", "sem")

    def __init__(self, name, t=None):
        self.name = name
        self.t = t
        self.last_w = None
        self.readers = []
        self.sem = None

    def __getitem__(self, idx):
        return self.t[idx]


class DmaSem:
    def __init__(self, sem):
        self.sem = sem
        self.count = 0
        self.last_op = None


class Op:
    __slots__ = ("eng", "fn", "deps", "needed", "cnt", "dsem", "dcnt")

    def __init__(self, eng, fn, deps):
        self.eng = eng
        self.fn = fn
        self.deps = deps
        self.needed = False
        self.cnt = 0
        self.dsem = None
        self.dcnt = 0


class Prog:
    ENGS = ("pe", "act", "dve", "pool", "sp")

    def __init__(self, nc, es):
        self.nc = nc
        self.es = es
        self.ops = {e: [] for e in self.ENGS}
        self.last = {e: None for e in self.ENGS}
        self.fence = {e: [] for e in self.ENGS}
        self.dsems = []
        self.banks = []
        self.bank_i = 0
        self.esem = {e: es.enter_context(nc.semaphore("s_" + e)) for e in ("pe", "act", "dve", "pool")}
        self.root = es
        self.pool_free = {}
        self.phase_bufs = None
        self.nsem = 0

    def sbuf(self, name, shape, dt, dma=False):
        self.nsb = getattr(self, "nsb", 0) + 1
        t = self.es.enter_context(self.nc.sbuf_tensor("sb%d_%s" % (self.nsb, name), shape, dt))
        b = Buf(name, t)
        if dma:
            self.add_dsem(b)
        return b

    def add_dsem(self, b):
        b.sem = {}
        if self.phase_bufs is not None:
            self.phase_bufs.append(b)

    def get_dsem(self, b, q):
        if q not in b.sem:
            free = self.pool_free.setdefault(q, [])
            if free:
                ds = free.pop()
            else:
                ds = DmaSem(self.root.enter_context(self.nc.semaphore("dsem%d" % self.nsem)))
                self.nsem += 1
                self.dsems.append(ds)
            b.sem[q] = ds
        return b.sem[q]

    def phase(self):
        prog = self

        class _Ph:
            def __enter__(self_):
                self_.st = ExitStack()
                self_.prev = prog.es
                self_.prev_bufs = prog.phase_bufs
                prog.es = self_.st
                prog.phase_bufs = []
                return self_

            def __exit__(self_, *a):
                prog.barrier()
                for b in prog.phase_bufs:
                    for q_, ds_ in b.sem.items():
                        prog.pool_free.setdefault(q_, []).append(ds_)
                prog.phase_bufs = self_.prev_bufs
                prog.es = self_.prev
                self_.st.close()
                return False
        return _Ph()

    def make_banks(self):
        for i in range(8):
            t = self.es.enter_context(self.nc.psum_tensor("bank%d" % i, [128, 512], F32))
            self.banks.append(Buf("bank%d" % i, t))

    def bank(self):
        b = self.banks[self.bank_i % 8]
        self.bank_i += 1
        return b

    def _deps(self, eng, reads, writes):
        deps = []
        for r in reads:
            if r.last_w is not None:
                deps.append(r.last_w)
        for w in writes:
            if w.last_w is not None:
                deps.append(w.last_w)
            deps.extend(w.readers)
        deps.extend(self.fence[eng])
        self.fence[eng] = []
        out = []
        seen = set()
        for d in deps:
            if id(d) in seen:
                continue
            seen.add(id(d))
            if d.eng == "pe" and eng == "pe" and d.dsem is None:
                continue
            out.append(d)
        return out

    def op(self, eng, fn, reads=(), writes=()):
        o = Op(eng, fn, self._deps(eng, reads, writes))
        for d in o.deps:
            d.needed = True
        self.ops[eng].append(o)
        self.last[eng] = o
        for r in reads:
            self._add_reader(r, o)
        for w in writes:
            w.last_w = o
            w.readers = []
        return o

    @staticmethod
    def _add_reader(buf, o):
        key = o.eng if o.dsem is None else id(o.dsem)
        rl = buf.readers
        for i, x in enumerate(rl):
            kx = x.eng if x.dsem is None else id(x.dsem)
            if kx == key:
                rl[i] = o
                return
        rl.append(o)

    def dma(self, q, out, in_, sbuf_side, reads=(), writes=()):
        ds = self.get_dsem(sbuf_side, q)
        o = Op(q, None, self._deps(q, reads, writes))
        if ds.last_op is not None and all(ds.last_op is not d for d in o.deps):
            o.deps.append(ds.last_op)
        for d in o.deps:
            d.needed = True
        ds.count += 1
        o.dsem = ds
        o.dcnt = ds.count
        ds.last_op = o
        o.fn = lambda e, out=out, in_=in_: e.dma_start(out=out, in_=in_)
        self.ops[q].append(o)
        self.last[q] = o
        for r in reads:
            self._add_reader(r, o)
        for w in writes:
            w.last_w = o
            w.readers = []
        return o

    def barrier(self):
        lasts = [self.last[e] for e in self.ENGS if self.last[e] is not None]
        for ds in self.dsems:
            if ds.last_op is not None:
                lasts.append(ds.last_op)
        for e in self.ENGS:
            self.fence[e] = list(lasts)

    def mm(self, out, lhsT, rhs, start, stop, reads, writes):
        return self.op("pe", lambda e: e.matmul(out, lhsT, rhs, start=start, stop=stop), reads, writes)

    def act(self, out, in_, func, reads, writes, bias=None, scale=None, eng="act"):
        kw = {}
        if bias is not None:
            kw["bias"] = bias
        if scale is not None:
            kw["scale"] = scale
        return self.op(eng, lambda e: e.activation(out=out, in_=in_, func=func, **kw), reads, writes)

    def tt(self, eng, out, in0, in1, op, reads, writes):
        return self.op(eng, lambda e: e.tensor_tensor(out=out, in0=in0, in1=in1, op=op), reads, writes)

    def ts(self, eng, out, in0, s1, s2, op0, op1, reads, writes):
        if s2 is None:
            return self.op(eng, lambda e: e.tensor_single_scalar(out=out, in_=in0, scalar=s1, op=op0), reads, writes)
        return self.op(eng, lambda e: e.tensor_scalar(out=out, in0=in0, scalar1=s1, scalar2=s2, op0=op0, op1=op1),
                       reads, writes)

    def stt(self, eng, out, in0, scalar, in1, op0, op1, reads, writes):
        return self.op(eng, lambda e: e.scalar_tensor_tensor(out=out, in0=in0, scalar=scalar, in1=in1, op0=op0, op1=op1),
                       reads, writes)

    def copy(self, eng, out, in_, reads, writes):
        if eng == "act":
            return self.op(eng, lambda e: e.activation(out=out, in_=in_, func=AF.Identity), reads, writes)
        return self.op(eng, lambda e: e.tensor_copy(out=out, in_=in_), reads, writes)

    def memset(self, eng, ap, val, writes):
        return self.op(eng, lambda e: e.memset(ap, val), (), writes)

    def emit(self, final_ops):
        nc = self.nc
        for e in ("pe", "act", "dve", "pool"):
            c = 0
            for o in self.ops[e]:
                if o.dsem is None and o.needed:
                    c += 1
                    o.cnt = c
                elif o.dsem is None:
                    o.cnt = -1
        for o in final_ops:
            assert o.dsem is not None
        handles = {"pe": None, "act": None, "dve": None, "pool": None, "sp": None}
        prog = self

        def run(eng_name, eng):
            waited = {}
            for o in prog.ops[eng_name]:
                for d in o.deps:
                    if d.dsem is not None:
                        key = id(d.dsem)
                        val = 16 * d.dcnt
                        sem = d.dsem.sem
                    else:
                        key = d.eng
                        val = d.cnt
                        sem = prog.esem[d.eng]
                        assert val > 0
                    if waited.get(key, 0) < val:
                        eng.wait_ge(sem, val)
                        waited[key] = val
                ins = o.fn(eng)
                if o.dsem is not None:
                    ins.then_inc(o.dsem.sem, 16)
                elif o.needed:
                    ins.then_inc(prog.esem[eng_name], 1)
            if eng_name == "pool":
                for o in final_ops:
                    eng.wait_ge(o.dsem.sem, 16 * o.dcnt)

        with nc.Block() as block:
            @block.tensor
            def _(e):
                run("pe", e)

            @block.scalar
            def _(e):
                run("act", e)

            @block.vector
            def _(e):
                run("dve", e)

            @block.gpsimd
            def _(e):
                run("pool", e)

            @block.sync
            def _(e):
                run("sp", e)


class VecLayout:
    def __init__(self):
        self.off = {}
        self.n = 0

    def add(self, name, ncols):
        self.off[name] = (self.n, ncols)
        self.n += ncols

    def sl(self, name, i=0, n=None):
        o, c = self.off[name]
        if n is None:
            n = c - i
        return slice(o + i, o + i + n)


def make_layout():
    L = VecLayout()
    L.add("cvec", 32)
    for l in range(NLAYER):
        L.add("ada_b%d" % l, 144)
        L.add("norm_g%d" % l, 48)
        L.add("b_merge%d" % l, 48)
        L.add("rconv_w%d" % l, 32)
        L.add("rconv_b%d" % l, 8)
        L.add("lru_b_a%d" % l, 16)
        L.add("lru_b_x%d" % l, 16)
        L.add("lam%d" % l, 16)
        L.add("sconv_w%d" % l, 24)
        L.add("sink%d" % l, 8)
    L.add("fnorm_g", 16)
    return L


def fm(v):
    v = np.asarray(v, np.float32)
    return np.ascontiguousarray(v.reshape(-1, 128).T)


def build(T):
    TTOT = T + NCTX
    nc = bass.Bass("TRN2", target_bir_lowering=False)
    L = make_layout()

    def din(name, shape, dt=F32):
        return nc.dram_tensor(name, list(shape), dt, kind="ExternalInput").ap()

    def dscr(name, shape, dt):
        return nc.dram_tensor(name, list(shape), dt, kind="Internal").ap()

    xT = din("xT", [D, TTOT])
    vecs_d = din("vecs", [128, L.n])
    ada_w = din("ada_w", [NLAYER, D, NMOD * D])
    w13_d = [din("ffn1_w13", [NLAYER, D, 2 * DFF]), din("ffn2_w13", [NLAYER, D, 2 * DFF])]
    w2_d = [din("ffn1_w2", [NLAYER, DFF, D]), din("ffn2_w2", [NLAYER, DFF, D])]
    w_in_d = din("w_in", [NLAYER, D, INCOLS])
    wbr_d = din("w_branch", [NLAYER, 3, 1024, D])
    wout_d = din("w_out", [NLAYER, D, D])
    lruw_d = din("lru_w", [NLAYER, 128, 32, 128])
    rope_d = din("rope", [2, 128, T])
    cbf_d = din("cbf", [128, CBFN], BF16)
    outT = nc.dram_tensor("outT", [D, T], F32, kind="ExternalOutput").ap()

    hbuf = dscr("hbuf", [D, TTOT], F32)
    zr = dscr("zr", [5 * 1024, TTOT], F32)
    zq = dscr("zq", [10 * 128, TTOT], BF16)
    zv = dscr("zv", [TTOT, 256], BF16)
    zg = dscr("zg", [3 * D, TTOT], BF16)
    hfb = dscr("hfb", [1024, TTOT], F32)
    ybuf = dscr("ybuf", [3 * 1024, TTOT], BF16)
    pw13 = [[dscr("pw13_%d_%d" % (f, l), [FC, 128, DC, 256], BF16) for l in range(NLAYER)] for f in range(2)]
    pw2 = [[dscr("pw2_%d_%d" % (f, l), [DC, 128, FC, 128], BF16) for l in range(NLAYER)] for f in range(2)]
    pwin = [dscr("pwin_%d" % l, [50, 128, DC, 256], BF16) for l in range(NLAYER)]
    pwbr = [[dscr("pwbr_%d_%d" % (l, i), [8, 128, 8, 256], BF16) for i in range(3)] for l in range(NLAYER)]
    pwout = [dscr("pwout_%d" % l, [8, 128, DC, 256], BF16) for l in range(NLAYER)]

    xTv = xT.rearrange("(c p) t -> p c t", p=128)
    hbv = hbuf.rearrange("(c p) t -> p c t", p=128)
    zrv = zr.rearrange("(c p) t -> p c t", p=128)
    zqv = zq.rearrange("(c p) t -> p c t", p=128)
    zgv = zg.rearrange("(c p) t -> p c t", p=128)
    zg4 = zg.rearrange("(i c p) t -> p i c t", i=3, p=128)
    ybv = ybuf.rearrange("(c p) t -> p c t", p=128)
    outv = outT.rearrange("(c p) t -> p c t", p=128)

    with ExitStack() as es:
        P = Prog(nc, es)
        P.make_banks()
        d_w = Buf("d_weights")
        d_in = Buf("d_inputs")
        d_h = {}
        d_z = Buf("d_z")
        d_y = Buf("d_y")
        d_hf = Buf("d_hf")
        d_out = Buf("d_out")

        def dh(key):
            if key not in d_h:
                d_h[key] = Buf("d_h%s" % (key,))
            return d_h[key]

        vecs = P.sbuf("vecs", [128, L.n], F32, dma=True)
        DVN = NLAYER * 2 * (144 + 9 * 16) + NLAYER * 32 + 64
        dv = P.sbuf("dv", [128, DVN], F32)
        cbf = P.sbuf("cbf", [128, CBFN], BF16, dma=True)
        ones_f = P.sbuf("ones_f", [128, 128], F32)
        ones_b = P.sbuf("ones_b", [128, 128], BF16)
        perm_ap = cbf[:, 0:128]
        ident_ap = cbf[:, 128:256]
        maskp_ap = cbf[:, 256:768]
        maskn_ap = cbf[:, 768:1280]

        dvo = {}
        dvn = [0]

        def dvadd(name, n):
            dvo[name] = (dvn[0], n)
            dvn[0] += n

        for l in range(NLAYER):
            for s in range(2):
                dvadd("mod%d%d" % (l, s), 144)
                for nm in ("A1", "B1", "G1", "A2", "B2", "G2", "A3", "B3", "G3"):
                    dvadd("%s%d%d" % (nm, l, s), 16)
            dvadd("lc%d" % l, 16)
            dvadd("lc2%d" % l, 16)
        dvadd("sc", 32)
        dvadd("tmp", 32)
        assert dvn[0] <= DVN

        def DV(name, i=0, n=None):
            o, c = dvo[name]
            if n is None:
                n = c - i
            return dv[:, o + i:o + i + n]

        def VC(name, i=0, n=None):
            return vecs[:, L.sl(name, i, n)]

        P.dma("sp", vecs[:, :], vecs_d, vecs, reads=[d_in], writes=[vecs])
        P.dma("sp", cbf[:, :], cbf_d, cbf, reads=[d_in], writes=[cbf])
        P.memset("dve", ones_f[:, :], 1.0, [ones_f])
        P.memset("dve", ones_b[:, :], 1.0, [ones_b])

        import os
        kstop0 = int(os.environ.get("KSTOP", "99"))
        import os
        kstop0 = int(os.environ.get("KSTOP", "99"))
        with P.phase():
            STG = 11008
            stg = [P.sbuf("stg%d" % i, [128, STG], F32, dma=True) for i in range(2)]
            pbf = [P.sbuf("pbf%d" % i, [128, STG], BF16, dma=True) for i in range(2)]
            cnt = [0]
            ceng = ["act", "dve", "pool"]
            cei = [0]
            groups = []

            def cast(out, in_, sbuf_in, sbuf_out):
                e = ceng[cei[0] % 3]
                cei[0] += 1
                P.copy(e, out, in_, [sbuf_in], [sbuf_out])

            def prep_generic(W, kcs, npan, dest, G):
                Wv = W.rearrange("(k p) n -> p k n", p=128)
                for q0 in range(0, npan, G):
                    def grp(q0=q0):
                        g = min(G, npan - q0)
                        s_ = stg[cnt[0] % 2]
                        b_ = pbf[cnt[0] % 2]
                        cnt[0] += 1
                        sv = s_[:, 0:kcs * g * 256].rearrange("p (k n) -> p k n", k=kcs)
                        P.dma("sp", sv, Wv[:, :, q0 * 256:(q0 + g) * 256], s_, reads=[d_in], writes=[s_])
                        bv = b_[:, 0:g * kcs * 256].rearrange("p (q k n) -> p q k n", q=g, k=kcs)
                        for q in range(g):
                            cast(bv[:, q, :, :], sv[:, :, q * 256:(q + 1) * 256], s_, b_)
                        P.dma("pool", dest[q0:q0 + g].rearrange("q p k n -> p q (k n)"),
                              b_[:, 0:g * kcs * 256].rearrange("p (q m) -> p q m", q=g), b_, reads=[b_], writes=[d_w])
                    groups.append(grp)

            def prep_w13(W, dest):
                Wv = W.rearrange("(k p) n -> p k n", p=128)
                G = 2
                for j0 in range(0, FC, G):
                    def grp(j0=j0):
                        g = min(G, FC - j0)
                        s_ = stg[cnt[0] % 2]
                        b_ = pbf[cnt[0] % 2]
                        cnt[0] += 1
                        sv = s_[:, 0:DC * 2 * g * 128].rearrange("p (k s n) -> p k s n", k=DC, s=2)
                        P.dma("sp", sv[:, :, 0, :], Wv[:, :, j0 * 128:(j0 + g) * 128], s_, reads=[d_in], writes=[s_])
                        P.dma("sp", sv[:, :, 1, :], Wv[:, :, DFF + j0 * 128:DFF + (j0 + g) * 128], s_, reads=[d_in], writes=[s_])
                        bv = b_[:, 0:g * DC * 256].rearrange("p (q k s n) -> p q k s n", q=g, k=DC, s=2)
                        for q in range(g):
                            for s2 in range(2):
                                cast(bv[:, q, :, s2, :], sv[:, :, s2, q * 128:(q + 1) * 128], s_, b_)
                        P.dma("pool", dest[j0:j0 + g].rearrange("q p k n -> p q (k n)"),
                              b_[:, 0:g * DC * 256].rearrange("p (q m) -> p q m", q=g), b_, reads=[b_], writes=[d_w])
                    groups.append(grp)

            def prep_w2(W, dest):
                Wv = W.rearrange("(k p) n -> p k n", p=128)
                for c0 in range(0, DC, 2):
                    def grp(c0=c0):
                        s_ = stg[cnt[0] % 2]
                        b_ = pbf[cnt[0] % 2]
                        cnt[0] += 1
                        sv = s_[:, 0:FC * 256].rearrange("p (k n) -> p k n", k=FC)
                        P.dma("sp", sv, Wv[:, :, c0 * 128:(c0 + 2) * 128], s_, reads=[d_in], writes=[s_])
                        bv = b_[:, 0:2 * FC * 128].rearrange("p (q k n) -> p q k n", q=2, k=FC)
                        for q in range(2):
                            cast(bv[:, q, :, :], sv[:, :, q * 128:(q + 1) * 128], s_, b_)
                        P.dma("pool", dest[c0:c0 + 2].rearrange("q p k n -> p q (k n)"),
                              b_[:, 0:2 * FC * 128].rearrange("p (q m) -> p q m", q=2), b_, reads=[b_], writes=[d_w])
                    groups.append(grp)

            for l in range(NLAYER if kstop0 >= 1 else 0):
                for f in range(2):
                    prep_w13(w13_d[f][l], pw13[f][l])
                    prep_w2(w2_d[f][l], pw2[f][l])
                prep_generic(w_in_d[l], DC, 50, pwin[l], 2)
                for i in range(3):
                    prep_generic(wbr_d[l, i], 8, 8, pwbr[l][i], 4)
                prep_generic(wout_d[l], DC, 8, pwout[l], 2)
            NAP = 72 * NLAYER
            per = (len(groups) + NAP - 1) // NAP
            gpos = [0]

            def emit_groups(k):
                for _ in range(k):
                    if gpos[0] < len(groups):
                        groups[gpos[0]]()
                        gpos[0] += 1

            apan = [P.sbuf("apan%d" % i, [128, DC, 256], F32, dma=True) for i in range(2)]
            P.act(DV("sc"), VC("cvec"), AF.Silu, [vecs], [dv])
            sc3 = DV("sc").rearrange("p (k s) -> p k s", s=2)
            pi = 0
            for l in range(NLAYER):
                bk = P.bank()
                bk3 = bk[:, 0:288].rearrange("p (j s) -> p j s", s=2)
                for pq in range(72):
                    pan = apan[pi % 2]
                    pi += 1
                    src = ada_w[l].rearrange("(k p) n -> p k n", p=128)[:, :, pq * 256:(pq + 1) * 256]
                    P.dma("sp", pan[:, :, :], src, pan, reads=[d_in], writes=[pan])
                    for jj in range(2):
                        j = pq * 2 + jj
                        for kc in range(DC):
                            P.mm(bk3[:, j, :], pan[:, kc, jj * 128:(jj + 1) * 128], sc3[:, kc, :],
                                 kc == 0, kc == DC - 1, [pan, dv], [bk])
                    emit_groups(per)
                for s in range(2):
                    P.tt("dve", DV("mod%d%d" % (l, s)), bk3[:, :, s], VC("ada_b%d" % l), ALU.add, [bk, vecs], [dv])
                    md = lambda m, l=l, s=s: DV("mod%d%d" % (l, s), m * 16, 16)
                    ng = lambda i, l=l: VC("norm_g%d" % l, i * 16, 16)
                    for (nm, gi, sh, scl, gt, gmul) in (("1", 0, 0, 1, 2, 0.5), ("2", 1, 3, 4, 5, 1.0), ("3", 2, 6, 7, 8, 0.5)):
                        P.stt("dve", DV("A%s%d%d" % (nm, l, s)), md(scl), 1.0, ng(gi), ALU.add, ALU.mult, [dv, vecs], [dv])
                        P.copy("dve", DV("B%s%d%d" % (nm, l, s)), md(sh), [dv], [dv])
                        P.ts("dve", DV("G%s%d%d" % (nm, l, s)), md(gt), gmul, None, ALU.mult, None, [dv], [dv])
                P.act(DV("tmp", 0, 16), VC("lam%d" % l), AF.Exp, [vecs], [dv], scale=-1.0)
                P.act(DV("tmp", 16, 16), DV("tmp", 0, 16), AF.Ln, [dv], [dv], bias=1.0)
                P.ts("dve", DV("lc%d" % l), DV("tmp", 16, 16), -8.0, None, ALU.mult, None, [dv], [dv])
                P.ts("dve", DV("lc2%d" % l), DV("tmp", 16, 16), -16.0, None, ALU.mult, None, [dv], [dv])
            emit_groups(len(groups))

        def tiles(with_ctx=True):
            ts_ = [(t0, TT, 0) for t0 in range(0, T, TT)]
            if with_ctx:
                ts_.append((T, NCTX, 1))
            return ts_

        class TileCtx:
            pass

        def alloc_tile_bufs():
            C = TileCtx()
            C.htile = P.sbuf("htile", [128, DC, TT], F32, dma=True)
            C.ub = [P.sbuf("ub%d" % i, [128, TT], BF16) for i in range(DC)]
            C.big = P.sbuf("big", [128, FC * TT], BF16, dma=True)
            C.ring8 = [P.sbuf("ring8_%d" % i, [128, DC * 256], BF16, dma=True) for i in range(3)]
            C.ring11 = [P.sbuf("ring11_%d" % i, [128, FC * 128], BF16, dma=True) for i in range(2)]
            C.r8i = 0
            C.r11i = 0
            C.tmpf = [P.sbuf("tmpf%d" % i, [128, TT], F32) for i in range(4)]
            C.tfi = 0
            C.rstd = P.sbuf("rstd", [128, TT], F32)
            C.tmpb = [P.sbuf("tmpb%d" % i, [128, TT], BF16) for i in range(3)]
            C.tbi = 0
            return C

        def tmp(C):
            b = C.tmpf[C.tfi % 4]
            C.tfi += 1
            return b

        def next8(C):
            b = C.ring8[C.r8i % 3]
            C.r8i += 1
            return b

        def next11(C):
            b = C.ring11[C.r11i % 2]
            C.r11i += 1
            return b

        def rms_stats(C, n):
            htile = C.htile
            bk = P.bank()
            for c in range(DC):
                sq = C.tmpb[C.tbi % 3]
                C.tbi += 1
                P.act(sq[:, :n], htile[:, c, :n], AF.Square, [htile], [sq])
                P.mm(bk[:, :n], ones_b[:, :], sq[:, :n], c == 0, c == DC - 1, [ones_b, sq], [bk])
            P.act(C.rstd[:, :n], bk[:, :n], AF.Sqrt, [bk], [C.rstd], scale=1.0 / D, bias=EPS)
            P.op("dve", lambda e: e.reciprocal(out=C.rstd[:, :n], in_=C.rstd[:, :n]), [C.rstd], [C.rstd])

        def rmsnorm_mod(C, n, A, B):
            rms_stats(C, n)
            for c in range(DC):
                t_ = tmp(C)
                P.stt("dve", t_[:, :n], C.htile[:, c, :n], A[:, c:c + 1], C.rstd[:, :n], ALU.mult, ALU.mult,
                      [C.htile, C.rstd, dv, vecs], [t_])
                P.act(C.ub[c][:, :n], t_[:, :n], AF.Identity, [t_, dv], [C.ub[c]], bias=B[:, c:c + 1])

        def ffn(C, f, l, n, G):
            htile, ub, big = C.htile, C.ub, C.big
            actv = big[:, 0:FC * TT].rearrange("p (j t) -> p j t", j=FC)
            for j in range(FC):
                slot = next8(C)
                sv = slot[:, :].rearrange("p (k n) -> p k n", k=DC)
                P.dma("sp", slot[:, :], pw13[f][l][j].rearrange("p k n -> p (k n)"), slot, reads=[d_w], writes=[slot])
                bg = P.bank()
                bu = P.bank()
                for kc in range(DC):
                    P.mm(bg[:, :n], sv[:, kc, 0:128], ub[kc][:, :n], kc == 0, kc == DC - 1, [slot, ub[kc]], [bg])
                for kc in range(DC):
                    P.mm(bu[:, :n], sv[:, kc, 128:256], ub[kc][:, :n], kc == 0, kc == DC - 1, [slot, ub[kc]], [bu])
                sg = tmp(C)
                P.act(sg[:, :n], bg[:, :n], AF.Silu, [bg], [sg])
                P.tt("dve", actv[:, j, :n], sg[:, :n], bu[:, :n], ALU.mult, [sg, bu], [big])
            for c in range(DC):
                slot = next11(C)
                sv = slot[:, :].rearrange("p (k n) -> p k n", k=FC)
                P.dma("sp", slot[:, :], pw2[f][l][c].rearrange("p k n -> p (k n)"), slot, reads=[d_w], writes=[slot])
                bk = P.bank()
                for j in range(FC):
                    P.mm(bk[:, :n], sv[:, j, :], actv[:, j, :n], j == 0, j == FC - 1, [slot, big], [bk])
                P.stt("dve", htile[:, c, :n], bk[:, :n], G[:, c:c + 1], htile[:, c, :n], ALU.mult, ALU.add,
                      [bk, htile, dv], [htile])

        def phaseA(l):
            with P.phase():
                C = alloc_tile_bufs()
                htile, ub = C.htile, C.ub
                ropets = [P.sbuf("ropet%d" % i, [128, 2, TT], F32, dma=True) for i in range(2)]
                qbf = [P.sbuf("qbf%d" % i, [128, TT], BF16) for i in range(2)]
                stf = [P.sbuf("stf%d" % i, [128, 2, TT], F32, dma=True) for i in range(3)]
                stb = [P.sbuf("stb%d" % i, [128, 2, TT], BF16, dma=True) for i in range(3)]
                stgv = P.sbuf("stgv", [128, 4, 256], BF16, dma=True)
                qbi = 0
                sfi = 0
                sbi = 0
                tl = tiles()
                src = xTv if l == 0 else hbv

                def load_h(ti):
                    (t0_, n_, s_) = tl[ti]
                    P.dma("sp", htile[:, :, :n_], src[:, :, t0_:t0_ + n_], htile,
                          reads=[d_in if l == 0 else dh((l, "c", t0_))], writes=[htile])
                load_h(0)
                for ti, (t0, n, s) in enumerate(tl):
                    ropet = ropets[ti % 2]
                    if not s:
                        P.dma("sp", ropet[:, :, :n], rope_d.rearrange("s p t -> p s t")[:, :, t0:t0 + n], ropet,
                              reads=[d_in], writes=[ropet])
                    rmsnorm_mod(C, n, DV("A1%d%d" % (l, s)), DV("B1%d%d" % (l, s)))
                    ksub = int(os.environ.get("KSUB", "99"))
                    if ksub >= 2:
                        ffn(C, 0, l, n, DV("G1%d%d" % (l, s)))
                    P.dma(os.environ.get("KSTQ", "pool"), hbv[:, :, t0:t0 + n], htile[:, :, :n], htile, reads=[htile], writes=[dh((l, "a", t0))])
                    if ksub < 3:
                        if ti + 1 < len(tl):
                            load_h(ti + 1)
                        continue
                    rmsnorm_mod(C, n, DV("A2%d%d" % (l, s)), DV("B2%d%d" % (l, s)))
                    if ti + 1 < len(tl):
                        load_h(ti + 1)
                    for q in range({3: 20, 4: 25, 5: 26}.get(ksub, 50)):
                        slot = next8(C)
                        sv = slot[:, :].rearrange("p (k n) -> p k n", k=DC)
                        P.dma("sp", slot[:, :], pwin[l][q].rearrange("p k n -> p (k n)"), slot, reads=[d_w], writes=[slot])
                        if q == 25:
                            for sb_ in range(n // 128):
                                bk = P.bank()
                                for kc in range(DC):
                                    P.mm(bk[:, 0:256], ub[kc][:, sb_ * 128:(sb_ + 1) * 128], sv[:, kc, :],
                                         kc == 0, kc == DC - 1, [slot, ub[kc]], [bk])
                                P.copy("act", stgv[:, sb_, :], bk[:, 0:256], [bk], [stgv])
                            P.dma("pool", zv[t0:t0 + n, :].rearrange("(b p) f -> p b f", p=128), stgv[:, 0:n // 128, :],
                                  stgv, reads=[stgv], writes=[d_z])
                            continue
                        if q < 20:
                            st_ = stf[sfi % 3]
                            sfi += 1
                        else:
                            st_ = stb[sbi % 3]
                            sbi += 1
                        for half in range(2):
                            ch = 2 * q + half
                            bk = P.bank()
                            for kc in range(DC):
                                P.mm(bk[:, :n], sv[:, kc, half * 128:(half + 1) * 128], ub[kc][:, :n],
                                     kc == 0, kc == DC - 1, [slot, ub[kc]], [bk])
                            if ch < 40:
                                P.copy("dve" if half == 0 else "act", st_[:, half, :n], bk[:, :n], [bk], [st_])
                            elif ch < 50:
                                if s:
                                    P.copy("act", st_[:, half, :n], bk[:, :n], [bk], [st_])
                                else:
                                    qb = qbf[qbi % 2]
                                    qbi += 1
                                    P.copy("act", qb[:, :n], bk[:, :n], [bk], [qb, bk])
                                    bk2 = P.bank()
                                    P.mm(bk2[:, :n], perm_ap, qb[:, :n], True, True, [cbf, qb], [bk2])
                                    t1 = tmp(C)
                                    t2 = tmp(C)
                                    P.tt("dve", t1[:, :n], bk[:, :n], ropet[:, 0, :n], ALU.mult, [bk, ropet], [t1])
                                    P.tt("dve", t2[:, :n], bk2[:, :n], ropet[:, 1, :n], ALU.mult, [bk2, ropet], [t2])
                                    P.tt("dve", st_[:, half, :n], t1[:, :n], t2[:, :n], ALU.add, [t1, t2], [st_])
                            else:
                                gi = ch - 52
                                P.act(st_[:, half, :n], bk[:, :n], AF.Sigmoid, [bk, vecs], [st_],
                                      bias=VC("b_merge%d" % l, gi, 1))
                        c0 = 2 * q
                        if c0 < 40:
                            P.dma("pool", zrv[:, c0:c0 + 2, t0:t0 + n], st_[:, :, :n], st_, reads=[st_], writes=[d_z])
                        elif c0 < 50:
                            P.dma("pool", zqv[:, c0 - 40:c0 - 38, t0:t0 + n], st_[:, :, :n], st_, reads=[st_], writes=[d_z])
                        else:
                            P.dma("pool", zgv[:, c0 - 52:c0 - 50, t0:t0 + n], st_[:, :, :n], st_, reads=[st_], writes=[d_z])

        def phaseB(l, want_ctx):
            with P.phase():
                lw = P.sbuf("lw", [128, 32, 128], BF16, dma=True)
                P.dma("pool", lw[:, :, :], lruw_d[l], lw, reads=[d_in], writes=[lw])
                st = P.sbuf("st", [128, 8, 2], F32)
                zero1 = P.sbuf("zero1", [128, 1], F32)
                P.memset("dve", zero1[:, :], 0.0, [zero1])
                carry = P.sbuf("carry", [128, 1], F32)
                xaf = P.sbuf("xaf", [128, T], F32)
                xabf = P.sbuf("xabf", [128, T], BF16)
                xac = P.sbuf("xac", [128, NCTX], F32)
                xacb = P.sbuf("xacb", [128, NCTX], BF16)
                rxh = [P.sbuf("rxh%d" % i, [128, SB + 3], F32, dma=True) for i in range(2)]
                rb = P.sbuf("rb", [128, SB], F32)
                igb = [P.sbuf("igb%d" % i, [128, SB], F32) for i in range(2)]
                ab = [P.sbuf("ab%d" % i, [128, SB], F32) for i in range(2)]
                a2b = [P.sbuf("a2b%d" % i, [128, SB], F32) for i in range(2)]
                ub = P.sbuf("ub", [128, SB], F32)
                hb = [P.sbuf("hb%d" % i, [128, SB], F32, dma=True) for i in range(2)]
                hfl = [P.sbuf("hfl%d" % i, [128, SB], F32, dma=True) for i in range(2)]
                rgl = [P.sbuf("rgl%d" % i, [128, SB], F32, dma=True) for i in range(2)]
                yab = [P.sbuf("yab%d" % i, [128, SB], BF16, dma=True) for i in range(2)]
                scgl = [P.sbuf("scgl%d" % i, [128, SB + 2], F32, dma=True) for i in range(2)]
                sxl = [P.sbuf("sxl%d" % i, [128, SB + 2], F32, dma=True) for i in range(2)]
                sbl = [P.sbuf("sbl%d" % i, [128, SB], F32, dma=True) for i in range(2)]
                pbs = [P.sbuf("pbs%d" % i, [128, SB + 2], F32) for i in range(2)]
                yts = [[P.sbuf("yts%d_%d" % (i, k), [128, SB], F32) for k in range(3)] for i in range(2)]
                ybo = [P.sbuf("ybo%d" % i, [128, SB], BF16, dma=True) for i in range(2)]
                cnt = {"rx": 0, "g": 0, "h": 0, "c": 0, "y": 0}

                def halo_load(buf, grp, n, base, length, s0, Lb, left, right):
                    lo = max(0, s0 - left)
                    hi = min(length, s0 + Lb + right)
                    o0 = s0 - left
                    if lo > o0:
                        P.memset("dve", buf[:, 0:lo - o0], 0.0, [buf])
                    if hi < s0 + Lb + right:
                        P.memset("dve", buf[:, hi - o0:Lb + left + right], 0.0, [buf])
                    row = grp * 8 + n
                    P.dma("sp", buf[:, lo - o0:hi - o0], zrv[:, row, base + lo:base + hi], buf, reads=[d_z], writes=[buf])

                def conv_block(n, base, length, s0, Lb, xa_t, xab_t, c0):
                    rx_ = rxh[cnt["rx"] % 2]
                    cnt["rx"] += 1
                    halo_load(rx_, 0, n, base, length, s0, Lb, 2, 1)
                    wv = lambda k: VC("rconv_w%d" % l, k * 8 + n, 1)
                    P.ts("dve", xa_t[:, c0:c0 + Lb], rx_[:, 0:Lb], wv(0), VC("rconv_b%d" % l, n, 1), ALU.mult, ALU.add,
                         [rx_, vecs], [xa_t])
                    for k in range(1, 4):
                        P.stt("dve", xa_t[:, c0:c0 + Lb], rx_[:, k:k + Lb], wv(k), xa_t[:, c0:c0 + Lb], ALU.mult, ALU.add,
                              [rx_, xa_t, vecs], [xa_t])
                    P.copy("dve", xab_t[:, c0:c0 + Lb], xa_t[:, c0:c0 + Lb], [xa_t], [xab_t])

                def gate_acts(n, d, Lb, xab_t, c0):
                    gi = cnt["g"] % 2
                    cnt["g"] += 1
                    ig_, a_, a2_ = igb[gi], ab[gi], a2b[gi]
                    for s_ in range(0, Lb, 512):
                        ns = min(512, Lb - s_)
                        ba = P.bank()
                        bx = P.bank()
                        P.mm(ba[:, :ns], lw[:, (0 * 2 + d) * 8 + n, :], xab_t[:, c0 + s_:c0 + s_ + ns], True, True, [lw, xab_t], [ba])
                        P.mm(bx[:, :ns], lw[:, (1 * 2 + d) * 8 + n, :], xab_t[:, c0 + s_:c0 + s_ + ns], True, True, [lw, xab_t], [bx])
                        P.act(rb[:, s_:s_ + ns], ba[:, :ns], AF.Sigmoid, [ba, vecs], [rb],
                              bias=VC("lru_b_a%d" % l, d * 8 + n, 1))
                        P.act(ig_[:, s_:s_ + ns], bx[:, :ns], AF.Sigmoid, [bx, vecs], [ig_],
                              bias=VC("lru_b_x%d" % l, d * 8 + n, 1))
                    P.act(a_[:, :Lb], rb[:, :Lb], AF.Exp, [rb, dv], [a_], scale=DV("lc%d" % l, d * 8 + n, 1))
                    P.act(a2_[:, :Lb], rb[:, :Lb], AF.Exp, [rb, dv], [a2_], scale=DV("lc2%d" % l, d * 8 + n, 1))
                    P.act(a2_[:, :Lb], a2_[:, :Lb], AF.Relu, [a2_], [a2_], scale=-1.0, bias=1.0)
                    P.act(a2_[:, :Lb], a2_[:, :Lb], AF.Sqrt, [a2_], [a2_])
                    return (ig_, a_, a2_)

                def gate_scan(n, d, base, s0, Lb, xa_t, c0, want_y, gbufs):
                    ig_, a_, a2_ = gbufs
                    P.tt("dve", ub[:, :Lb], a2_[:, :Lb], ig_[:, :Lb], ALU.mult, [a2_, ig_], [ub])
                    P.tt("dve", ub[:, :Lb], ub[:, :Lb], xa_t[:, c0:c0 + Lb], ALU.mult, [ub, xa_t], [ub])
                    h_ = hb[cnt["h"] % 2]
                    cnt["h"] += 1
                    if d == 0:
                        P.op("dve", lambda e, Lb=Lb, h_=h_, a_=a_: e.tensor_tensor_scan(
                            out=h_[:, 0:Lb], data0=a_[:, 0:Lb], data1=ub[:, 0:Lb], initial=carry[:, 0:1],
                            op0=ALU.mult, op1=ALU.add), [a_, ub, carry], [h_])
                        P.copy("dve", carry[:, :], h_[:, Lb - 1:Lb], [h_], [carry])
                    else:
                        rs = slice(Lb - 1, None, -1)
                        P.op("dve", lambda e, rs=rs, h_=h_, a_=a_: e.tensor_tensor_scan(
                            out=h_[:, rs], data0=a_[:, rs], data1=ub[:, rs], initial=carry[:, 0:1],
                            op0=ALU.mult, op1=ALU.add), [a_, ub, carry], [h_])
                        P.copy("dve", carry[:, :], h_[:, 0:1], [h_], [carry])
                    if want_y:
                        if d == 0:
                            P.dma("pool", hfb[n * 128:(n + 1) * 128, base + s0:base + s0 + Lb], h_[:, :Lb], h_,
                                  reads=[h_], writes=[d_hf])
                        else:
                            ci = cnt["c"] % 2
                            cnt["c"] += 1
                            hf_, rg_, ya_ = hfl[ci], rgl[ci], yab[ci]
                            P.dma("sp", hf_[:, :Lb], hfb[n * 128:(n + 1) * 128, base + s0:base + s0 + Lb], hf_,
                                  reads=[d_hf], writes=[hf_])
                            P.dma("sp", rg_[:, :Lb], zrv[:, 8 + n, base + s0:base + s0 + Lb], rg_,
                                  reads=[d_z], writes=[rg_])
                            P.act(rg_[:, :Lb], rg_[:, :Lb], AF.Gelu_apprx_tanh, [rg_], [rg_])
                            P.tt("dve", hf_[:, :Lb], hf_[:, :Lb], h_[:, :Lb], ALU.add, [hf_, h_], [hf_])
                            P.tt("dve", ya_[:, :Lb], hf_[:, :Lb], rg_[:, :Lb], ALU.mult, [hf_, rg_], [ya_])
                            P.dma("pool", ybv[:, n, base + s0:base + s0 + Lb], ya_[:, :Lb], ya_,
                                  reads=[ya_], writes=[d_y])

                def sconv_p(n, base, length, s0, Lb):
                    si = cnt["y"] % 2
                    cnt["y"] += 1
                    halo_load(scgl[si], 3, n, base, length, s0, Lb, 1, 1)
                    halo_load(sxl[si], 4, n, base, length, s0, Lb, 1, 1)
                    P.dma("sp", sbl[si][:, :Lb], zrv[:, 16 + n, base + s0:base + s0 + Lb], sbl[si], reads=[d_z], writes=[sbl[si]])
                    P.tt("dve", pbs[si][:, :Lb + 2], scgl[si][:, :Lb + 2], sxl[si][:, :Lb + 2], ALU.mult,
                         [scgl[si], sxl[si]], [pbs[si]])
                    return si

                def sconv_taps(n, Lb, si):
                    wv = lambda k: VC("sconv_w%d" % l, k * 8 + n, 1)
                    for k in range(3):
                        P.act(yts[si][k][:, :Lb], pbs[si][:, k:k + Lb], AF.Identity, [pbs[si], vecs], [yts[si][k]], scale=wv(k))

                def sconv_fin(n, base, s0, Lb, si):
                    y0, y1, y2 = yts[si]
                    P.tt("dve", y0[:, :Lb], y0[:, :Lb], y1[:, :Lb], ALU.add, [y0, y1], [y0])
                    P.tt("dve", y0[:, :Lb], y0[:, :Lb], y2[:, :Lb], ALU.add, [y0, y2], [y0])
                    yo = ybo[si]
                    P.tt("dve", yo[:, :Lb], y0[:, :Lb], sbl[si][:, :Lb], ALU.mult, [y0, sbl[si]], [yo])
                    P.dma("pool", ybv[:, 8 + n, base + s0:base + s0 + Lb], yo[:, :Lb], yo, reads=[yo], writes=[d_y])

                def rglru_seq(n, base, length, h0, want_y, save_final, xa_t, xab_t, do_sconv):
                    nb = (length + SB - 1) // SB
                    blocks = [(bi * SB, min(SB, length - bi * SB)) for bi in range(nb)]
                    for d in range(2):
                        P.copy("dve", carry[:, :], h0[d], [st, zero1], [carry])
                        order = blocks if d == 0 else list(reversed(blocks))
                        if d == 0:
                            conv_block(n, base, length, order[0][0], order[0][1], xa_t, xab_t, order[0][0])
                        for i, (s0, Lb) in enumerate(order):
                            si = None
                            if d == 1 and do_sconv:
                                si = sconv_p(n, base, length, s0, Lb)
                            gb = gate_acts(n, d, Lb, xab_t, s0)
                            if si is not None:
                                sconv_taps(n, Lb, si)
                            if d == 0 and i + 1 < len(order):
                                conv_block(n, base, length, order[i + 1][0], order[i + 1][1], xa_t, xab_t, order[i + 1][0])
                            gate_scan(n, d, base, s0, Lb, xa_t, s0, want_y, gb)
                            if si is not None:
                                sconv_fin(n, base, s0, Lb, si)
                        if save_final:
                            P.copy("dve", st[:, n, d:d + 1], carry[:, :], [carry], [st])

                for n in range(8):
                    rglru_seq(n, T, NCTX, [zero1[:, 0:1], zero1[:, 0:1]], want_ctx, True, xac, xacb, want_ctx)
                    rglru_seq(n, 0, T, [st[:, n, 0:1], st[:, n, 1:2]], True, False, xaf, xabf, True)

            with P.phase():
                esk = P.sbuf("esink", [128, 8, 128], F32)
                P.act(DV("tmp", 0, 8), VC("sink%d" % l), AF.Exp, [vecs], [dv])
                for h in range(8):
                    P.act(esk[:, h, :], ones_f[:, :], AF.Identity, [dv, ones_f], [esk], scale=DV("tmp", h, 1))
                kctx = P.sbuf("kctx", [128, 2, NCTX], BF16, dma=True)
                vctx = P.sbuf("vctx", [128, 2, 256], BF16, dma=True)
                P.dma("sp", kctx[:, :, :], zqv[:, 8:10, T:T + NCTX], kctx, reads=[d_z], writes=[kctx])
                P.dma("sp", vctx[:, :, :], zv[T:T + NCTX, :].rearrange("(b p) f -> p b f", p=128), vctx,
                      reads=[d_z], writes=[vctx])
                qg = [P.sbuf("qg%d" % i, [128, 8, 512], BF16, dma=True) for i in range(2)]
                kg = [P.sbuf("kg%d" % i, [128, 2, 768], BF16, dma=True) for i in range(2)]
                vg = [P.sbuf("vg%d" % i, [128, 6, 256], BF16, dma=True) for i in range(2)]
                og = [P.sbuf("og%d" % i, [128, 8, 512], BF16, dma=True) for i in range(2)]
                pts = [P.sbuf("pt%d" % i, [128, 512], BF16) for i in range(3)]
                dsm = [P.sbuf("dsm%d" % i, [128, 512], F32) for i in range(2)]
                pti = [0]
                dsi = [0]

                def attend(qt, qcol, chunks, kv, ogt, ocol):
                    bo = P.bank()
                    bd = P.bank()
                    rhs_q = qt[:, kv * 4:(kv + 1) * 4, qcol:qcol + 128]
                    sbanks = []

                    def score(ci):
                        kT, v_, m_, bufs = chunks[ci]
                        b_ = P.bank()
                        P.mm(b_[:, :], kT, rhs_q, True, m_ is None, [qt] + bufs, [b_])
                        if m_ is not None:
                            P.mm(b_[:, :], ident_ap, m_, False, True, [cbf], [b_])
                        sbanks.append(b_)
                    nchunk = len(chunks)
                    score(0)
                    if nchunk > 1:
                        score(1)
                    for ci in range(nchunk):
                        kT, v_, m_, bufs = chunks[ci]
                        pt = pts[pti[0] % 3]
                        pti[0] += 1
                        P.act(pt[:, :], sbanks[ci][:, :], AF.Exp, [sbanks[ci]], [pt], scale=ATT_SCALE)
                        P.mm(bo[:, :], v_, pt[:, :], ci == 0, ci == nchunk - 1, [pt] + bufs, [bo])
                        P.mm(bd[:, :], ones_b[:, :], pt[:, :], ci == 0, ci == nchunk - 1, [pt, ones_b], [bd])
                        if ci + 2 < nchunk:
                            score(ci + 2)
                    ds_ = dsm[dsi[0] % 2]
                    dsi[0] += 1
                    P.tt("dve", ds_[:, :], bd[:, :], esk[:, kv * 4:(kv + 1) * 4, :], ALU.add, [bd, esk], [ds_])
                    P.op("dve", lambda e: e.reciprocal(out=ds_[:, :], in_=ds_[:, :]), [ds_], [ds_])
                    P.tt("dve", ogt[:, kv * 4:(kv + 1) * 4, ocol:ocol + 128],
                         bo[:, :].rearrange("p (h q) -> p h q", h=4),
                         ds_[:, :].rearrange("p (h q) -> p h q", h=4), ALU.mult, [bo, ds_], [ogt])

                NBLK = T // 128
                for gq in range(T // 512):
                    g0 = gq * 512
                    qt, kt, vt, ot = qg[gq % 2], kg[gq % 2], vg[gq % 2], og[gq % 2]
                    P.dma("sp", qt[:, :, :], zqv[:, 0:8, g0:g0 + 512], qt, reads=[d_z], writes=[qt])
                    lo = max(0, g0 - 128)
                    hi = min(T, g0 + 640)
                    P.dma("sp", kt[:, :, lo - (g0 - 128):hi - (g0 - 128)], zqv[:, 8:10, lo:hi], kt, reads=[d_z], writes=[kt])
                    P.dma("sp", vt[:, (lo - (g0 - 128)) // 128:(hi - (g0 - 128)) // 128, :],
                          zv[lo:hi, :].rearrange("(b p) f -> p b f", p=128), vt, reads=[d_z], writes=[vt])
                    for i in range(4):
                        nblk = gq * 4 + i
                        for kv in range(2):
                            chunks = []
                            for rel in (-1, 0, 1):
                                if 0 <= nblk + rel < NBLK:
                                    kb = i + 1 + rel
                                    m_ = None if rel == 0 else (maskp_ap if rel < 0 else maskn_ap)
                                    chunks.append((kt[:, kv, kb * 128:(kb + 1) * 128], vt[:, kb, kv * 128:(kv + 1) * 128],
                                                   m_, [kt, vt]))
                            for cc in range(2):
                                chunks.append((kctx[:, kv, cc * 128:(cc + 1) * 128], vctx[:, cc, kv * 128:(kv + 1) * 128],
                                               None, [kctx, vctx]))
                            attend(qt, i * 128, chunks, kv, ot, i * 128)
                    P.dma("pool", ybv[:, 16:24, g0:g0 + 512], ot[:, :, :], ot, reads=[ot], writes=[d_y])
                if want_ctx:
                    qt, ot = qg[0], og[0]
                    P.dma("sp", qt[:, :, 0:NCTX], zqv[:, 0:8, T:T + NCTX], qt, reads=[d_z], writes=[qt])
                    for i in range(2):
                        for kv in range(2):
                            chunks = []
                            for cc in range(2):
                                chunks.append((kctx[:, kv, cc * 128:(cc + 1) * 128], vctx[:, cc, kv * 128:(kv + 1) * 128],
                                               None, [kctx, vctx]))
                            attend(qt, i * 128, chunks, kv, ot, i * 128)
                    P.dma("pool", ybv[:, 16:24, T:T + NCTX], ot[:, :, 0:NCTX], ot, reads=[ot], writes=[d_y])

        final_ops = []

        def phaseC(l, last):
            with P.phase():
                C = alloc_tile_bufs()
                htile, big = C.htile, C.big
                gts = [P.sbuf("gts%d" % i, [128, 3, TT], BF16, dma=True) for i in range(2)]
                gti = 0
                ytb = P.sbuf("ytile", [128, 24, TT], BF16, dma=True)
                ytile = ytb
                wbr = [P.sbuf("wbr%d" % i, [128, 8 * 256], BF16, dma=True) for i in range(4)]
                wbi = 0
                merged = big[:, 0:16 * TT].rearrange("p (c t) -> p c t", c=16)
                for (t0, n, s) in tiles(with_ctx=not last):
                    P.dma("sp", ytile[:, :, :n], ybv[:, :, t0:t0 + n], ytb, reads=[d_y], writes=[ytb])
                    for q in range(8):
                        bks = []
                        for i in range(3):
                            slot = wbr[wbi % 4]
                            wbi += 1
                            sv = slot[:, 0:8 * 256].rearrange("p (k n) -> p k n", k=8)
                            P.dma("sp", slot[:, 0:8 * 256], pwbr[l][i][q].rearrange("p k n -> p (k n)"), slot,
                                  reads=[d_w], writes=[slot])
                            pair = []
                            for half in range(2):
                                bk = P.bank()
                                for kc in range(8):
                                    P.mm(bk[:, :n], sv[:, kc, half * 128:(half + 1) * 128], ytile[:, i * 8 + kc, :n],
                                         kc == 0, kc == 7, [slot, ytb], [bk])
                                pair.append(bk)
                            bks.append(pair)
                        for half in range(2):
                            c = 2 * q + half
                            gt = gts[gti % 2]
                            gti += 1
                            P.dma("sp", gt[:, :, :n], zg4[:, :, c, t0:t0 + n], gt, reads=[d_z], writes=[gt])
                            ts3 = [tmp(C) for _ in range(3)]
                            for i in range(3):
                                P.tt("dve", ts3[i][:, :n], bks[i][half][:, :n], gt[:, i, :n], ALU.mult,
                                     [bks[i][half], gt], [ts3[i]])
                            P.tt("dve", ts3[0][:, :n], ts3[0][:, :n], ts3[1][:, :n], ALU.add, [ts3[0], ts3[1]], [ts3[0]])
                            P.tt("dve", merged[:, c, :n], ts3[0][:, :n], ts3[2][:, :n], ALU.add, [ts3[0], ts3[2]], [big])
                    G2 = DV("G2%d%d" % (l, s))
                    P.dma("sp", htile[:, :, :n], hbv[:, :, t0:t0 + n], htile, reads=[dh((l, "a", t0))], writes=[htile])
                    for q in range(8):
                        slot = next8(C)
                        sv = slot[:, :].rearrange("p (k n) -> p k n", k=DC)
                        P.dma("sp", slot[:, :], pwout[l][q].rearrange("p k n -> p (k n)"), slot, reads=[d_w], writes=[slot])
                        for half in range(2):
                            c = 2 * q + half
                            bk = P.bank()
                            for kc in range(DC):
                                P.mm(bk[:, :n], sv[:, kc, half * 128:(half + 1) * 128], merged[:, kc, :n],
                                     kc == 0, kc == DC - 1, [slot, big], [bk])
                            P.stt("dve", htile[:, c, :n], bk[:, :n], G2[:, c:c + 1], htile[:, c, :n], ALU.mult, ALU.add,
                                  [bk, htile, dv], [htile])
                    rmsnorm_mod(C, n, DV("A3%d%d" % (l, s)), DV("B3%d%d" % (l, s)))
                    ffn(C, 1, l, n, DV("G3%d%d" % (l, s)))
                    if last:
                        rms_stats(C, n)
                        for c in range(DC):
                            P.stt("dve", htile[:, c, :n], htile[:, c, :n], VC("fnorm_g", c, 1), C.rstd[:, :n],
                                  ALU.mult, ALU.mult, [htile, C.rstd, vecs], [htile])
                        o = P.dma("pool", outv[:, :, t0:t0 + n], htile[:, :, :n], htile, reads=[htile], writes=[d_out])
                        final_ops.append(o)
                    else:
                        P.dma("pool", hbv[:, :, t0:t0 + n], htile[:, :, :n], htile, reads=[htile],
                              writes=[dh((l + 1, "c", t0))])

        import os
        kstop = int(os.environ.get("KSTOP", "99"))
        step = 2
        for l in range(NLAYER):
            last = (l == NLAYER - 1)
            for ph in (phaseA, phaseB, phaseC):
                if step <= kstop:
                    if ph is phaseA:
                        ph(l)
                    elif ph is phaseB:
                        ph(l, not last)
                    else:
                        ph(l, last)
                step += 1
        P.emit(final_ops)
    return nc


def host_consts(T):
    t = np.arange(T)
    row = (t // 64).astype(np.float32)
    col = (t % 64).astype(np.float32)
    inv = (np.float32(10000.0) ** (-np.arange(0, 64, 2, dtype=np.float32) / np.float32(64))).astype(np.float32)
    ang_r = (row[None, :] * inv[:, None]).astype(np.float32)
    ang_c = (col[None, :] * inv[:, None]).astype(np.float32)
    cos_full = np.concatenate([np.cos(ang_r), np.cos(ang_r), np.cos(ang_c), np.cos(ang_c)], 0).astype(np.float32)
    sin_sgn = np.concatenate([-np.sin(ang_r), np.sin(ang_r), -np.sin(ang_c), np.sin(ang_c)], 0).astype(np.float32)
    rope = np.stack([cos_full, sin_sgn], 0)
    cb = np.zeros((128, CBFN), np.float32)
    for m in range(128):
        partner = m + 32 if (m % 64) < 32 else m - 32
        cb[partner, m] = 1.0
    cb[:, 128:256] = np.eye(128, dtype=np.float32)
    j = np.arange(128)[:, None]
    i = np.arange(128)[None, :]
    mp = np.where(i <= j, 0.0, -30000.0).astype(np.float32)
    mn = np.where(j <= i, 0.0, -30000.0).astype(np.float32)
    cb[:, 256:768] = np.tile(mp, (1, 4))
    cb[:, 768:1280] = np.tile(mn, (1, 4))
    return rope, cb.astype(ml_dtypes.bfloat16)


_NC_CACHE = {}


def kernel(x, c, ctx, c_ctx, ada_w, ada_b, norm_g, ffn1_w13, ffn1_w2, w_in, b_merge,
           rnn_conv_w, rnn_conv_b, lru_w_a, lru_b_a, lru_w_x, lru_b_x, lru_lambda,
           sc_conv_w, attn_sink, w_branch, w_out, ffn2_w13, ffn2_w2, final_norm_g):
    f32 = lambda a: np.ascontiguousarray(np.asarray(a), dtype=np.float32)
    x = f32(x)
    B, T, _ = x.shape
    ctx = f32(ctx)
    L = make_layout()
    rope, cb = host_consts(T)
    lru_w_a = f32(lru_w_a)
    lru_w_x = f32(lru_w_x)
    lw = np.stack([lru_w_a, lru_w_x], 1)
    lw = np.ascontiguousarray(lw.transpose(0, 4, 1, 2, 3, 5).reshape(NLAYER, 128, 32, 128))
    shared = {
        "ada_w": f32(ada_w), "ffn1_w13": f32(ffn1_w13), "ffn2_w13": f32(ffn2_w13), "ffn1_w2": f32(ffn1_w2),
        "ffn2_w2": f32(ffn2_w2), "w_in": f32(w_in), "w_branch": f32(w_branch), "w_out": f32(w_out),
        "lru_w": lw, "rope": rope, "cbf": cb,
    }
    base = np.zeros((128, L.n), np.float32)

    def put(name, arr):
        sl = L.sl(name)
        base[:, sl] = arr

    for l in range(NLAYER):
        put("ada_b%d" % l, fm(f32(ada_b)[l]))
        put("norm_g%d" % l, fm(f32(norm_g)[l].reshape(-1)))
        put("b_merge%d" % l, fm(f32(b_merge)[l].reshape(-1)))
        put("rconv_w%d" % l, fm(f32(rnn_conv_w)[l].reshape(-1)))
        put("rconv_b%d" % l, fm(f32(rnn_conv_b)[l]))
        put("lru_b_a%d" % l, fm(f32(lru_b_a)[l].reshape(-1)))
        put("lru_b_x%d" % l, fm(f32(lru_b_x)[l].reshape(-1)))
        put("lam%d" % l, fm(f32(lru_lambda)[l].reshape(-1)))
        put("sconv_w%d" % l, fm(f32(sc_conv_w)[l].reshape(-1)))
        put("sink%d" % l, np.tile(f32(attn_sink)[l][None, :], (128, 1)))
    put("fnorm_g", fm(f32(final_norm_g)))
    in_maps = []
    for b in range(B):
        v = base.copy()
        cv = np.stack([fm(f32(c)[b]), fm(f32(c_ctx))], -1).reshape(128, 32)
        v[:, L.sl("cvec")] = cv
        xt = np.ascontiguousarray(np.concatenate([x[b].T, ctx[b].T], axis=1))
        m = dict(shared)
        m["xT"] = xt
        m["vecs"] = v
        in_maps.append(m)
    if T not in _NC_CACHE:
        _NC_CACHE[T] = build(T)
    nc = _NC_CACHE[T]
    if B == 4:
        place = [0, 1, 4, 5]
        zero = {k: np.zeros_like(v) for k, v in in_maps[0].items()}
        maps8 = [zero] * 8
        maps8 = list(maps8)
        for b, c_ in enumerate(place):
            maps8[c_] = in_maps[b]
        res = run_bass_kernel_spmd(nc, maps8, core_ids=list(range(8)))
        outs = [res.results[c_]["outT"] for c_ in place]
    else:
        res = run_bass_kernel_spmd(nc, in_maps, core_ids=list(range(B)))
        outs = [r["outT"] for r in res.results]
    out = np.stack([np.ascontiguousarray(o.T) for o in outs], 0)
    return out.astype(np.float32)
```

```python
import numpy as np
import ml_dtypes
from contextlib import ExitStack
import concourse.bass as bass
import concourse.mybir as mybir
from concourse.bass_utils import run_bass_kernel_spmd

F32 = mybir.dt.float32
BF16 = mybir.dt.bfloat16
AF = mybir.ActivationFunctionType
ALU = mybir.AluOpType

D = 2048
DC = 16
DFF = 5504
FC = 43
NCTX = 256
NMOD = 9
INCOLS = 12800
NLAYER = 2
EPS = 1e-6
TT = 512
SB = 1024
ATT_SCALE = 128 ** -0.5
CBFN = 1280


class Buf:
    __slots__ = ("name", "t", "last_w", "readers", "sem")

    def __init__(self, name, t=None):
        self.name = name
        self.t = t
        self.last_w = None
        self.readers = []
        self.sem = None

    def __getitem__(self, idx):
        return self.t[idx]


class DmaSem:
    def __init__(self, sem):
        self.sem = sem
        self.count = 0
        self.last_op = None


class Op:
    __slots__ = ("eng", "fn", "deps", "needed", "cnt", "dsem", "dcnt")

    def __init__(self, eng, fn, deps):
        self.eng = eng
        self.fn = fn
        self.deps = deps
        self.needed = False
        self.cnt = 0
        self.dsem = None
        self.dcnt = 0


class Prog:
    ENGS = ("pe", "act", "dve", "pool", "sp")

    def __init__(self, nc, es):
        self.nc = nc
        self.es = es
        self.ops = {e: [] for e in self.ENGS}
        self.last = {e: None for e in self.ENGS}
        self.fence = {e: [] for e in self.ENGS}
        self.dsems = []
        self.banks = []
        self.bank_i = 0
        self.esem = {e: es.enter_context(nc.semaphore("s_" + e)) for e in ("pe", "act", "dve", "pool")}
        self.root = es
        self.pool_free = {}
        self.phase_bufs = None
        self.nsem = 0

    def sbuf(self, name, shape, dt, dma=False):
        self.nsb = getattr(self, "nsb", 0) + 1
        t = self.es.enter_context(self.nc.sbuf_tensor("sb%d_%s" % (self.nsb, name), shape, dt))
        b = Buf(name, t)
        if dma:
            self.add_dsem(b)
        return b

    def add_dsem(self, b):
        b.sem = {}
        if self.phase_bufs is not None:
            self.phase_bufs.append(b)

    def get_dsem(self, b, q):
        if q not in b.sem:
            free = self.pool_free.setdefault(q, [])
            if free:
                ds = free.pop()
            else:
                ds = DmaSem(self.root.enter_context(self.nc.semaphore("dsem%d" % self.nsem)))
                self.nsem += 1
                self.dsems.append(ds)
            b.sem[q] = ds
        return b.sem[q]

    def phase(self):
        prog = self

        class _Ph:
            def __enter__(self_):
                self_.st = ExitStack()
                self_.prev = prog.es
                self_.prev_bufs = prog.phase_bufs
                prog.es = self_.st
                prog.phase_bufs = []
                return self_

            def __exit__(self_, *a):
                prog.barrier()
                for b in prog.phase_bufs:
                    for q_, ds_ in b.sem.items():
                        prog.pool_free.setdefault(q_, []).append(ds_)
                prog.phase_bufs = self_.prev_bufs
                prog.es = self_.prev
                self_.st.close()
                return False
        return _Ph()

    def make_banks(self):
        for i in range(8):
            t = self.es.enter_context(self.nc.psum_tensor("bank%d" % i, [128, 512], F32))
            self.banks.append(Buf("bank%d" % i, t))

    def bank(self):
        b = self.banks[self.bank_i % 8]
        self.bank_i += 1
        return b

    def _deps(self, eng, reads, writes):
        deps = []
        for r in reads:
            if r.last_w is not None:
                deps.append(r.last_w)
        for w in writes:
            if w.last_w is not None:
                deps.append(w.last_w)
            deps.extend(w.readers)
        deps.extend(self.fence[eng])
        self.fence[eng] = []
        out = []
        seen = set()
        for d in deps:
            if id(d) in seen:
                continue
            seen.add(id(d))
            if d.eng == "pe" and eng == "pe" and d.dsem is None:
                continue
            out.append(d)
        return out

    def op(self, eng, fn, reads=(), writes=()):
        o = Op(eng, fn, self._deps(eng, reads, writes))
        for d in o.deps:
            d.needed = True
        self.ops[eng].append(o)
        self.last[eng] = o
        for r in reads:
            self._add_reader(r, o)
        for w in writes:
            w.last_w = o
            w.readers = []
        return o

    @staticmethod
    def _add_reader(buf, o):
        key = o.eng if o.dsem is None else id(o.dsem)
        rl = buf.readers
        for i, x in enumerate(rl):
            kx = x.eng if x.dsem is None else id(x.dsem)
            if kx == key:
                rl[i] = o
                return
        rl.append(o)

    def dma(self, q, out, in_, sbuf_side, reads=(), writes=()):
        ds = self.get_dsem(sbuf_side, q)
        o = Op(q, None, self._deps(q, reads, writes))
        if ds.last_op is not None and all(ds.last_op is not d for d in o.deps):
            o.deps.append(ds.last_op)
        for d in o.deps:
            d.needed = True
        ds.count += 1
        o.dsem = ds
        o.dcnt = ds.count
        ds.last_op = o
        o.fn = lambda e, out=out, in_=in_: e.dma_start(out=out, in_=in_)
        self.ops[q].append(o)
        self.last[q] = o
        for r in reads:
            self._add_reader(r, o)
        for w in writes:
            w.last_w = o
            w.readers = []
        return o

    def barrier(self):
        lasts = [self.last[e] for e in self.ENGS if self.last[e] is not None]
        for ds in self.dsems:
            if ds.last_op is not None:
                lasts.append(ds.last_op)
        for e in self.ENGS:
            self.fence[e] = list(lasts)

    def mm(self, out, lhsT, rhs, start, stop, reads, writes):
        return self.op("pe", lambda e: e.matmul(out, lhsT, rhs, start=start, stop=stop), reads, writes)

    def act(self, out, in_, func, reads, writes, bias=None, scale=None, eng="act"):
        kw = {}
        if bias is not None:
            kw["bias"] = bias
        if scale is not None:
            kw["scale"] = scale
        return self.op(eng, lambda e: e.activation(out=out, in_=in_, func=func, **kw), reads, writes)

    def tt(self, eng, out, in0, in1, op, reads, writes):
        return self.op(eng, lambda e: e.tensor_tensor(out=out, in0=in0, in1=in1, op=op), reads, writes)

    def ts(self, eng, out, in0, s1, s2, op0, op1, reads, writes):
        if s2 is None:
            return self.op(eng, lambda e: e.tensor_single_scalar(out=out, in_=in0, scalar=s1, op=op0), reads, writes)
        return self.op(eng, lambda e: e.tensor_scalar(out=out, in0=in0, scalar1=s1, scalar2=s2, op0=op0, op1=op1),
                       reads, writes)

    def stt(self, eng, out, in0, scalar, in1, op0, op1, reads, writes):
        return self.op(eng, lambda e: e.scalar_tensor_tensor(out=out, in0=in0, scalar=scalar, in1=in1, op0=op0, op1=op1),
                       reads, writes)

    def copy(self, eng, out, in_, reads, writes):
        if eng == "act":
            return self.op(eng, lambda e: e.activation(out=out, in_=in_, func=AF.Identity), reads, writes)
        return self.op(eng, lambda e: e.tensor_copy(out=out, in_=in_), reads, writes)

    def memset(self, eng, ap, val, writes):
        return self.op(eng, lambda e: e.memset(ap, val), (), writes)

    def emit(self, final_ops):
        nc = self.nc
        for e in ("pe", "act", "dve", "pool"):
            c = 0
            for o in self.ops[e]:
                if o.dsem is None and o.needed:
                    c += 1
                    o.cnt = c
                elif o.dsem is None:
                    o.cnt = -1
        for o in final_ops:
            assert o.dsem is not None
        handles = {"pe": None, "act": None, "dve": None, "pool": None, "sp": None}
        prog = self

        def run(eng_name, eng):
            waited = {}
            for o in prog.ops[eng_name]:
                for d in o.deps:
                    if d.dsem is not None:
                        key = id(d.dsem)
                        val = 16 * d.dcnt
                        sem = d.dsem.sem
                    else:
                        key = d.eng
                        val = d.cnt
                        sem = prog.esem[d.eng]
                        assert val > 0
                    if waited.get(key, 0) < val:
                        eng.wait_ge(sem, val)
                        waited[key] = val
                ins = o.fn(eng)
                if o.dsem is not None:
                    ins.then_inc(o.dsem.sem, 16)
                elif o.needed:
                    ins.then_inc(prog.esem[eng_name], 1)
            if eng_name == "pool":
                for o in final_ops:
                    eng.wait_ge(o.dsem.sem, 16 * o.dcnt)

        with nc.Block() as block:
            @block.tensor
            def _(e):
                run("pe", e)

            @block.scalar
            def _(e):
                run("act", e)

            @block.vector
            def _(e):
                run("dve", e)

            @block.gpsimd
            def _(e):
                run("pool", e)

            @block.sync
            def _(e):
                run("sp", e)


class VecLayout:
    def __init__(self):
        self.off = {}
        self.n = 0

    def add(self, name, ncols):
        self.off[name] = (self.n, ncols)
        self.n += ncols

    def sl(self, name, i=0, n=None):
        o, c = self.off[name]
        if n is None:
            n = c - i
        return slice(o + i, o + i + n)


def make_layout():
    L = VecLayout()
    L.add("cvec", 32)
    for l in range(NLAYER):
        L.add("ada_b%d" % l, 144)
        L.add("norm_g%d" % l, 48)
        L.add("b_merge%d" % l, 48)
        L.add("rconv_w%d" % l, 32)
        L.add("rconv_b%d" % l, 8)
        L.add("lru_b_a%d" % l, 16)
        L.add("lru_b_x%d" % l, 16)
        L.add("lam%d" % l, 16)
        L.add("sconv_w%d" % l, 24)
        L.add("sink%d" % l, 8)
    L.add("fnorm_g", 16)
    return L


def fm(v):
    v = np.asarray(v, np.float32)
    return np.ascontiguousarray(v.reshape(-1, 128).T)


def build(T):
    TTOT = T + NCTX
    nc = bass.Bass("TRN2", target_bir_lowering=False)
    L = make_layout()

    def din(name, shape, dt=F32):
        return nc.dram_tensor(name, list(shape), dt, kind="ExternalInput").ap()

    def dscr(name, shape, dt):
        return nc.dram_tensor(name, list(shape), dt, kind="Internal").ap()

    xT = din("xT", [D, TTOT])
    vecs_d = din("vecs", [128, L.n])
    ada_w = din("ada_w", [NLAYER, D, NMOD * D])
    w13_d = [din("ffn1_w13", [NLAYER, D, 2 * DFF]), din("ffn2_w13", [NLAYER, D, 2 * DFF])]
    w2_d = [din("ffn1_w2", [NLAYER, DFF, D]), din("ffn2_w2", [NLAYER, DFF, D])]
    w_in_d = din("w_in", [NLAYER, D, INCOLS])
    wbr_d = din("w_branch", [NLAYER, 3, 1024, D])
    wout_d = din("w_out", [NLAYER, D, D])
    lruw_d = din("lru_w", [NLAYER, 128, 32, 128])
    rope_d = din("rope", [2, 128, T])
    cbf_d = din("cbf", [128, CBFN], BF16)
    outT = nc.dram_tensor("outT", [D, T], F32, kind="ExternalOutput").ap()

    hbuf = dscr("hbuf", [D, TTOT], F32)
    zr = dscr("zr", [5 * 1024, TTOT], F32)
    zq = dscr("zq", [10 * 128, TTOT], BF16)
    zv = dscr("zv", [TTOT, 256], BF16)
    zg = dscr("zg", [3 * D, TTOT], BF16)
    hfb = dscr("hfb", [1024, TTOT], F32)
    ybuf = dscr("ybuf", [3 * 1024, TTOT], BF16)
    pw13 = [[dscr("pw13_%d_%d" % (f, l), [FC, 128, DC, 256], BF16) for l in range(NLAYER)] for f in range(2)]
    pw2 = [[dscr("pw2_%d_%d" % (f, l), [DC, 128, FC, 128], BF16) for l in range(NLAYER)] for f in range(2)]
    pwin = [dscr("pwin_%d" % l, [50, 128, DC, 256], BF16) for l in range(NLAYER)]
    pwbr = [[dscr("pwbr_%d_%d" % (l, i), [8, 128, 8, 256], BF16) for i in range(3)] for l in range(NLAYER)]
    pwout = [dscr("pwout_%d" % l, [8, 128, DC, 256], BF16) for l in range(NLAYER)]

    xTv = xT.rearrange("(c p) t -> p c t", p=128)
    hbv = hbuf.rearrange("(c p) t -> p c t", p=128)
    zrv = zr.rearrange("(c p) t -> p c t", p=128)
    zqv = zq.rearrange("(c p) t -> p c t", p=128)
    zgv = zg.rearrange("(c p) t -> p c t", p=128)
    zg4 = zg.rearrange("(i c p) t -> p i c t", i=3, p=128)
    ybv = ybuf.rearrange("(c p) t -> p c t", p=128)
    outv = outT.rearrange("(c p) t -> p c t", p=128)

    with ExitStack() as es:
        P = Prog(nc, es)
        P.make_banks()
        d_w = Buf("d_weights")
        d_in = Buf("d_inputs")
        d_h = {}
        d_z = Buf("d_z")
        d_y = Buf("d_y")
        d_hf = Buf("d_hf")
        d_out = Buf("d_out")

        def dh(key):
            if key not in d_h:
                d_h[key] = Buf("d_h%s" % (key,))
            return d_h[key]

        vecs = P.sbuf("vecs", [128, L.n], F32, dma=True)
        DVN = NLAYER * 2 * (144 + 9 * 16) + NLAYER * 32 + 64
        dv = P.sbuf("dv", [128, DVN], F32)
        cbf = P.sbuf("cbf", [128, CBFN], BF16, dma=True)
        ones_f = P.sbuf("ones_f", [128, 128], F32)
        ones_b = P.sbuf("ones_b", [128, 128], BF16)
        perm_ap = cbf[:, 0:128]
        ident_ap = cbf[:, 128:256]
        maskp_ap = cbf[:, 256:768]
        maskn_ap = cbf[:, 768:1280]

        dvo = {}
        dvn = [0]

        def dvadd(name, n):
            dvo[name] = (dvn[0], n)
            dvn[0] += n

        for l in range(NLAYER):
            for s in range(2):
                dvadd("mod%d%d" % (l, s), 144)
                for nm in ("A1", "B1", "G1", "A2", "B2", "G2", "A3", "B3", "G3"):
                    dvadd("%s%d%d" % (nm, l, s), 16)
            dvadd("lc%d" % l, 16)
            dvadd("lc2%d" % l, 16)
        dvadd("sc", 32)
        dvadd("tmp", 32)
        assert dvn[0] <= DVN

        def DV(name, i=0, n=None):
            o, c = dvo[name]
            if n is None:
                n = c - i
            return dv[:, o + i:o + i + n]

        def VC(name, i=0, n=None):
            return vecs[:, L.sl(name, i, n)]

        P.dma("sp", vecs[:, :], vecs_d, vecs, reads=[d_in], writes=[vecs])
        P.dma("sp", cbf[:, :], cbf_d, cbf, reads=[d_in], writes=[cbf])
        P.memset("dve", ones_f[:, :], 1.0, [ones_f])
        P.memset("dve", ones_b[:, :], 1.0, [ones_b])

        import os
        kstop0 = int(os.environ.get("KSTOP", "99"))
        import os
        kstop0 = int(os.environ.get("KSTOP", "99"))
        with P.phase():
            STG = 11008
            stg = [P.sbuf("stg%d" % i, [128, STG], F32, dma=True) for i in range(2)]
            pbf = [P.sbuf("pbf%d" % i, [128, STG], BF16, dma=True) for i in range(2)]
            cnt = [0]
            ceng = ["act", "dve", "pool"]
            cei = [0]
            groups = []

            def cast(out, in_, sbuf_in, sbuf_out):
                e = ceng[cei[0] % 3]
                cei[0] += 1
                P.copy(e, out, in_, [sbuf_in], [sbuf_out])

            def prep_generic(W, kcs, npan, dest, G):
                Wv = W.rearrange("(k p) n -> p k n", p=128)
                for q0 in range(0, npan, G):
                    def grp(q0=q0):
                        g = min(G, npan - q0)
                        s_ = stg[cnt[0] % 2]
                        b_ = pbf[cnt[0] % 2]
                        cnt[0] += 1
                        sv = s_[:, 0:kcs * g * 256].rearrange("p (k n) -> p k n", k=kcs)
                        P.dma("sp", sv, Wv[:, :, q0 * 256:(q0 + g) * 256], s_, reads=[d_in], writes=[s_])
                        bv = b_[:, 0:g * kcs * 256].rearrange("p (q k n) -> p q k n", q=g, k=kcs)
                        for q in range(g):
                            cast(bv[:, q, :, :], sv[:, :, q * 256:(q + 1) * 256], s_, b_)
                        P.dma("pool", dest[q0:q0 + g].rearrange("q p k n -> p q (k n)"),
                              b_[:, 0:g * kcs * 256].rearrange("p (q m) -> p q m", q=g), b_, reads=[b_], writes=[d_w])
                    groups.append(grp)

            def prep_w13(W, dest):
                Wv = W.rearrange("(k p) n -> p k n", p=128)
                G = 2
                for j0 in range(0, FC, G):
                    def grp(j0=j0):
                        g = min(G, FC - j0)
                        s_ = stg[cnt[0] % 2]
                        b_ = pbf[cnt[0] % 2]
                        cnt[0] += 1
                        sv = s_[:, 0:DC * 2 * g * 128].rearrange("p (k s n) -> p k s n", k=DC, s=2)
                        P.dma("sp", sv[:, :, 0, :], Wv[:, :, j0 * 128:(j0 + g) * 128], s_, reads=[d_in], writes=[s_])
                        P.dma("sp", sv[:, :, 1, :], Wv[:, :, DFF + j0 * 128:DFF + (j0 + g) * 128], s_, reads=[d_in], writes=[s_])
                        bv = b_[:, 0:g * DC * 256].rearrange("p (q k s n) -> p q k s n", q=g, k=DC, s=2)
                        for q in range(g):
                            for s2 in range(2):
                                cast(bv[:, q, :, s2, :], sv[:, :, s2, q * 128:(q + 1) * 128], s_, b_)
                        P.dma("pool", dest[j0:j0 + g].rearrange("q p k n -> p q (k n)"),
                              b_[:, 0:g * DC * 256].rearrange("p (q m) -> p q m", q=g), b_, reads=[b_], writes=[d_w])
                    groups.append(grp)

            def prep_w2(W, dest):
                Wv = W.rearrange("(k p) n -> p k n", p=128)
                for c0 in range(0, DC, 2):
                    def grp(c0=c0):
                        s_ = stg[cnt[0] % 2]
                        b_ = pbf[cnt[0] % 2]
                        cnt[0] += 1
                        sv = s_[:, 0:FC * 256].rearrange("p (k n) -> p k n", k=FC)
                        P.dma("sp", sv, Wv[:, :, c0 * 128:(c0 + 2) * 128], s_, reads=[d_in], writes=[s_])
                        bv = b_[:, 0:2 * FC * 128].rearrange("p (q k n) -> p q k n", q=2, k=FC)
                        for q in range(2):
                            cast(bv[:, q, :, :], sv[:, :, q * 128:(q + 1) * 128], s_, b_)
                        P.dma("pool", dest[c0:c0 + 2].rearrange("q p k n -> p q (k n)"),
                              b_[:, 0:2 * FC * 128].rearrange("p (q m) -> p q m", q=2), b_, reads=[b_], writes=[d_w])
                    groups.append(grp)

            for l in range(NLAYER if kstop0 >= 1 else 0):
                for f in range(2):
                    prep_w13(w13_d[f][l], pw13[f][l])
                    prep_w2(w2_d[f][l], pw2[f][l])
                prep_generic(w_in_d[l], DC, 50, pwin[l], 2)
                for i in range(3):
                    prep_generic(wbr_d[l, i], 8, 8, pwbr[l][i], 4)
                prep_generic(wout_d[l], DC, 8, pwout[l], 2)
            NAP = 72 * NLAYER
            per = (len(groups) + NAP - 1) // NAP
            gpos = [0]

            def emit_groups(k):
                for _ in range(k):
                    if gpos[0] < len(groups):
                        groups[gpos[0]]()
                        gpos[0] += 1

            apan = [P.sbuf("apan%d" % i, [128, DC, 256], F32, dma=True) for i in range(2)]
            P.act(DV("sc"), VC("cvec"), AF.Silu, [vecs], [dv])
            sc3 = DV("sc").rearrange("p (k s) -> p k s", s=2)
            pi = 0
            for l in range(NLAYER):
                bk = P.bank()
                bk3 = bk[:, 0:288].rearrange("p (j s) -> p j s", s=2)
                for pq in range(72):
                    pan = apan[pi % 2]
                    pi += 1
                    src = ada_w[l].rearrange("(k p) n -> p k n", p=128)[:, :, pq * 256:(pq + 1) * 256]
                    P.dma("sp", pan[:, :, :], src, pan, reads=[d_in], writes=[pan])
                    for jj in range(2):
                        j = pq * 2 + jj
                        for kc in range(DC):
                            P.mm(bk3[:, j, :], pan[:, kc, jj * 128:(jj + 1) * 128], sc3[:, kc, :],
                                 kc == 0, kc == DC - 1, [pan, dv], [bk])
                    emit_groups(per)
                for s in range(2):
                    P.tt("dve", DV("mod%d%d" % (l, s)), bk3[:, :, s], VC("ada_b%d" % l), ALU.add, [bk, vecs], [dv])
                    md = lambda m, l=l, s=s: DV("mod%d%d" % (l, s), m * 16, 16)
                    ng = lambda i, l=l: VC("norm_g%d" % l, i * 16, 16)
                    for (nm, gi, sh, scl, gt, gmul) in (("1", 0, 0, 1, 2, 0.5), ("2", 1, 3, 4, 5, 1.0), ("3", 2, 6, 7, 8, 0.5)):
                        P.stt("dve", DV("A%s%d%d" % (nm, l, s)), md(scl), 1.0, ng(gi), ALU.add, ALU.mult, [dv, vecs], [dv])
                        P.copy("dve", DV("B%s%d%d" % (nm, l, s)), md(sh), [dv], [dv])
                        P.ts("dve", DV("G%s%d%d" % (nm, l, s)), md(gt), gmul, None, ALU.mult, None, [dv], [dv])
                P.act(DV("tmp", 0, 16), VC("lam%d" % l), AF.Exp, [vecs], [dv], scale=-1.0)
                P.act(DV("tmp", 16, 16), DV("tmp", 0, 16), AF.Ln, [dv], [dv], bias=1.0)
                P.ts("dve", DV("lc%d" % l), DV("tmp", 16, 16), -8.0, None, ALU.mult, None, [dv], [dv])
                P.ts("dve", DV("lc2%d" % l), DV("tmp", 16, 16), -16.0, None, ALU.mult, None, [dv], [dv])
            emit_groups(len(groups))

        def tiles(with_ctx=True):
            ts_ = [(t0, TT, 0) for t0 in range(0, T, TT)]
            if with_ctx:
                ts_.append((T, NCTX, 1))
            return ts_

        class TileCtx:
            pass

        def alloc_tile_bufs():
            C = TileCtx()
            C.htile = P.sbuf("htile", [128, DC, TT], F32, dma=True)
            C.ub = [P.sbuf("ub%d" % i, [128, TT], BF16) for i in range(DC)]
            C.big = P.sbuf("big", [128, FC * TT], BF16, dma=True)
            C.ring8 = [P.sbuf("ring8_%d" % i, [128, DC * 256], BF16, dma=True) for i in range(3)]
            C.ring11 = [P.sbuf("ring11_%d" % i, [128, FC * 128], BF16, dma=True) for i in range(2)]
            C.r8i = 0
            C.r11i = 0
            C.tmpf = [P.sbuf("tmpf%d" % i, [128, TT], F32) for i in range(4)]
            C.tfi = 0
            C.rstd = P.sbuf("rstd", [128, TT], F32)
            C.tmpb = [P.sbuf("tmpb%d" % i, [128, TT], BF16) for i in range(4)]
            C.tbi = 0
            return C

        def tmp(C):
            b = C.tmpf[C.tfi % 4]
            C.tfi += 1
            return b

        def next8(C):
            b = C.ring8[C.r8i % 3]
            C.r8i += 1
            return b

        def next11(C):
            b = C.ring11[C.r11i % 2]
            C.r11i += 1
            return b

        def rms_stats(C, n):
            htile = C.htile
            bk = P.bank()
            for c in range(DC):
                sq = C.tmpb[C.tbi % 4]
                C.tbi += 1
                if c % 2 == 0:
                    P.act(sq[:, :n], htile[:, c, :n], AF.Square, [htile], [sq])
                else:
                    P.tt("dve", sq[:, :n], htile[:, c, :n], htile[:, c, :n], ALU.mult, [htile], [sq])
                P.mm(bk[:, :n], ones_b[:, :], sq[:, :n], c == 0, c == DC - 1, [ones_b, sq], [bk])
            P.act(C.rstd[:, :n], bk[:, :n], AF.Sqrt, [bk], [C.rstd], scale=1.0 / D, bias=EPS)
            P.op("dve", lambda e: e.reciprocal(out=C.rstd[:, :n], in_=C.rstd[:, :n]), [C.rstd], [C.rstd])

        def rmsnorm_mod(C, n, A, B):
            rms_stats(C, n)
            for c in range(DC):
                t_ = tmp(C)
                P.stt("dve", t_[:, :n], C.htile[:, c, :n], A[:, c:c + 1], C.rstd[:, :n], ALU.mult, ALU.mult,
                      [C.htile, C.rstd, dv, vecs], [t_])
                P.act(C.ub[c][:, :n], t_[:, :n], AF.Identity, [t_, dv], [C.ub[c]], bias=B[:, c:c + 1])

        def ffn(C, f, l, n, G):
            htile, ub, big = C.htile, C.ub, C.big
            actv = big[:, 0:FC * TT].rearrange("p (j t) -> p j t", j=FC)
            for j in range(FC):
                slot = next8(C)
                sv = slot[:, :].rearrange("p (k n) -> p k n", k=DC)
                P.dma("sp", slot[:, :], pw13[f][l][j].rearrange("p k n -> p (k n)"), slot, reads=[d_w], writes=[slot])
                bg = P.bank()
                bu = P.bank()
                for kc in range(DC):
                    P.mm(bg[:, :n], sv[:, kc, 0:128], ub[kc][:, :n], kc == 0, kc == DC - 1, [slot, ub[kc]], [bg])
                for kc in range(DC):
                    P.mm(bu[:, :n], sv[:, kc, 128:256], ub[kc][:, :n], kc == 0, kc == DC - 1, [slot, ub[kc]], [bu])
                sg = tmp(C)
                P.act(sg[:, :n], bg[:, :n], AF.Silu, [bg], [sg])
                P.tt("dve", actv[:, j, :n], sg[:, :n], bu[:, :n], ALU.mult, [sg, bu], [big])
            for c in range(DC):
                slot = next11(C)
                sv = slot[:, :].rearrange("p (k n) -> p k n", k=FC)
                P.dma("sp", slot[:, :], pw2[f][l][c].rearrange("p k n -> p (k n)"), slot, reads=[d_w], writes=[slot])
                bk = P.bank()
                for j in range(FC):
                    P.mm(bk[:, :n], sv[:, j, :], actv[:, j, :n], j == 0, j == FC - 1, [slot, big], [bk])
                P.stt("dve", htile[:, c, :n], bk[:, :n], G[:, c:c + 1], htile[:, c, :n], ALU.mult, ALU.add,
                      [bk, htile, dv], [htile])

        def phaseA(l):
            with P.phase():
                C = alloc_tile_bufs()
                htile, ub = C.htile, C.ub
                ropets = [P.sbuf("ropet%d" % i, [128, 2, TT], F32, dma=True) for i in range(2)]
                qbf = [P.sbuf("qbf%d" % i, [128, TT], BF16) for i in range(2)]
                stf = [P.sbuf("stf%d" % i, [128, 2, TT], F32, dma=True) for i in range(3)]
                stb = [P.sbuf("stb%d" % i, [128, 2, TT], BF16, dma=True) for i in range(3)]
                stgv = P.sbuf("stgv", [128, 4, 256], BF16, dma=True)
                qbi = 0
                sfi = 0
                sbi = 0
                tl = tiles()
                src = xTv if l == 0 else hbv

                def load_h(ti):
                    (t0_, n_, s_) = tl[ti]
                    P.dma("sp", htile[:, :, :n_], src[:, :, t0_:t0_ + n_], htile,
                          reads=[d_in if l == 0 else dh((l, "c", t0_))], writes=[htile])
                load_h(0)
                for ti, (t0, n, s) in enumerate(tl):
                    ropet = ropets[ti % 2]
                    if not s:
                        P.dma("sp", ropet[:, :, :n], rope_d.rearrange("s p t -> p s t")[:, :, t0:t0 + n], ropet,
                              reads=[d_in], writes=[ropet])
                    rmsnorm_mod(C, n, DV("A1%d%d" % (l, s)), DV("B1%d%d" % (l, s)))
                    ksub = int(os.environ.get("KSUB", "99"))
                    if ksub >= 2:
                        ffn(C, 0, l, n, DV("G1%d%d" % (l, s)))
                    P.dma(os.environ.get("KSTQ", "pool"), hbv[:, :, t0:t0 + n], htile[:, :, :n], htile, reads=[htile], writes=[dh((l, "a", t0))])
                    if ksub < 3:
                        if ti + 1 < len(tl):
                            load_h(ti + 1)
                        continue
                    rmsnorm_mod(C, n, DV("A2%d%d" % (l, s)), DV("B2%d%d" % (l, s)))
                    if ti + 1 < len(tl):
                        load_h(ti + 1)
                    for q in range({3: 20, 4: 25, 5: 26}.get(ksub, 50)):
                        slot = next8(C)
                        sv = slot[:, :].rearrange("p (k n) -> p k n", k=DC)
                        P.dma("sp", slot[:, :], pwin[l][q].rearrange("p k n -> p (k n)"), slot, reads=[d_w], writes=[slot])
                        if q == 25:
                            for sb_ in range(n // 128):
                                bk = P.bank()
                                for kc in range(DC):
                                    P.mm(bk[:, 0:256], ub[kc][:, sb_ * 128:(sb_ + 1) * 128], sv[:, kc, :],
                                         kc == 0, kc == DC - 1, [slot, ub[kc]], [bk])
                                P.copy("act", stgv[:, sb_, :], bk[:, 0:256], [bk], [stgv])
                            P.dma("pool", zv[t0:t0 + n, :].rearrange("(b p) f -> p b f", p=128), stgv[:, 0:n // 128, :],
                                  stgv, reads=[stgv], writes=[d_z])
                            continue
                        if q < 20:
                            st_ = stf[sfi % 3]
                            sfi += 1
                        else:
                            st_ = stb[sbi % 3]
                            sbi += 1
                        for half in range(2):
                            ch = 2 * q + half
                            bk = P.bank()
                            for kc in range(DC):
                                P.mm(bk[:, :n], sv[:, kc, half * 128:(half + 1) * 128], ub[kc][:, :n],
                                     kc == 0, kc == DC - 1, [slot, ub[kc]], [bk])
                            if ch < 40:
                                P.copy("dve" if half == 0 else "act", st_[:, half, :n], bk[:, :n], [bk], [st_])
                            elif ch < 50:
                                if s:
                                    P.copy("act", st_[:, half, :n], bk[:, :n], [bk], [st_])
                                else:
                                    qb = qbf[qbi % 2]
                                    qbi += 1
                                    P.copy("act", qb[:, :n], bk[:, :n], [bk], [qb, bk])
                                    bk2 = P.bank()
                                    P.mm(bk2[:, :n], perm_ap, qb[:, :n], True, True, [cbf, qb], [bk2])
                                    t1 = tmp(C)
                                    t2 = tmp(C)
                                    P.tt("dve", t1[:, :n], bk[:, :n], ropet[:, 0, :n], ALU.mult, [bk, ropet], [t1])
                                    P.tt("dve", t2[:, :n], bk2[:, :n], ropet[:, 1, :n], ALU.mult, [bk2, ropet], [t2])
                                    P.tt("dve", st_[:, half, :n], t1[:, :n], t2[:, :n], ALU.add, [t1, t2], [st_])
                            else:
                                gi = ch - 52
                                P.act(st_[:, half, :n], bk[:, :n], AF.Sigmoid, [bk, vecs], [st_],
                                      bias=VC("b_merge%d" % l, gi, 1))
                        c0 = 2 * q
                        if c0 < 40:
                            P.dma("pool", zrv[:, c0:c0 + 2, t0:t0 + n], st_[:, :, :n], st_, reads=[st_], writes=[d_z])
                        elif c0 < 50:
                            P.dma("pool", zqv[:, c0 - 40:c0 - 38, t0:t0 + n], st_[:, :, :n], st_, reads=[st_], writes=[d_z])
                        else:
                            P.dma("pool", zgv[:, c0 - 52:c0 - 50, t0:t0 + n], st_[:, :, :n], st_, reads=[st_], writes=[d_z])

        def phaseB(l, want_ctx):
            with P.phase():
                lw = P.sbuf("lw", [128, 32, 128], BF16, dma=True)
                P.dma("pool", lw[:, :, :], lruw_d[l], lw, reads=[d_in], writes=[lw])
                st = P.sbuf("st", [128, 8, 2], F32)
                zero1 = P.sbuf("zero1", [128, 1], F32)
                P.memset("dve", zero1[:, :], 0.0, [zero1])
                carry = P.sbuf("carry", [128, 1], F32)
                xaf = P.sbuf("xaf", [128, T], F32)
                xabf = P.sbuf("xabf", [128, T], BF16)
                xac = P.sbuf("xac", [128, NCTX], F32)
                xacb = P.sbuf("xacb", [128, NCTX], BF16)
                rxh = [P.sbuf("rxh%d" % i, [128, SB + 3], F32, dma=True) for i in range(2)]
                rb = P.sbuf("rb", [128, SB], F32)
                igb = [P.sbuf("igb%d" % i, [128, SB], F32) for i in range(2)]
                ab = [P.sbuf("ab%d" % i, [128, SB], F32) for i in range(2)]
                a2b = [P.sbuf("a2b%d" % i, [128, SB], F32) for i in range(2)]
                ub = P.sbuf("ub", [128, SB], F32)
                hb = [P.sbuf("hb%d" % i, [128, SB], F32, dma=True) for i in range(2)]
                hfl = [P.sbuf("hfl%d" % i, [128, SB], F32, dma=True) for i in range(2)]
                rgl = [P.sbuf("rgl%d" % i, [128, SB], F32, dma=True) for i in range(2)]
                yab = [P.sbuf("yab%d" % i, [128, SB], BF16, dma=True) for i in range(2)]
                scgl = [P.sbuf("scgl%d" % i, [128, SB + 2], F32, dma=True) for i in range(2)]
                sxl = [P.sbuf("sxl%d" % i, [128, SB + 2], F32, dma=True) for i in range(2)]
                sbl = [P.sbuf("sbl%d" % i, [128, SB], F32, dma=True) for i in range(2)]
                pbs = [P.sbuf("pbs%d" % i, [128, SB + 2], F32) for i in range(2)]
                yts = [[P.sbuf("yts%d_%d" % (i, k), [128, SB], F32) for k in range(3)] for i in range(2)]
                ybo = [P.sbuf("ybo%d" % i, [128, SB], BF16, dma=True) for i in range(2)]
                cnt = {"rx": 0, "g": 0, "h": 0, "c": 0, "y": 0}

                def halo_load(buf, grp, n, base, length, s0, Lb, left, right):
                    lo = max(0, s0 - left)
                    hi = min(length, s0 + Lb + right)
                    o0 = s0 - left
                    if lo > o0:
                        P.memset("dve", buf[:, 0:lo - o0], 0.0, [buf])
                    if hi < s0 + Lb + right:
                        P.memset("dve", buf[:, hi - o0:Lb + left + right], 0.0, [buf])
                    row = grp * 8 + n
                    P.dma("sp", buf[:, lo - o0:hi - o0], zrv[:, row, base + lo:base + hi], buf, reads=[d_z], writes=[buf])

                def conv_block(n, base, length, s0, Lb, xa_t, xab_t, c0):
                    rx_ = rxh[cnt["rx"] % 2]
                    cnt["rx"] += 1
                    halo_load(rx_, 0, n, base, length, s0, Lb, 2, 1)
                    wv = lambda k: VC("rconv_w%d" % l, k * 8 + n, 1)
                    P.ts("dve", xa_t[:, c0:c0 + Lb], rx_[:, 0:Lb], wv(0), VC("rconv_b%d" % l, n, 1), ALU.mult, ALU.add,
                         [rx_, vecs], [xa_t])
                    for k in range(1, 4):
                        P.stt("dve", xa_t[:, c0:c0 + Lb], rx_[:, k:k + Lb], wv(k), xa_t[:, c0:c0 + Lb], ALU.mult, ALU.add,
                              [rx_, xa_t, vecs], [xa_t])
                    P.copy("dve", xab_t[:, c0:c0 + Lb], xa_t[:, c0:c0 + Lb], [xa_t], [xab_t])

                def gate_acts(n, d, Lb, xab_t, c0):
                    gi = cnt["g"] % 2
                    cnt["g"] += 1
                    ig_, a_, a2_ = igb[gi], ab[gi], a2b[gi]
                    for s_ in range(0, Lb, 512):
                        ns = min(512, Lb - s_)
                        ba = P.bank()
                        bx = P.bank()
                        P.mm(ba[:, :ns], lw[:, (0 * 2 + d) * 8 + n, :], xab_t[:, c0 + s_:c0 + s_ + ns], True, True, [lw, xab_t], [ba])
                        P.mm(bx[:, :ns], lw[:, (1 * 2 + d) * 8 + n, :], xab_t[:, c0 + s_:c0 + s_ + ns], True, True, [lw, xab_t], [bx])
                        P.act(rb[:, s_:s_ + ns], ba[:, :ns], AF.Sigmoid, [ba, vecs], [rb],
                              bias=VC("lru_b_a%d" % l, d * 8 + n, 1))
                        P.act(ig_[:, s_:s_ + ns], bx[:, :ns], AF.Sigmoid, [bx, vecs], [ig_],
                              bias=VC("lru_b_x%d" % l, d * 8 + n, 1))
                    P.act(a_[:, :Lb], rb[:, :Lb], AF.Exp, [rb, dv], [a_], scale=DV("lc%d" % l, d * 8 + n, 1))
                    P.act(a2_[:, :Lb], rb[:, :Lb], AF.Exp, [rb, dv], [a2_], scale=DV("lc2%d" % l, d * 8 + n, 1))
                    P.act(a2_[:, :Lb], a2_[:, :Lb], AF.Relu, [a2_], [a2_], scale=-1.0, bias=1.0)
                    P.act(a2_[:, :Lb], a2_[:, :Lb], AF.Sqrt, [a2_], [a2_])
                    return (ig_, a_, a2_)

                def gate_scan(n, d, base, s0, Lb, xa_t, c0, want_y, gbufs):
                    ig_, a_, a2_ = gbufs
                    P.tt("dve", ub[:, :Lb], a2_[:, :Lb], ig_[:, :Lb], ALU.mult, [a2_, ig_], [ub])
                    P.tt("dve", ub[:, :Lb], ub[:, :Lb], xa_t[:, c0:c0 + Lb], ALU.mult, [ub, xa_t], [ub])
                    h_ = hb[cnt["h"] % 2]
                    cnt["h"] += 1
                    if d == 0:
                        P.op("dve", lambda e, Lb=Lb, h_=h_, a_=a_: e.tensor_tensor_scan(
                            out=h_[:, 0:Lb], data0=a_[:, 0:Lb], data1=ub[:, 0:Lb], initial=carry[:, 0:1],
                            op0=ALU.mult, op1=ALU.add), [a_, ub, carry], [h_])
                        P.copy("dve", carry[:, :], h_[:, Lb - 1:Lb], [h_], [carry])
                    else:
                        rs = slice(Lb - 1, None, -1)
                        P.op("dve", lambda e, rs=rs, h_=h_, a_=a_: e.tensor_tensor_scan(
                            out=h_[:, rs], data0=a_[:, rs], data1=ub[:, rs], initial=carry[:, 0:1],
                            op0=ALU.mult, op1=ALU.add), [a_, ub, carry], [h_])
                        P.copy("dve", carry[:, :], h_[:, 0:1], [h_], [carry])
                    if want_y:
                        if d == 0:
                            P.dma("pool", hfb[n * 128:(n + 1) * 128, base + s0:base + s0 + Lb], h_[:, :Lb], h_,
                                  reads=[h_], writes=[d_hf])
                        else:
                            ci = cnt["c"] % 2
                            cnt["c"] += 1
                            hf_, rg_, ya_ = hfl[ci], rgl[ci], yab[ci]
                            P.dma("sp", hf_[:, :Lb], hfb[n * 128:(n + 1) * 128, base + s0:base + s0 + Lb], hf_,
                                  reads=[d_hf], writes=[hf_])
                            P.dma("sp", rg_[:, :Lb], zrv[:, 8 + n, base + s0:base + s0 + Lb], rg_,
                                  reads=[d_z], writes=[rg_])
                            P.act(rg_[:, :Lb], rg_[:, :Lb], AF.Gelu_apprx_tanh, [rg_], [rg_])
                            P.tt("dve", hf_[:, :Lb], hf_[:, :Lb], h_[:, :Lb], ALU.add, [hf_, h_], [hf_])
                            P.tt("dve", ya_[:, :Lb], hf_[:, :Lb], rg_[:, :Lb], ALU.mult, [hf_, rg_], [ya_])
                            P.dma("pool", ybv[:, n, base + s0:base + s0 + Lb], ya_[:, :Lb], ya_,
                                  reads=[ya_], writes=[d_y])

                def sconv_p(n, base, length, s0, Lb):
                    si = cnt["y"] % 2
                    cnt["y"] += 1
                    halo_load(scgl[si], 3, n, base, length, s0, Lb, 1, 1)
                    halo_load(sxl[si], 4, n, base, length, s0, Lb, 1, 1)
                    P.dma("sp", sbl[si][:, :Lb], zrv[:, 16 + n, base + s0:base + s0 + Lb], sbl[si], reads=[d_z], writes=[sbl[si]])
                    P.tt("dve", pbs[si][:, :Lb + 2], scgl[si][:, :Lb + 2], sxl[si][:, :Lb + 2], ALU.mult,
                         [scgl[si], sxl[si]], [pbs[si]])
                    return si

                def sconv_taps(n, Lb, si):
                    wv = lambda k: VC("sconv_w%d" % l, k * 8 + n, 1)
                    for k in range(3):
                        P.act(yts[si][k][:, :Lb], pbs[si][:, k:k + Lb], AF.Identity, [pbs[si], vecs], [yts[si][k]], scale=wv(k))

                def sconv_fin(n, base, s0, Lb, si):
                    y0, y1, y2 = yts[si]
                    P.tt("dve", y0[:, :Lb], y0[:, :Lb], y1[:, :Lb], ALU.add, [y0, y1], [y0])
                    P.tt("dve", y0[:, :Lb], y0[:, :Lb], y2[:, :Lb], ALU.add, [y0, y2], [y0])
                    yo = ybo[si]
                    P.tt("dve", yo[:, :Lb], y0[:, :Lb], sbl[si][:, :Lb], ALU.mult, [y0, sbl[si]], [yo])
                    P.dma("pool", ybv[:, 8 + n, base + s0:base + s0 + Lb], yo[:, :Lb], yo, reads=[yo], writes=[d_y])

                def rglru_seq(n, base, length, h0, want_y, save_final, xa_t, xab_t, do_sconv):
                    nb = (length + SB - 1) // SB
                    blocks = [(bi * SB, min(SB, length - bi * SB)) for bi in range(nb)]
                    for d in range(2):
                        P.copy("dve", carry[:, :], h0[d], [st, zero1], [carry])
                        order = blocks if d == 0 else list(reversed(blocks))
                        if d == 0:
                            conv_block(n, base, length, order[0][0], order[0][1], xa_t, xab_t, order[0][0])
                        for i, (s0, Lb) in enumerate(order):
                            si = None
                            if d == 1 and do_sconv:
                                si = sconv_p(n, base, length, s0, Lb)
                            gb = gate_acts(n, d, Lb, xab_t, s0)
                            if si is not None:
                                sconv_taps(n, Lb, si)
                            if d == 0 and i + 1 < len(order):
                                conv_block(n, base, length, order[i + 1][0], order[i + 1][1], xa_t, xab_t, order[i + 1][0])
                            gate_scan(n, d, base, s0, Lb, xa_t, s0, want_y, gb)
                            if si is not None:
                                sconv_fin(n, base, s0, Lb, si)
                        if save_final:
                            P.copy("dve", st[:, n, d:d + 1], carry[:, :], [carry], [st])

                for n in range(8):
                    rglru_seq(n, T, NCTX, [zero1[:, 0:1], zero1[:, 0:1]], want_ctx, True, xac, xacb, want_ctx)
                    rglru_seq(n, 0, T, [st[:, n, 0:1], st[:, n, 1:2]], True, False, xaf, xabf, True)

            with P.phase():
                esk = P.sbuf("esink", [128, 8, 128], F32)
                P.act(DV("tmp", 0, 8), VC("sink%d" % l), AF.Exp, [vecs], [dv])
                for h in range(8):
                    P.act(esk[:, h, :], ones_f[:, :], AF.Identity, [dv, ones_f], [esk], scale=DV("tmp", h, 1))
                kctx = P.sbuf("kctx", [128, 2, NCTX], BF16, dma=True)
                vctx = P.sbuf("vctx", [128, 2, 256], BF16, dma=True)
                P.dma("sp", kctx[:, :, :], zqv[:, 8:10, T:T + NCTX], kctx, reads=[d_z], writes=[kctx])
                P.dma("sp", vctx[:, :, :], zv[T:T + NCTX, :].rearrange("(b p) f -> p b f", p=128), vctx,
                      reads=[d_z], writes=[vctx])
                qg = [P.sbuf("qg%d" % i, [128, 8, 512], BF16, dma=True) for i in range(2)]
                kg = [P.sbuf("kg%d" % i, [128, 2, 768], BF16, dma=True) for i in range(2)]
                vg = [P.sbuf("vg%d" % i, [128, 6, 256], BF16, dma=True) for i in range(2)]
                og = [P.sbuf("og%d" % i, [128, 8, 512], BF16, dma=True) for i in range(2)]
                pts = [P.sbuf("pt%d" % i, [128, 512], BF16) for i in range(3)]
                dsm = [P.sbuf("dsm%d" % i, [128, 512], F32) for i in range(2)]
                pti = [0]
                dsi = [0]

                def attend(qt, qcol, chunks, kv, ogt, ocol):
                    bo = P.bank()
                    bd = P.bank()
                    rhs_q = qt[:, kv * 4:(kv + 1) * 4, qcol:qcol + 128]
                    sbanks = []

                    def score(ci):
                        kT, v_, m_, bufs = chunks[ci]
                        b_ = P.bank()
                        P.mm(b_[:, :], kT, rhs_q, True, m_ is None, [qt] + bufs, [b_])
                        if m_ is not None:
                            P.mm(b_[:, :], ident_ap, m_, False, True, [cbf], [b_])
                        sbanks.append(b_)
                    nchunk = len(chunks)
                    score(0)
                    if nchunk > 1:
                        score(1)
                    for ci in range(nchunk):
                        kT, v_, m_, bufs = chunks[ci]
                        pt = pts[pti[0] % 3]
                        pti[0] += 1
                        P.act(pt[:, :], sbanks[ci][:, :], AF.Exp, [sbanks[ci]], [pt], scale=ATT_SCALE)
                        P.mm(bo[:, :], v_, pt[:, :], ci == 0, ci == nchunk - 1, [pt] + bufs, [bo])
                        P.mm(bd[:, :], ones_b[:, :], pt[:, :], ci == 0, ci == nchunk - 1, [pt, ones_b], [bd])
                        if ci + 2 < nchunk:
                            score(ci + 2)
                    ds_ = dsm[dsi[0] % 2]
                    dsi[0] += 1
                    P.tt("dve", ds_[:, :], bd[:, :], esk[:, kv * 4:(kv + 1) * 4, :], ALU.add, [bd, esk], [ds_])
                    P.op("dve", lambda e: e.reciprocal(out=ds_[:, :], in_=ds_[:, :]), [ds_], [ds_])
                    P.tt("dve", ogt[:, kv * 4:(kv + 1) * 4, ocol:ocol + 128],
                         bo[:, :].rearrange("p (h q) -> p h q", h=4),
                         ds_[:, :].rearrange("p (h q) -> p h q", h=4), ALU.mult, [bo, ds_], [ogt])

                NBLK = T // 128
                for gq in range(T // 512):
                    g0 = gq * 512
                    qt, kt, vt, ot = qg[gq % 2], kg[gq % 2], vg[gq % 2], og[gq % 2]
                    P.dma("sp", qt[:, :, :], zqv[:, 0:8, g0:g0 + 512], qt, reads=[d_z], writes=[qt])
                    lo = max(0, g0 - 128)
                    hi = min(T, g0 + 640)
                    P.dma("sp", kt[:, :, lo - (g0 - 128):hi - (g0 - 128)], zqv[:, 8:10, lo:hi], kt, reads=[d_z], writes=[kt])
                    P.dma("sp", vt[:, (lo - (g0 - 128)) // 128:(hi - (g0 - 128)) // 128, :],
                          zv[lo:hi, :].rearrange("(b p) f -> p b f", p=128), vt, reads=[d_z], writes=[vt])
                    for i in range(4):
                        nblk = gq * 4 + i
                        for kv in range(2):
                            chunks = []
                            for rel in (-1, 0, 1):
                                if 0 <= nblk + rel < NBLK:
                                    kb = i + 1 + rel
                                    m_ = None if rel == 0 else (maskp_ap if rel < 0 else maskn_ap)
                                    chunks.append((kt[:, kv, kb * 128:(kb + 1) * 128], vt[:, kb, kv * 128:(kv + 1) * 128],
                                                   m_, [kt, vt]))
                            for cc in range(2):
                                chunks.append((kctx[:, kv, cc * 128:(cc + 1) * 128], vctx[:, cc, kv * 128:(kv + 1) * 128],
                                               None, [kctx, vctx]))
                            attend(qt, i * 128, chunks, kv, ot, i * 128)
                    P.dma("pool", ybv[:, 16:24, g0:g0 + 512], ot[:, :, :], ot, reads=[ot], writes=[d_y])
                if want_ctx:
                    qt, ot = qg[0], og[0]
                    P.dma("sp", qt[:, :, 0:NCTX], zqv[:, 0:8, T:T + NCTX], qt, reads=[d_z], writes=[qt])
                    for i in range(2):
                        for kv in range(2):
                            chunks = []
                            for cc in range(2):
                                chunks.append((kctx[:, kv, cc * 128:(cc + 1) * 128], vctx[:, cc, kv * 128:(kv + 1) * 128],
                                               None, [kctx, vctx]))
                            attend(qt, i * 128, chunks, kv, ot, i * 128)
                    P.dma("pool", ybv[:, 16:24, T:T + NCTX], ot[:, :, 0:NCTX], ot, reads=[ot], writes=[d_y])

        final_ops = []

        def phaseC(l, last):
            with P.phase():
                C = alloc_tile_bufs()
                htile, big = C.htile, C.big
                gts = [P.sbuf("gts%d" % i, [128, 3, TT], BF16, dma=True) for i in range(2)]
                gti = 0
                ytb = P.sbuf("ytile", [128, 24, TT], BF16, dma=True)
                ytile = ytb
                wbr = [P.sbuf("wbr%d" % i, [128, 8 * 256], BF16, dma=True) for i in range(4)]
                wbi = 0
                merged = big[:, 0:16 * TT].rearrange("p (c t) -> p c t", c=16)
                for (t0, n, s) in tiles(with_ctx=not last):
                    P.dma("sp", ytile[:, :, :n], ybv[:, :, t0:t0 + n], ytb, reads=[d_y], writes=[ytb])
                    for q in range(8):
                        bks = []
                        for i in range(3):
                            slot = wbr[wbi % 4]
                            wbi += 1
                            sv = slot[:, 0:8 * 256].rearrange("p (k n) -> p k n", k=8)
                            P.dma("sp", slot[:, 0:8 * 256], pwbr[l][i][q].rearrange("p k n -> p (k n)"), slot,
                                  reads=[d_w], writes=[slot])
                            pair = []
                            for half in range(2):
                                bk = P.bank()
                                for kc in range(8):
                                    P.mm(bk[:, :n], sv[:, kc, half * 128:(half + 1) * 128], ytile[:, i * 8 + kc, :n],
                                         kc == 0, kc == 7, [slot, ytb], [bk])
                                pair.append(bk)
                            bks.append(pair)
                        for half in range(2):
                            c = 2 * q + half
                            gt = gts[gti % 2]
                            gti += 1
                            P.dma("sp", gt[:, :, :n], zg4[:, :, c, t0:t0 + n], gt, reads=[d_z], writes=[gt])
                            ts3 = [tmp(C) for _ in range(3)]
                            for i in range(3):
                                P.tt("dve", ts3[i][:, :n], bks[i][half][:, :n], gt[:, i, :n], ALU.mult,
                                     [bks[i][half], gt], [ts3[i]])
                            P.tt("dve", ts3[0][:, :n], ts3[0][:, :n], ts3[1][:, :n], ALU.add, [ts3[0], ts3[1]], [ts3[0]])
                            P.tt("dve", merged[:, c, :n], ts3[0][:, :n], ts3[2][:, :n], ALU.add, [ts3[0], ts3[2]], [big])
                    G2 = DV("G2%d%d" % (l, s))
                    P.dma("sp", htile[:, :, :n], hbv[:, :, t0:t0 + n], htile, reads=[dh((l, "a", t0))], writes=[htile])
                    for q in range(8):
                        slot = next8(C)
                        sv = slot[:, :].rearrange("p (k n) -> p k n", k=DC)
                        P.dma("sp", slot[:, :], pwout[l][q].rearrange("p k n -> p (k n)"), slot, reads=[d_w], writes=[slot])
                        for half in range(2):
                            c = 2 * q + half
                            bk = P.bank()
                            for kc in range(DC):
                                P.mm(bk[:, :n], sv[:, kc, half * 128:(half + 1) * 128], merged[:, kc, :n],
                                     kc == 0, kc == DC - 1, [slot, big], [bk])
                            P.stt("dve", htile[:, c, :n], bk[:, :n], G2[:, c:c + 1], htile[:, c, :n], ALU.mult, ALU.add,
                                  [bk, htile, dv], [htile])
                    rmsnorm_mod(C, n, DV("A3%d%d" % (l, s)), DV("B3%d%d" % (l, s)))
                    ffn(C, 1, l, n, DV("G3%d%d" % (l, s)))
                    if last:
                        rms_stats(C, n)
                        for c in range(DC):
                            P.stt("dve", htile[:, c, :n], htile[:, c, :n], VC("fnorm_g", c, 1), C.rstd[:, :n],
                                  ALU.mult, ALU.mult, [htile, C.rstd, vecs], [htile])
                        o = P.dma("pool", outv[:, :, t0:t0 + n], htile[:, :, :n], htile, reads=[htile], writes=[d_out])
                        final_ops.append(o)
                    else:
                        P.dma("pool", hbv[:, :, t0:t0 + n], htile[:, :, :n], htile, reads=[htile],
                              writes=[dh((l + 1, "c", t0))])

        import os
        kstop = int(os.environ.get("KSTOP", "99"))
        step = 2
        for l in range(NLAYER):
            last = (l == NLAYER - 1)
            for ph in (phaseA, phaseB, phaseC):
                if step <= kstop:
                    if ph is phaseA:
                        ph(l)
                    elif ph is phaseB:
                        ph(l, not last)
                    else:
                        ph(l, last)
                step += 1
        P.emit(final_ops)
    return nc


def host_consts(T):
    t = np.arange(T)
    row = (t // 64).astype(np.float32)
    col = (t % 64).astype(np.float32)
    inv = (np.float32(10000.0) ** (-np.arange(0, 64, 2, dtype=np.float32) / np.float32(64))).astype(np.float32)
    ang_r = (row[None, :] * inv[:, None]).astype(np.float32)
    ang_c = (col[None, :] * inv[:, None]).astype(np.float32)
    cos_full = np.concatenate([np.cos(ang_r), np.cos(ang_r), np.cos(ang_c), np.cos(ang_c)], 0).astype(np.float32)
    sin_sgn = np.concatenate([-np.sin(ang_r), np.sin(ang_r), -np.sin(ang_c), np.sin(ang_c)], 0).astype(np.float32)
    rope = np.stack([cos_full, sin_sgn], 0)
    cb = np.zeros((128, CBFN), np.float32)
    for m in range(128):
        partner = m + 32 if (m % 64) < 32 else m - 32
        cb[partner, m] = 1.0
    cb[:, 128:256] = np.eye(128, dtype=np.float32)
    j = np.arange(128)[:, None]
    i = np.arange(128)[None, :]
    mp = np.where(i <= j, 0.0, -30000.0).astype(np.float32)
    mn = np.where(j <= i, 0.0, -30000.0).astype(np.float32)
    cb[:, 256:768] = np.tile(mp, (1, 4))
    cb[:, 768:1280] = np.tile(mn, (1, 4))
    return rope, cb.astype(ml_dtypes.bfloat16)


_NC_CACHE = {}


def kernel(x, c, ctx, c_ctx, ada_w, ada_b, norm_g, ffn1_w13, ffn1_w2, w_in, b_merge,
           rnn_conv_w, rnn_conv_b, lru_w_a, lru_b_a, lru_w_x, lru_b_x, lru_lambda,
           sc_conv_w, attn_sink, w_branch, w_out, ffn2_w13, ffn2_w2, final_norm_g):
    f32 = lambda a: np.ascontiguousarray(np.asarray(a), dtype=np.float32)
    x = f32(x)
    B, T, _ = x.shape
    ctx = f32(ctx)
    L = make_layout()
    rope, cb = host_consts(T)
    lru_w_a = f32(lru_w_a)
    lru_w_x = f32(lru_w_x)
    lw = np.stack([lru_w_a, lru_w_x], 1)
    lw = np.ascontiguousarray(lw.transpose(0, 4, 1, 2, 3, 5).reshape(NLAYER, 128, 32, 128))
    shared = {
        "ada_w": f32(ada_w), "ffn1_w13": f32(ffn1_w13), "ffn2_w13": f32(ffn2_w13), "ffn1_w2": f32(ffn1_w2),
        "ffn2_w2": f32(ffn2_w2), "w_in": f32(w_in), "w_branch": f32(w_branch), "w_out": f32(w_out),
        "lru_w": lw, "rope": rope, "cbf": cb,
    }
    base = np.zeros((128, L.n), np.float32)

    def put(name, arr):
        sl = L.sl(name)
        base[:, sl] = arr

    for l in range(NLAYER):
        put("ada_b%d" % l, fm(f32(ada_b)[l]))
        put("norm_g%d" % l, fm(f32(norm_g)[l].reshape(-1)))
        put("b_merge%d" % l, fm(f32(b_merge)[l].reshape(-1)))
        put("rconv_w%d" % l, fm(f32(rnn_conv_w)[l].reshape(-1)))
        put("rconv_b%d" % l, fm(f32(rnn_conv_b)[l]))
        put("lru_b_a%d" % l, fm(f32(lru_b_a)[l].reshape(-1)))
        put("lru_b_x%d" % l, fm(f32(lru_b_x)[l].reshape(-1)))
        put("lam%d" % l, fm(f32(lru_lambda)[l].reshape(-1)))
        put("sconv_w%d" % l, fm(f32(sc_conv_w)[l].reshape(-1)))
        put("sink%d" % l, np.tile(f32(attn_sink)[l][None, :], (128, 1)))
    put("fnorm_g", fm(f32(final_norm_g)))
    in_maps = []
    for b in range(B):
        v = base.copy()
        cv = np.stack([fm(f32(c)[b]), fm(f32(c_ctx))], -1).reshape(128, 32)
        v[:, L.sl("cvec")] = cv
        xt = np.ascontiguousarray(np.concatenate([x[b].T, ctx[b].T], axis=1))
        m = dict(shared)
        m["xT"] = xt
        m["vecs"] = v
        in_maps.append(m)
    if T not in _NC_CACHE:
        _NC_CACHE[T] = build(T)
    nc = _NC_CACHE[T]
    if B == 4:
        place = [0, 1, 4, 5]
        zero = {k: np.zeros_like(v) for k, v in in_maps[0].items()}
        maps8 = [zero] * 8
        maps8 = list(maps8)
        for b, c_ in enumerate(place):
            maps8[c_] = in_maps[b]
        res = run_bass_kernel_spmd(nc, maps8, core_ids=list(range(8)))
        outs = [res.results[c_]["outT"] for c_ in place]
    else:
        res = run_bass_kernel_spmd(nc, in_maps, core_ids=list(range(B)))
        outs = [r["outT"] for r in res.results]
    out = np.stack([np.ascontiguousarray(o.T) for o in outs], 0)
    return out.astype(np.float32)
```
